# Optimizing a Trainium2 kernel written in Bass

```python
import math
import jax, jax.numpy as jnp
from jax import lax
import numpy as np

D_MODEL = 2048
BATCH = 4
SEQ = 4096
DEPTH = 2

CHUNK = 64
N_MIXERS = 2
N_SSD_LAYERS = (DEPTH + 1) // 2
N_FOX_LAYERS = DEPTH // 2
EPS = 1e-6
N_MOD = 6

SSD_EXPAND = 2
SSD_D_INNER = SSD_EXPAND * D_MODEL
SSD_HEAD_DIM = 64
SSD_HEADS = SSD_D_INNER // SSD_HEAD_DIM
SSD_GROUPS = 8
SSD_STATE = 128
SSD_CONV = 4
SSD_CONV_DIM = SSD_D_INNER + 2 * SSD_GROUPS * SSD_STATE
SSD_IN_DIM = SSD_D_INNER + SSD_CONV_DIM + SSD_HEADS

FOX_HEAD_DIM = 128
FOX_HEADS = D_MODEL // FOX_HEAD_DIM
FOX_WIDTH = FOX_HEADS * FOX_HEAD_DIM
FOX_IN_DIM = 3 * FOX_WIDTH + FOX_HEADS
Q_BLOCK = 128

D_FF = 4 * D_MODEL

kernel_name = 'hybrid_ssd_fox_adaln_trunk'


def rms_norm(x, g):
    xf = x.astype(jnp.float32)
    y = xf * lax.rsqrt(jnp.mean(xf * xf, axis=-1, keepdims=True) + EPS)
    return (y * g.astype(jnp.float32)).astype(x.dtype)


def modulate(h, shift, scale):
    return h * (1.0 + scale) + shift


def causal_depthwise_conv(x, w, bias):
    out = lax.conv_general_dilated(
        x, w[:, None, :].astype(x.dtype), window_strides=(1,),
        padding=[(SSD_CONV - 1, 0)], dimension_numbers=('NWC', 'WIO', 'NWC'),
        feature_group_count=x.shape[-1])
    return out + bias.astype(x.dtype)


def ssd_chunked_scan(xh, dt, A, Bg, Cg):
    b, l, h, p = xh.shape
    g, n = Bg.shape[2], Bg.shape[3]
    r = h // g
    nc = l // CHUNK
    x = xh.reshape(b, nc, CHUNK, g, r, p)
    dtc = dt.reshape(b, nc, CHUNK, g, r)
    xdt = x * dtc[..., None]
    a_cum = jnp.cumsum(dtc * A.reshape(g, r), axis=2)
    Bc = Bg.reshape(b, nc, CHUNK, g, n)
    Cc = Cg.reshape(b, nc, CHUNK, g, n)
    causal = jnp.tril(jnp.ones((CHUNK, CHUNK), dtype=bool))
    seg = a_cum[:, :, :, None] - a_cum[:, :, None, :]
    decay = jnp.exp(jnp.where(causal[None, None, :, :, None, None], seg, -jnp.inf))
    scores = jnp.einsum('bctgn,bcsgn->bctsg', Cc, Bc)
    y_diag = jnp.einsum('bctsgr,bcsgrp->bctgrp', scores[..., None] * decay, xdt)
    decay_to_end = jnp.exp(a_cum[:, :, -1:] - a_cum)
    states = jnp.einsum('bcsgn,bcsgrp->bcgrpn', Bc, xdt * decay_to_end[..., None])
    chunk_decay = jnp.exp(a_cum[:, :, -1])

    def step(carry, inp):
        st, dec = inp
        return carry * dec[..., None, None] + st, carry

    init = jnp.zeros((b, g, r, p, n), jnp.float32)
    _, prev = lax.scan(step, init, (jnp.moveaxis(states, 1, 0), jnp.moveaxis(chunk_decay, 1, 0)))
    prev = jnp.moveaxis(prev, 0, 1)
    y_off = jnp.einsum('bctgn,bcgrpn->bctgrp', Cc, prev) * jnp.exp(a_cum)[..., None]
    return (y_diag + y_off).reshape(b, l, h, p)


def ssd_mixer(u, w_in, conv_w, conv_b, dt_bias, A_log, D_skip, gnorm, w_out):
    b, l, _ = u.shape
    zxbcdt = u @ w_in
    z, xbc, dt_raw = jnp.split(zxbcdt, [SSD_D_INNER, SSD_D_INNER + SSD_CONV_DIM], axis=-1)
    xbc = jax.nn.silu(causal_depthwise_conv(xbc, conv_w, conv_b))
    xs, Bm, Cm = jnp.split(xbc, [SSD_D_INNER, SSD_D_INNER + SSD_GROUPS * SSD_STATE], axis=-1)
    dt = jax.nn.softplus(dt_raw.astype(jnp.float32) + dt_bias.astype(jnp.float32))
    A = -jnp.exp(A_log.astype(jnp.float32))
    xh = xs.reshape(b, l, SSD_HEADS, SSD_HEAD_DIM).astype(jnp.float32)
    y = ssd_chunked_scan(
        xh, dt, A,
        Bm.reshape(b, l, SSD_GROUPS, SSD_STATE).astype(jnp.float32),
        Cm.reshape(b, l, SSD_GROUPS, SSD_STATE).astype(jnp.float32))
    y = y + xh * D_skip.astype(jnp.float32)[:, None]
    y = y.reshape(b, l, SSD_D_INNER) * jax.nn.silu(z.astype(jnp.float32))
    yg = y.reshape(b, l, SSD_GROUPS, SSD_D_INNER // SSD_GROUPS)
    yg = yg * lax.rsqrt(jnp.mean(yg * yg, axis=-1, keepdims=True) + EPS)
    y = yg.reshape(b, l, SSD_D_INNER) * gnorm.astype(jnp.float32)
    return y.astype(u.dtype) @ w_out


def fox_mixer(u, w_in, b_f, w_out):
    b, l, _ = u.shape
    qkvf = u @ w_in
    q, k, v, f_logit = jnp.split(qkvf, [FOX_WIDTH, 2 * FOX_WIDTH, 3 * FOX_WIDTH], axis=-1)
    q = q.reshape(b, l, FOX_HEADS, FOX_HEAD_DIM)
    k = k.reshape(b, l, FOX_HEADS, FOX_HEAD_DIM)
    v = v.reshape(b, l, FOX_HEADS, FOX_HEAD_DIM)
    log_f = jax.nn.log_sigmoid(f_logit.astype(jnp.float32) + b_f.astype(jnp.float32))
    cum = jnp.moveaxis(jnp.cumsum(log_f, axis=1), 1, 2)
    scale = FOX_HEAD_DIM ** -0.5
    outs = []
    for blk in range(l // Q_BLOCK):
        q0 = blk * Q_BLOCK
        q1 = q0 + Q_BLOCK
        s = jnp.einsum('bqhd,bkhd->bhqk', q[:, q0:q1], k[:, :q1]).astype(jnp.float32) * scale
        s = s + cum[:, :, q0:q1, None] - cum[:, :, None, :q1]
        causal = jnp.arange(q0, q1)[:, None] >= jnp.arange(q1)[None, :]
        pr = jax.nn.softmax(jnp.where(causal, s, -jnp.inf), axis=-1)
        outs.append(jnp.einsum('bhqk,bkhd->bqhd', pr.astype(v.dtype), v[:, :q1]))
    o = jnp.concatenate(outs, axis=1).reshape(b, l, FOX_WIDTH)
    return o @ w_out


def sq_relu_mlp(u, w_up, w_down):
    return jnp.square(jax.nn.relu(u @ w_up)) @ w_down


def setup_inputs(seed: int = 0) -> dict:
    key = jax.random.key(seed)
    ks = jax.random.split(key, 24)
    f32 = jnp.float32

    def nrm(k, shape, fan_in, s=1.0):
        return jax.random.normal(k, shape, f32) * (s * fan_in ** -0.5)

    def gain(k, shape):
        return 1.0 + 0.05 * jax.random.normal(k, shape, f32)

    x = jax.random.normal(ks[0], (BATCH, SEQ, D_MODEL), f32)
    c = jax.random.normal(ks[1], (BATCH, D_MODEL), f32)
    norm_mix = gain(ks[2], (DEPTH, D_MODEL))
    norm_mlp = gain(ks[3], (DEPTH, D_MODEL))
    w_ada = nrm(ks[4], (DEPTH, D_MODEL, N_MOD * D_MODEL), D_MODEL, 0.5)
    b_ada = 0.02 * jax.random.normal(ks[5], (DEPTH, N_MOD * D_MODEL), f32)
    w_up = nrm(ks[6], (DEPTH, D_MODEL, D_FF), D_MODEL)
    w_down = nrm(ks[7], (DEPTH, D_FF, D_MODEL), D_FF)
    ssd_w_in = nrm(ks[8], (N_SSD_LAYERS, D_MODEL, SSD_IN_DIM), D_MODEL)
    ssd_conv_w = nrm(ks[9], (N_SSD_LAYERS, SSD_CONV, SSD_CONV_DIM), SSD_CONV)
    ssd_conv_b = 0.02 * jax.random.normal(ks[10], (N_SSD_LAYERS, SSD_CONV_DIM), f32)
    dt0 = jnp.exp(jax.random.uniform(ks[11], (N_SSD_LAYERS, SSD_HEADS), f32,
                                     math.log(1e-3), math.log(1e-1)))
    ssd_dt_bias = dt0 + jnp.log(-jnp.expm1(-dt0))
    ssd_A_log = jnp.log(jax.random.uniform(ks[12], (N_SSD_LAYERS, SSD_HEADS), f32, 1.0, 16.0))
    ssd_D = gain(ks[13], (N_SSD_LAYERS, SSD_HEADS))
    ssd_gnorm = gain(ks[14], (N_SSD_LAYERS, SSD_D_INNER))
    ssd_w_out = nrm(ks[15], (N_SSD_LAYERS, SSD_D_INNER, D_MODEL), SSD_D_INNER)
    fox_w_in = nrm(ks[16], (N_FOX_LAYERS, D_MODEL, FOX_IN_DIM), D_MODEL)
    fox_b_f = jax.random.uniform(ks[17], (N_FOX_LAYERS, FOX_HEADS), f32, 1.0, 4.0)
    fox_w_out = nrm(ks[18], (N_FOX_LAYERS, FOX_WIDTH, D_MODEL), FOX_WIDTH)
    final_norm = gain(ks[19], (D_MODEL,))
    return {'x': x, 'c': c, 'norm_mix': norm_mix, 'norm_mlp': norm_mlp,
            'w_ada': w_ada, 'b_ada': b_ada, 'w_up': w_up, 'w_down': w_down,
            'ssd_w_in': ssd_w_in, 'ssd_conv_w': ssd_conv_w, 'ssd_conv_b': ssd_conv_b,
            'ssd_dt_bias': ssd_dt_bias, 'ssd_A_log': ssd_A_log, 'ssd_D': ssd_D,
            'ssd_gnorm': ssd_gnorm, 'ssd_w_out': ssd_w_out,
            'fox_w_in': fox_w_in, 'fox_b_f': fox_b_f, 'fox_w_out': fox_w_out,
            'final_norm': final_norm}


def reference(x, c, norm_mix, norm_mlp, w_ada, b_ada, w_up, w_down,
              ssd_w_in, ssd_conv_w, ssd_conv_b, ssd_dt_bias, ssd_A_log, ssd_D,
              ssd_gnorm, ssd_w_out, fox_w_in, fox_b_f, fox_w_out, final_norm):
    cond = jax.nn.silu(c)
    h = x
    for i in range(DEPTH):
        mod = cond @ w_ada[i] + b_ada[i]
        sh_a, sc_a, g_a, sh_f, sc_f, g_f = jnp.split(mod[:, None, :], N_MOD, axis=-1)
        u = modulate(rms_norm(h, norm_mix[i]), sh_a, sc_a)
        j = i // N_MIXERS
        if i % N_MIXERS == 0:
            mix = ssd_mixer(u, ssd_w_in[j], ssd_conv_w[j], ssd_conv_b[j], ssd_dt_bias[j],
                            ssd_A_log[j], ssd_D[j], ssd_gnorm[j], ssd_w_out[j])
        else:
            mix = fox_mixer(u, fox_w_in[j], fox_b_f[j], fox_w_out[j])
        h = h + g_a * mix
        u = modulate(rms_norm(h, norm_mlp[i]), sh_f, sc_f)
        h = h + g_f * sq_relu_mlp(u, w_up[i], w_down[i])
    return rms_norm(h, final_norm)
```

```python
import contextlib
import numpy as np
import ml_dtypes
import concourse.bass as bass
import concourse.mybir as mybir
from concourse.bass_utils import run_bass_kernel_spmd

F32 = mybir.dt.float32
BF16 = mybir.dt.bfloat16
AF = mybir.ActivationFunctionType
ALU = mybir.AluOpType

D = 2048
DC = 16
SEQ = 4096
BATCH = 4
T = 2048
TT = 512
NTT = T // TT
EPS = 1e-6
DFF = 8192
CW = 256
DI = 4096
NH = 64
NG = 8
NS = 128
HP = 64
FH = 16
FD = 128

ENGS = ("pe", "act", "dve", "pool", "sp")
NO_SELF_SYNC = ("pe",)


class Buf:
    __slots__ = ("name", "writer", "readers", "dma_readers")

    def __init__(self, name=""):
        self.name = name
        self.writer = None
        self.readers = {}
        self.dma_readers = []


class Op:
    __slots__ = ("eng", "fn", "deps", "is_dma", "needs_inc", "tok_sem", "tok_val", "dma_slot", "prev_same_sem", "is_cc")

    def __init__(self, eng, fn, is_dma):
        self.eng = eng
        self.fn = fn
        self.deps = []
        self.is_dma = is_dma
        self.needs_inc = False
        self.tok_sem = None
        self.tok_val = None
        self.dma_slot = None
        self.prev_same_sem = None
        self.is_cc = False


class Sched:
    def __init__(self, nc, n_dma_sems=10):
        self.nc = nc
        self.ops = {e: [] for e in ENGS}
        self.n_dma_sems = n_dma_sems
        self.dma_count = {e: 0 for e in ENGS}
        self.dma_last = {}
        self.cc_ops = []

    def _add(self, eng, fn, reads, writes, is_dma):
        op = Op(eng, fn, is_dma)
        deps = []
        for b in reads:
            if b.writer is not None:
                deps.append(b.writer)
        for b in writes:
            if b.writer is not None:
                deps.append(b.writer)
            deps.extend(b.readers.values())
            deps.extend(b.dma_readers)
        for b in reads:
            if is_dma:
                b.dma_readers.append(op)
            else:
                b.readers[eng] = op
        for b in writes:
            b.writer = op
            b.readers = {}
            b.dma_readers = []
        if is_dma:
            slot = self.dma_count[eng] % self.n_dma_sems
            self.dma_count[eng] += 1
            op.dma_slot = slot
            op.prev_same_sem = self.dma_last.get((eng, slot))
            self.dma_last[(eng, slot)] = op
        seen = set()
        for d in deps:
            if d is op or id(d) in seen:
                continue
            seen.add(id(d))
            op.deps.append(d)
        self.ops[eng].append(op)
        return op

    def op(self, eng, fn, reads=(), writes=()):
        return self._add(eng, fn, reads, writes, False)

    def dma(self, eng, fn, reads=(), writes=()):
        return self._add(eng, fn, reads, writes, True)

    def cc(self, fn):
        op = Op("pool", fn, True)
        op.is_cc = True
        self.cc_ops.append(op)
        self.ops["pool"].append(op)
        return op

    def barrier(self):
        lasts = []
        for e in ENGS:
            for op in reversed(self.ops[e]):
                if not op.is_dma and op.fn is not None:
                    lasts.append(op)
                    break
        dmas = list(self.dma_last.values()) + list(self.cc_ops)
        for e in ENGS:
            op = Op(e, None, False)
            op.deps = list(lasts) + dmas
            self.ops[e].append(op)

    @staticmethod
    def _skip(op, d):
        return (not d.is_dma) and (not op.is_dma) and op.fn is not None and d.eng == op.eng and op.eng in NO_SELF_SYNC

    def emit(self, final_wait_ops=()):
        nc = self.nc
        for e in ENGS:
            for op in self.ops[e]:
                for d in op.deps:
                    if d.is_dma or self._skip(op, d):
                        continue
                    d.needs_inc = True
        with contextlib.ExitStack() as st:
            csem = {e: st.enter_context(nc.semaphore("c_" + e)) for e in ENGS}
            dsem = {}
            for e in ENGS:
                for s in range(min(self.n_dma_sems, self.dma_count[e])):
                    dsem[(e, s)] = st.enter_context(nc.semaphore("d_%s_%d" % (e, s)))
            for op in self.cc_ops:
                op.tok_sem = st.enter_context(nc.semaphore("cc%d" % self.cc_ops.index(op)))
                op.tok_val = 1
            for e in ENGS:
                cnt = 0
                dcnt = {}
                for op in self.ops[e]:
                    if op.is_cc:
                        continue
                    if op.is_dma:
                        k = (e, op.dma_slot)
                        dcnt[k] = dcnt.get(k, 0) + 16
                        op.tok_sem = dsem[k]
                        op.tok_val = dcnt[k]
                    elif op.needs_inc:
                        cnt += 1
                        op.tok_sem = csem[e]
                        op.tok_val = cnt
            block = st.enter_context(nc.Block())

            def make(e):
                def body(eng):
                    waited = {}
                    for op in self.ops[e]:
                        best = {}
                        cand = []
                        if op.is_dma and op.prev_same_sem is not None:
                            p = op.prev_same_sem
                            cand.append((p.tok_sem, p.tok_val))
                        for d in op.deps:
                            if self._skip(op, d) or d.tok_sem is None:
                                continue
                            cand.append((d.tok_sem, d.tok_val))
                        for s, v in cand:
                            k = id(s)
                            if k not in best or best[k][1] < v:
                                best[k] = (s, v)
                        for k, (s, v) in best.items():
                            if waited.get(k, 0) >= v:
                                continue
                            waited[k] = v
                            eng.wait_ge(s, v)
                        if op.fn is None:
                            continue
                        inst = op.fn(eng)
                        if op.is_cc:
                            inst.then_inc(op.tok_sem)
                        elif op.is_dma:
                            inst.then_inc(op.tok_sem, 16)
                        elif op.needs_inc:
                            inst.then_inc(op.tok_sem, 1)
                    if e == "sp":
                        for op in final_wait_ops:
                            eng.wait_ge(op.tok_sem, op.tok_val)
                return body

            block.tensor(make("pe"))
            block.scalar(make("act"))
            block.vector(make("dve"))
            block.gpsimd(make("pool"))
            block.sync(make("sp"))


def tile_weight(w, cw=CW, rows_per_block=2048):
    K, N = w.shape
    rb = K // rows_per_block
    kc = rows_per_block // 128
    assert N % cw == 0
    x = w.reshape(rb, kc, 128, N // cw, cw)
    return np.ascontiguousarray(x.transpose(0, 3, 2, 1, 4))


def col_layout(v):
    return np.ascontiguousarray(v.reshape(-1, 128).T)


def feat_major(x2d):
    t, d = x2d.shape
    return np.ascontiguousarray(x2d.reshape(t, d // 128, 128).transpose(2, 1, 0))


def from_feat_major(y):
    p, c, t = y.shape
    return np.ascontiguousarray(y.transpose(2, 1, 0).reshape(t, c * p))


class Prog:
    def __init__(self):
        self.nc = bass.Bass("TRN2", target_bir_lowering=False)
        self.st = contextlib.ExitStack()
        self.S = Sched(self.nc)
        self.dram = {}
        self.nwb = 0
        self.bank_rr = 0
        self.pst = contextlib.ExitStack()
        self.phase_alloc = False

    def end_phase(self):
        self.S.barrier()
        self.pst.close()
        self.pst = contextlib.ExitStack()

    def din(self, name, shape, dt=F32):
        t = self.nc.dram_tensor(name, list(shape), dt, kind="ExternalInput").ap()
        self.dram[name] = t
        return t

    def dout(self, name, shape, dt=F32):
        t = self.nc.dram_tensor(name, list(shape), dt, kind="ExternalOutput").ap()
        self.dram[name] = t
        return t

    def dscratch(self, name, shape, dt=F32):
        t = self.nc.dram_tensor(name, list(shape), dt, kind="Internal").ap()
        self.dram[name] = t
        return t

    def sb(self, name, shape, dt=F32):
        st = self.pst if self.phase_alloc else self.st
        return st.enter_context(self.nc.sbuf_tensor(name, list(shape), dt))

    def ps(self, name, shape, dt=F32):
        st = self.pst if self.phase_alloc else self.st
        return st.enter_context(self.nc.psum_tensor(name, list(shape), dt))

    def setup_common(self):
        S = self.S
        self.consts_d = self.din("consts", [128, 4, 128])
        self.cst = self.sb("cst", [128, 4, 128])
        self.cst_bf = self.sb("cst_bf", [128, 4, 128], BF16)
        self.b_cst = Buf("cst")
        S.dma("sp", lambda e: e.dma_start(out=self.cst[:], in_=self.consts_d), writes=[self.b_cst])
        S.op("dve", lambda e: e.tensor_copy(out=self.cst_bf[:], in_=self.cst[:]), reads=[self.b_cst], writes=[self.b_cst])
        self.Lm = self.cst[:, 0, :]
        self.Um = self.cst[:, 1, :]
        self.ones = self.cst[:, 2, :]
        self.ident = self.cst[:, 3, :]
        self.ones_bf = self.cst_bf[:, 2, :]
        self.ident_bf = self.cst_bf[:, 3, :]
        self.NWB = 4
        self.wb = [self.sb("wb%d" % i, [128, 16, CW], BF16) for i in range(self.NWB)]
        self.b_wb = [Buf("wb%d" % i) for i in range(self.NWB)]
        self.mm_ps = [self.ps("mmps%d" % i, [128, 512]) for i in range(2)]
        self.b_mm = [Buf("mmps%d" % i) for i in range(2)]
        self.hT = self.sb("hT", [128, DC, TT])
        self.b_hT = Buf("hT")
        self.uT = self.sb("uT", [128, DC, TT], BF16)
        self.b_uT = Buf("uT")
        self.rstd = self.sb("rstd", [128, TT])
        self.b_rstd = Buf("rstd")
        self.ntmp = [self.sb("ntmp%d" % i, [128, TT]) for i in range(2)]
        self.b_ntmp = [Buf("ntmp%d" % i) for i in range(2)]
        self.yT = self.sb("big", [128, 32, TT], BF16)
        self.b_yT = Buf("yT")
        self.hid = self.yT[:, 0:16, :]
        self.b_hid = self.b_yT
        self.rl = [self.sb("rl%d" % i, [128, TT], BF16) for i in range(2)]
        self.b_rl = [Buf("rl%d" % i) for i in range(2)]

    def next_bank(self):
        i = self.bank_rr % 2
        self.bank_rr += 1
        return self.mm_ps[i], self.b_mm[i]

    def wtile(self, dram_tiles, idx):
        i = self.nwb % self.NWB
        self.nwb += 1
        wb, b = self.wb[i], self.b_wb[i]
        src = dram_tiles[idx] if isinstance(idx, int) else dram_tiles[idx[0], idx[1]]
        self.S.dma("pool", lambda e: e.dma_start(out=wb[:], in_=src, max_dma_last_dim=8192), writes=[b])
        return wb, b

    def adaln(self, w_ada_t, b_ada_col, c_col, norm_mix_col, norm_mlp_col, mod_out=None, mod_in=None, half_dram=None):
        S = self.S
        self.mod = self.sb("mod", [128, 2, 96])
        self.b_mod = Buf("mod")
        self.gs = self.sb("gs", [128, 2, 2, DC])
        self.b_gs = Buf("gs")
        nm = self.sb("nm", [128, 2, 2, DC])
        b_nm = Buf("nm")
        S.dma("sp", lambda e: e.dma_start(out=nm[:, :, 0, :], in_=norm_mix_col), writes=[b_nm])
        S.dma("sp", lambda e: e.dma_start(out=nm[:, :, 1, :], in_=norm_mlp_col), writes=[b_nm])
        if mod_in is not None:
            S.dma("sp", lambda e: e.dma_start(out=self.mod[:], in_=mod_in), writes=[self.b_mod])
        else:
            cc = self.sb("cc", [128, DC])
            cbf = self.sb("cbf", [128, DC], BF16)
            bada = self.sb("bada", [128, 2, 96])
            b_cc, b_bada = Buf("cc"), Buf("bada")
            S.dma("sp", lambda e: e.dma_start(out=cc[:], in_=c_col), writes=[b_cc])
            nb = 48 if half_dram is not None else 96
            S.dma("sp", lambda e: e.dma_start(out=bada[:, :, 0:nb], in_=b_ada_col), writes=[b_bada])
            S.op("act", lambda e: e.activation(out=cbf[:], in_=cc[:], func=AF.Silu), reads=[b_cc], writes=[b_cc])
            if half_dram is None:
                for i in range(2):
                    ps, bps = self.next_bank()
                    for ct in range(12288 // CW):
                        wt, bw = self.wtile(w_ada_t, (i, ct))
                        for jj in range(CW // 128):
                            col = ct * (CW // 128) + jj
                            for k in range(16):
                                S.op("pe", lambda e, ps=ps, wt=wt, jj=jj, k=k, col=col: e.matmul(
                                    ps[:, col:col + 1], lhsT=wt[:, k, jj * 128:(jj + 1) * 128], rhs=cbf[:, k:k + 1],
                                    start=(k == 0), stop=(k == 15)), reads=[bw, b_cc], writes=[bps])
                    S.op("dve", lambda e, ps=ps, i=i: e.tensor_tensor(out=self.mod[:, i, :], in0=ps[:, 0:96], in1=bada[:, i, :], op=ALU.add),
                         reads=[bps, b_bada], writes=[self.b_mod])
                if mod_out is not None:
                    self.mod_store = S.dma("act", lambda e: e.dma_start(out=mod_out, in_=self.mod[:]), reads=[self.b_mod], writes=[Buf()])
            else:
                m_own, m_g = half_dram
                mh = self.sb("modh", [128, 2, 48])
                b_mh = Buf("modh")
                for i in range(2):
                    ps, bps = self.next_bank()
                    for ct in range(24):
                        wt, bw = self.wtile(w_ada_t, (i, ct))
                        for jj in range(CW // 128):
                            col = ct * (CW // 128) + jj
                            for k in range(16):
                                S.op("pe", lambda e, ps=ps, wt=wt, jj=jj, k=k, col=col: e.matmul(
                                    ps[:, col:col + 1], lhsT=wt[:, k, jj * 128:(jj + 1) * 128], rhs=cbf[:, k:k + 1],
                                    start=(k == 0), stop=(k == 15)), reads=[bw, b_cc], writes=[bps])
                    S.op("dve", lambda e, ps=ps, i=i: e.tensor_tensor(out=mh[:, i, :], in0=ps[:, 0:48], in1=bada[:, i, 0:48], op=ALU.add),
                         reads=[bps, b_bada], writes=[b_mh])
                S.dma("act", lambda e: e.dma_start(out=m_own.rearrange("p (i c) -> p i c", i=2), in_=mh[:]), reads=[b_mh], writes=[Buf()])
                S.barrier()
                S.cc(lambda e: e.collective_compute("AllGather", ALU.bypass, replica_groups=PAIRS, ins=[m_own], outs=[m_g]))
                S.barrier()
                for r in range(2):
                    S.dma("sp", lambda e, r=r: e.dma_start(out=self.mod[:, :, 48 * r:48 * r + 48],
                                                           in_=m_g[128 * r:128 * r + 128, :].rearrange("p (i c) -> p i c", i=2)), writes=[self.b_mod])
        for i in range(2):
            for j, off in ((0, 16), (1, 64)):
                S.op("dve", lambda e, i=i, j=j, off=off: e.scalar_tensor_tensor(
                    out=self.gs[:, i, j, :], in0=self.mod[:, i, off:off + 16], scalar=1.0, in1=nm[:, i, j, :],
                    op0=ALU.add, op1=ALU.mult), reads=[self.b_mod, b_nm], writes=[self.b_gs])

    def modv(self, layer, which, c):
        off = {"sh_a": 0, "g_a": 32, "sh_f": 48, "g_f": 80}[which]
        return self.mod[:, layer, off + c:off + c + 1]

    def norm_mod(self, layer, j, ntok=TT, gs_ap=None, out_f32=None):
        S = self.S
        n = ntok
        if not hasattr(self, "b_sq"):
            self.b_sq = [Buf("sq%d" % q) for q in range(4)]
        for q in range(4):
            S.op("act", lambda e, q=q: e.activation(out=self.uT[:, 4 * q:4 * q + 4, 0:n], in_=self.hT[:, 4 * q:4 * q + 4, 0:n], func=AF.Square),
                 reads=[self.b_hT], writes=[self.b_uT, self.b_sq[q]])
        ps, bps = self.next_bank()
        for c in range(DC):
            S.op("pe", lambda e, c=c, ps=ps: e.matmul(ps[:, 0:n], lhsT=self.ones_bf, rhs=self.uT[:, c, 0:n],
                                                       start=(c == 0), stop=(c == DC - 1)),
                 reads=[self.b_sq[c // 4], self.b_cst], writes=[bps])
        S.op("act", lambda e, ps=ps: e.activation(out=self.rstd[:, 0:n], in_=ps[:, 0:n], func=AF.Sqrt, bias=EPS, scale=1.0 / D),
             reads=[bps], writes=[self.b_rstd])
        S.op("dve", lambda e: e.reciprocal(out=self.rstd[:, 0:n], in_=self.rstd[:, 0:n]), reads=[self.b_rstd], writes=[self.b_rstd])
        for c in range(DC):
            if out_f32 is not None:
                S.op("dve", lambda e, c=c: e.scalar_tensor_tensor(
                    out=out_f32[:, c, 0:n], in0=self.hT[:, c, 0:n], scalar=gs_ap[:, c:c + 1], in1=self.rstd[:, 0:n],
                    op0=ALU.mult, op1=ALU.mult), reads=[self.b_hT, self.b_rstd, self.b_gs], writes=[self.b_outf])
                continue
            tmp, bt = self.ntmp[c % 2], self.b_ntmp[c % 2]
            S.op("dve", lambda e, c=c, tmp=tmp: e.scalar_tensor_tensor(
                out=tmp[:, 0:n], in0=self.hT[:, c, 0:n], scalar=self.gs[:, layer, j, c:c + 1], in1=self.rstd[:, 0:n],
                op0=ALU.mult, op1=ALU.mult), reads=[self.b_hT, self.b_rstd, self.b_gs], writes=[bt])
            sh = self.modv(layer, "sh_a" if j == 0 else "sh_f", c)
            S.op("act", lambda e, c=c, tmp=tmp, sh=sh: e.activation(out=self.uT[:, c, 0:n], in_=tmp[:, 0:n], func=AF.Identity, bias=sh, scale=1.0),
                 reads=[bt, self.b_mod], writes=[self.b_uT, self.b_sq[c // 4]])

    def mlp(self, layer, w_up_t, w_down_t, nfb=DFF // 2048, do_down=True, wl=None):
        S = self.S
        self.norm_mod(layer, 1)
        npc = CW // 128
        wl = layer if wl is None else wl
        for fb in range(nfb):
            for ct in range(2048 // CW):
                wt, bw = self.wtile(w_up_t, (wl, fb * (2048 // CW) + ct))
                for jj in range(npc):
                    jf = ct * npc + jj
                    ps, bps = self.next_bank()
                    for k in range(DC):
                        S.op("pe", lambda e, ps=ps, wt=wt, jj=jj, k=k: e.matmul(
                            ps[:], lhsT=wt[:, k, jj * 128:(jj + 1) * 128], rhs=self.uT[:, k, :], start=(k == 0), stop=(k == DC - 1)),
                            reads=[bw, self.b_uT], writes=[bps])
                    rl, brl = self.rl[jf % 2], self.b_rl[jf % 2]
                    S.op("act", lambda e, ps=ps, rl=rl: e.activation(out=rl[:], in_=ps[:], func=AF.Relu), reads=[bps], writes=[brl])
                    S.op("dve", lambda e, rl=rl, jf=jf: e.tensor_tensor(out=self.hid[:, jf, :], in0=rl[:], in1=rl[:], op=ALU.mult),
                         reads=[brl], writes=[self.b_hid])
            for ct in range(D // CW if do_down else 0):
                wt, bw = self.wtile(w_down_t, (wl * (DFF // 2048) + fb, ct))
                for jj in range(npc):
                    d = ct * npc + jj
                    ps, bps = self.next_bank()
                    for k in range(16):
                        S.op("pe", lambda e, ps=ps, wt=wt, jj=jj, k=k: e.matmul(
                            ps[:], lhsT=wt[:, k, jj * 128:(jj + 1) * 128], rhs=self.hid[:, k, :], start=(k == 0), stop=(k == 15)),
                            reads=[bw, self.b_hid], writes=[bps])
                    g = self.modv(layer, "g_f", d)
                    S.op("dve", lambda e, ps=ps, d=d, g=g: e.scalar_tensor_tensor(
                        out=self.hT[:, d, :], in0=ps[:], scalar=g, in1=self.hT[:, d, :], op0=ALU.mult, op1=ALU.add),
                        reads=[bps, self.b_mod, self.b_hT], writes=[self.b_hT])

    def finish(self, final_ops):
        self.S.emit(final_wait_ops=final_ops)
        self.pst.close()
        self.st.close()
        return self.nc


def make_consts():
    k = np.arange(128)
    L = (k[:, None] <= k[None, :]).astype(np.float32)
    U = (k[:, None] > k[None, :]).astype(np.float32)
    ones = np.ones((128, 128), np.float32)
    ident = np.eye(128, dtype=np.float32)
    return np.ascontiguousarray(np.stack([L, U, ones, ident], axis=1))


def ssd_setup(P, d):
    S = P.S
    P.seg_ps = P.ps("segps", [128, 512]); P.b_seg = Buf("seg")
    P.tp_ps = P.ps("tpps", [128, 1024], BF16); P.b_tp = Buf("tp")
    P.yd_ps = P.ps("ydps", [128, 512]); P.b_yd = Buf("yd")
    P.yo_ps = P.ps("yops", [128, 512]); P.b_yo = Buf("yo")
    P.st_ps = P.ps("stps", [128, 512]); P.b_st = Buf("st")
    P.misc_ps = P.ps("miscps", [128, 512])
    P.b_sc = Buf("misc"); P.b_sm = [P.b_sc] * 4
    P.zs = P.sb("zs", [128, 4, 512], BF16); P.b_zs = Buf("zs")
    P.xtm = P.sb("xtm", [128, 4, 512], BF16); P.b_xtm = Buf("xtm")
    P.BT = P.sb("BT", [128, 512], BF16); P.b_BT = Buf("BT")
    P.CT = P.sb("CT", [128, 512], BF16); P.b_CT = Buf("CT")
    P.Btm = P.sb("Btm", [128, 4, 128], BF16); P.b_Btm = Buf("Btm")
    P.raw = [P.sb("raw%d" % i, [128, 515]) for i in range(2)]; P.b_raw = [Buf("raw%d" % i) for i in range(2)]
    P.acc = [P.sb("acc%d" % i, [128, 512]) for i in range(2)]; P.b_acc = [Buf("acc%d" % i) for i in range(2)]
    P.xsT = [P.sb("xsT%d" % i, [128, 512], BF16) for i in range(2)]; P.b_xsT = [Buf("xsT%d" % i) for i in range(2)]
    P.tnh = P.rl; P.b_tnh = P.b_rl
    P.nconv = 0
    P.pending_tp = None
    P.halo = P.sb("halo", [128, 48, 3]); P.b_halo = [Buf("halo%d" % i) for i in range(48)]
    P.cw = P.sb("cwc", [128, 48, 4]); P.cb = P.sb("cbc", [128, 48]); P.b_cw = Buf("cw")
    P.dtb = P.sb("dtb", [128, 64]); P.Abc = P.sb("Abc", [128, 64]); P.Dbc = P.sb("Dbc", [128, 64]); P.b_prm = Buf("prm")
    P.wdt = P.sb("wdt", [128, 16, 64], BF16); P.b_wdt = Buf("wdt")
    P.dt = P.sb("dt", [128, 4, 64]); P.dtA = P.sb("dtA", [128, 4, 64]); P.ea = P.sb("ea", [128, 4, 64])
    P.dte = P.sb("dte", [128, 4, 64]); P.cd = P.sb("cd", [128, 4, 64])
    P.b_dt = Buf("dt"); P.b_dtA = Buf("dtA"); P.b_ea = Buf("ea"); P.b_dte = Buf("dte"); P.b_cd = Buf("cd")
    P.sp1 = P.sb("sp1", [128, 64]); P.sp2 = P.sb("sp2", [128, 64]); P.sp3 = P.sb("sp3", [128, 64]); P.b_sp = Buf("sp")
    P.nh = 64
    P.R = [P.sb("R%d" % i, [128, 4, 128]) for i in range(2)]; P.b_R = [Buf("R%d" % i) for i in range(2)]
    P.dec = P.sb("dec", [128, 4, 8, 128], BF16); P.b_dec = [Buf("dec%d" % i) for i in range(4)]
    P.sm = P.sb("smk", [128, 128], BF16); P.b_smk = Buf("smk")
    P.Mh = [P.sb("Mh%d" % i, [128, 8, 128], BF16) for i in range(2)]; P.b_Mh = [Buf("Mh%d" % i) for i in range(2)]
    P.xdt = [P.sb("xdt%d" % i, [128, 512], BF16) for i in range(2)]; P.b_xdt = [Buf("xdt%d" % i) for i in range(2)]
    P.xdte = [P.sb("xdte%d" % i, [128, 512], BF16) for i in range(2)]; P.b_xdte = [Buf("xdte%d" % i) for i in range(2)]
    P.y1 = P.sb("y1", [128, 512]); P.y2 = P.sb("y2", [128, 512]); P.b_y1 = Buf("y1"); P.b_y2 = Buf("y2")
    P.yn = P.sb("yn", [128, 512], BF16); P.b_yn = Buf("yn")
    P.xD = P.sb("xD", [128, 512], BF16); P.b_xD = Buf("xD")
    P.ss = P.sb("ss", [128, 2]); P.b_ss = Buf("ss")
    P.Sst = P.sb("Sst", [128, 8, 512]); P.b_S = [Buf("S%d" % g) for g in range(8)]
    P.Sbf = P.sb("Sbf", [128, 512], BF16); P.b_Sbf = Buf("Sbf")
    P.gn = P.sb("gn", [128, 32]); P.b_gn = Buf("gn")
    P.flag = P.sb("flag_sb", [128, 1]); P.b_flag = Buf("flag")
    S.dma("sp", lambda e: e.dma_start(out=P.cw[:], in_=d["convw_col"]), writes=[P.b_cw])
    S.dma("sp", lambda e: e.dma_start(out=P.cb[:], in_=d["convb_col"]), writes=[P.b_cw])
    S.dma("sp", lambda e: e.dma_start(out=P.dtb[:], in_=d["dtb_bc"]), writes=[P.b_prm])
    S.dma("sp", lambda e: e.dma_start(out=P.Abc[:], in_=d["alog_bc"]), writes=[P.b_prm])
    S.dma("sp", lambda e: e.dma_start(out=P.Dbc[:], in_=d["D_bc"]), writes=[P.b_prm])
    S.dma("sp", lambda e: e.dma_start(out=P.flag[:], in_=d["flag"]), writes=[P.b_flag])
    S.dma("pool", lambda e: e.dma_start(out=P.wdt[:], in_=d["w_dt"]), writes=[P.b_wdt])
    S.dma("sp", lambda e: e.dma_start(out=P.gn[:], in_=d["gn_col"]), writes=[P.b_gn])
    S.op("dve", lambda e: e.tensor_scalar(out=P.cw[:], in0=P.cw[:], scalar1=0.5, scalar2=None, op0=ALU.mult), reads=[P.b_cw], writes=[P.b_cw])
    S.op("dve", lambda e: e.tensor_scalar(out=P.cb[:], in0=P.cb[:], scalar1=0.5, scalar2=None, op0=ALU.mult), reads=[P.b_cw], writes=[P.b_cw])
    S.op("dve", lambda e: e.tensor_scalar(out=P.gn[:], in0=P.gn[:], scalar1=0.5, scalar2=None, op0=ALU.mult), reads=[P.b_gn], writes=[P.b_gn])
    S.op("act", lambda e: e.activation(out=P.Abc[:], in_=P.Abc[:], func=AF.Exp), reads=[P.b_prm], writes=[P.b_prm])
    S.op("dve", lambda e: e.tensor_scalar(out=P.Abc[:], in0=P.Abc[:], scalar1=-1.0, scalar2=None, op0=ALU.mult),
         reads=[P.b_prm], writes=[P.b_prm])


def ssd_init_state(P, S_in, halo_in):
    S = P.S
    bs = P.b_S
    S.dma("sp", lambda e: e.dma_start(out=P.Sst[:].rearrange("p g c -> p (g c)"), in_=S_in), writes=bs)
    S.dma("sp", lambda e: e.dma_start(out=P.halo[:], in_=halo_in), writes=P.b_halo)
    S.op("dve", lambda e: e.tensor_scalar(out=P.Sst[:], in0=P.Sst[:], scalar1=P.flag[:, 0:1], scalar2=None, op0=ALU.mult),
         reads=bs + [P.b_flag], writes=bs)
    S.op("dve", lambda e: e.tensor_scalar(out=P.halo[:], in0=P.halo[:], scalar1=P.flag[:, 0:1], scalar2=None, op0=ALU.mult),
         reads=P.b_halo + [P.b_flag], writes=P.b_halo)


def ssd_dt(P):
    S = P.S
    mp = P.misc_ps
    for sub in range(4):
        nh = P.nh
        ps = mp[:, 320:320 + nh]; bps = P.b_sm[3]
        for k in range(DC):
            S.op("pe", lambda e, ps=ps, k=k, sub=sub: e.matmul(ps, lhsT=P.uT[:, k, sub * 128:(sub + 1) * 128], rhs=P.wdt[:, k, :],
                                                           start=(k == 0), stop=(k == DC - 1)),
                 reads=[P.b_uT, P.b_wdt], writes=[bps])
        S.op("dve", lambda e, ps=ps: e.tensor_tensor(out=P.sp1[:], in0=ps, in1=P.dtb[:], op=ALU.add), reads=[bps, P.b_prm], writes=[P.b_sp])
        S.op("dve", lambda e: e.scalar_tensor_tensor(out=P.sp2[:], in0=P.sp1[:], scalar=-1.0, in1=P.sp1[:], op0=ALU.mult, op1=ALU.min),
             reads=[P.b_sp], writes=[P.b_sp])
        S.op("act", lambda e: e.activation(out=P.sp2[:], in_=P.sp2[:], func=AF.Exp), reads=[P.b_sp], writes=[P.b_sp])
        S.op("act", lambda e: e.activation(out=P.sp3[:], in_=P.sp2[:], func=AF.Ln, bias=1.0, scale=1.0), reads=[P.b_sp], writes=[P.b_sp])
        S.op("dve", lambda e, sub=sub: e.scalar_tensor_tensor(out=P.dt[:, sub, :], in0=P.sp1[:], scalar=0.0, in1=P.sp3[:], op0=ALU.max, op1=ALU.add),
             reads=[P.b_sp], writes=[P.b_dt])
        S.op("dve", lambda e, sub=sub: e.tensor_tensor(out=P.dtA[:, sub, :], in0=P.dt[:, sub, :], in1=P.Abc[:], op=ALU.mult),
             reads=[P.b_dt, P.b_prm], writes=[P.b_dtA])
        for i, (lhs, dst, bd) in enumerate(((P.Lm, P.ea, P.b_ea), (P.Um, P.dte, P.b_dte), (P.ones, P.cd, P.b_cd))):
            pss = mp[:, 128 + 64 * i:128 + 64 * i + nh]; bp = P.b_sm[i]
            S.op("pe", lambda e, pss=pss, lhs=lhs, sub=sub: e.matmul(pss, lhsT=lhs, rhs=P.dtA[:, sub, :], start=True, stop=True),
                 reads=[P.b_dtA, P.b_cst], writes=[bp])
            S.op("act", lambda e, pss=pss, dst=dst, sub=sub: e.activation(out=dst[:, sub, :], in_=pss, func=AF.Exp), reads=[bp], writes=[bd])


def ssd_conv_front(P, ps, bps, ci):
    S = P.S
    bh = P.b_halo[ci]
    i = P.nconv % 2
    P.nconv += 1
    raw, braw, acc, bacc = P.raw[i], P.b_raw[i], P.acc[i], P.b_acc[i]
    S.op("act", lambda e: e.activation(out=raw[:, 3:515], in_=ps[:], func=AF.Copy), reads=[bps], writes=[braw])
    S.op("act", lambda e: e.activation(out=acc[:], in_=ps[:], func=AF.Identity, bias=P.cb[:, ci:ci + 1], scale=P.cw[:, ci, 3:4]),
         reads=[bps, P.b_cw], writes=[bacc])
    S.op("dve", lambda e: e.tensor_copy(out=raw[:, 0:3], in_=P.halo[:, ci, :]), reads=[bh], writes=[braw])
    return i


def ssd_conv_back(P, i, ci, out_ap, b_out):
    S = P.S
    bh = P.b_halo[ci]
    raw, braw, acc, bacc, tnh, btnh = P.raw[i], P.b_raw[i], P.acc[i], P.b_acc[i], P.tnh[i], P.b_tnh[i]
    for j in (2, 1, 0):
        S.op("dve", lambda e, j=j: e.scalar_tensor_tensor(out=acc[:], in0=raw[:, j:j + 512], scalar=P.cw[:, ci, j:j + 1], in1=acc[:],
                                                          op0=ALU.mult, op1=ALU.add), reads=[braw, bacc, P.b_cw], writes=[bacc])
    S.op("dve", lambda e: e.tensor_copy(out=P.halo[:, ci, :], in_=raw[:, 512:515]), reads=[braw], writes=[bh])
    S.op("act", lambda e: e.activation(out=tnh[:], in_=acc[:], func=AF.Tanh), reads=[bacc], writes=[btnh])
    S.op("dve", lambda e: e.scalar_tensor_tensor(out=out_ap, in0=tnh[:], scalar=1.0, in1=acc[:], op0=ALU.add, op1=ALU.mult),
         reads=[btnh, bacc], writes=[b_out])


def ssd_transpose4(P, src, b_src, dst_fn, b_dst):
    S = P.S
    for sub in range(4):
        S.op("pe", lambda e, sub=sub: e.transpose(out=P.tp_ps[:, sub * 128:(sub + 1) * 128], in_=src[:, sub * 128:(sub + 1) * 128],
                                                   identity=P.ident_bf), reads=[b_src, P.b_cst], writes=[P.b_tp])
    for sub in range(4):
        S.op("act", lambda e, sub=sub: e.activation(out=dst_fn(sub), in_=P.tp_ps[:, sub * 128:(sub + 1) * 128], func=AF.Copy),
             reads=[P.b_tp], writes=[b_dst])


def ssd_dec_steps(P, g):
    S = P.S
    steps = []
    for c in range(4):
        for hh in range(2):
            def step(c=c, hh=hh):
                i = (c * 2 + hh) % 2
                R, bR = P.R[i], P.b_R[i]
                h4 = slice(g * 8 + hh * 4, g * 8 + hh * 4 + 4)
                S.op("dve", lambda e: e.tensor_tensor(out=R[:], in0=P.dtA[:, c, h4].unsqueeze(2).to_broadcast([128, 4, 128]),
                                                      in1=P.Lm.unsqueeze(1).to_broadcast([128, 4, 128]), op=ALU.mult),
                     reads=[P.b_dtA, P.b_cst], writes=[bR])
                S.op("pe", lambda e: e.matmul(P.seg_ps[:], lhsT=P.Um, rhs=R[:].rearrange("p r t -> p (r t)"), start=True, stop=True),
                     reads=[bR, P.b_cst], writes=[P.b_seg])
                S.op("act", lambda e: e.activation(out=P.dec[:, c, hh * 4:(hh + 1) * 4, :], in_=P.seg_ps[:].rearrange("p (r t) -> p r t", r=4), func=AF.Exp),
                     reads=[P.b_seg], writes=[P.b_dec[c]])
            steps.append(step)
    return steps


def ssd_flush_tp(P):
    if P.pending_tp is not None:
        f = P.pending_tp
        P.pending_tp = None
        f()


def ssd_group(P, g, w_in_g, pre):
    S = P.S
    mp = P.misc_ps
    steps = [] if pre else ssd_dec_steps(P, g)

    def step():
        if steps:
            steps.pop(0)()
    if not pre:
        wts = [P.wtile(w_in_g, (g, ct)) for ct in range(2)]
        for sub in range(4):
            ps, bps = P.next_bank()
            for ct in range(2):
                wt, bw = wts[ct]
                for k in range(DC):
                    S.op("pe", lambda e, ps=ps, wt=wt, ct=ct, k=k, sub=sub: e.matmul(
                        ps[:, ct * 256:(ct + 1) * 256], lhsT=P.uT[:, k, sub * 128:(sub + 1) * 128], rhs=wt[:, k, :],
                        start=(k == 0), stop=(k == DC - 1)), reads=[bw, P.b_uT], writes=[bps])
            tnh, btnh = P.tnh[sub % 2], P.b_tnh[sub % 2]
            S.op("act", lambda e, ps=ps, tnh=tnh: e.activation(out=tnh[:], in_=ps[:], func=AF.Tanh, scale=0.5), reads=[bps], writes=[btnh])
            S.op("dve", lambda e, ps=ps, sub=sub, tnh=tnh: e.scalar_tensor_tensor(out=P.zs[:, sub, :], in0=tnh[:], scalar=1.0, in1=ps[:], op0=ALU.add, op1=ALU.mult),
                 reads=[btnh, bps], writes=[P.b_zs])
            step()
    for ct in range(2, 5):
        wt, bw = P.wtile(w_in_g, (g, ct))
        for jj in range(2):
            ci = g * 6 + (ct - 2) * 2 + jj
            ps, bps = P.next_bank()
            for k in range(DC):
                S.op("pe", lambda e, ps=ps, wt=wt, jj=jj, k=k: e.matmul(
                    ps[:], lhsT=wt[:, k, jj * 128:(jj + 1) * 128], rhs=P.uT[:, k, :], start=(k == 0), stop=(k == DC - 1)),
                    reads=[bw, P.b_uT], writes=[bps])
            i = ssd_conv_front(P, ps, bps, ci)
            ssd_flush_tp(P)
            step()
            if ct < 4:
                xc = (ct - 2) * 2 + jj
                xs, bxs = P.xsT[i], P.b_xsT[i]
                ssd_conv_back(P, i, ci, xs[:], bxs)
                P.pending_tp = (lambda xs=xs, bxs=bxs, xc=xc: ssd_transpose4(
                    P, xs, bxs, lambda sub: P.xtm[:, sub, xc * 128:(xc + 1) * 128], P.b_xtm))
            elif jj == 0:
                ssd_conv_back(P, i, ci, P.BT[:], P.b_BT)
                P.pending_tp = (lambda: ssd_transpose4(P, P.BT, P.b_BT, lambda sub: P.Btm[:, sub, :], P.b_Btm))
            else:
                ssd_conv_back(P, i, ci, P.CT[:], P.b_CT)
    ssd_flush_tp(P)
    while steps:
        step()
    bS = P.b_S[g]
    Sg = P.Sst[:, g, :]
    hs = slice(g * 8, (g + 1) * 8)

    def front(c):
        i = c % 2
        cs = slice(c * 128, (c + 1) * 128)
        xdt, bxdt, xdte, bxdte = P.xdt[i], P.b_xdt[i], P.xdte[i], P.b_xdte[i]
        S.op("dve", lambda e: e.tensor_tensor(out=xdt[:].rearrange("p (r q) -> p r q", r=8), in0=P.xtm[:, c, :].rearrange("p (r q) -> p r q", r=8),
                                              in1=P.dt[:, c, hs].unsqueeze(2).to_broadcast([128, 8, 64]), op=ALU.mult),
             reads=[P.b_xtm, P.b_dt], writes=[bxdt])
        S.op("dve", lambda e: e.tensor_tensor(out=xdte[:].rearrange("p (r q) -> p r q", r=8), in0=xdt[:].rearrange("p (r q) -> p r q", r=8),
                                              in1=P.dte[:, c, hs].unsqueeze(2).to_broadcast([128, 8, 64]), op=ALU.mult),
             reads=[bxdt, P.b_dte], writes=[bxdte])
        if not pre:
            sc = mp[:, 0:128]
            Mh, bMh = P.Mh[i], P.b_Mh[i]
            S.op("pe", lambda e: e.matmul(sc, lhsT=P.BT[:, cs], rhs=P.CT[:, cs], start=True, stop=True),
                 reads=[P.b_BT, P.b_CT], writes=[P.b_sc])
            S.op("dve", lambda e: e.tensor_tensor(out=P.sm[:], in0=sc, in1=P.Lm, op=ALU.mult), reads=[P.b_sc, P.b_cst], writes=[P.b_smk])
            S.op("dve", lambda e: e.tensor_tensor(out=Mh[:], in0=P.dec[:, c, :, :],
                                                  in1=P.sm[:].unsqueeze(1).to_broadcast([128, 8, 128]), op=ALU.mult),
                 reads=[P.b_dec[c], P.b_smk], writes=[bMh])

    def back(c):
        i = c % 2
        cs = slice(c * 128, (c + 1) * 128)
        xdt, bxdt, xdte, bxdte = P.xdt[i], P.b_xdt[i], P.xdte[i], P.b_xdte[i]
        if not pre:
            Mh, bMh = P.Mh[i], P.b_Mh[i]
            S.op("act", lambda e: e.activation(out=P.Sbf[:], in_=Sg, func=AF.Copy), reads=[bS], writes=[P.b_Sbf])
            S.op("dve", lambda e: e.tensor_tensor(out=P.xD[:].rearrange("p (r q) -> p r q", r=8), in0=P.xtm[:, c, :].rearrange("p (r q) -> p r q", r=8),
                                                  in1=P.Dbc[:, hs].unsqueeze(2).to_broadcast([128, 8, 64]), op=ALU.mult),
                 reads=[P.b_xtm, P.b_prm], writes=[P.b_xD])
            S.op("pe", lambda e: e.matmul(P.yd_ps[:], lhsT=P.ident_bf, rhs=P.xD[:], start=True, stop=False),
                 reads=[P.b_xD, P.b_cst], writes=[P.b_yd])
            for r in range(8):
                S.op("pe", lambda e, r=r: e.matmul(P.yd_ps[:, r * 64:(r + 1) * 64], lhsT=Mh[:, r, :], rhs=xdt[:, r * 64:(r + 1) * 64],
                                                   start=False, stop=(r == 7)), reads=[bMh, bxdt], writes=[P.b_yd])
            S.op("pe", lambda e: e.matmul(P.yo_ps[:], lhsT=P.CT[:, cs], rhs=P.Sbf[:], start=True, stop=True),
                 reads=[P.b_CT, P.b_Sbf], writes=[P.b_yo])
        S.op("pe", lambda e: e.matmul(P.st_ps[:], lhsT=P.Btm[:, c, :], rhs=xdte[:], start=True, stop=True),
             reads=[P.b_Btm, bxdte], writes=[P.b_st])
        if not pre:
            S.op("dve", lambda e: e.tensor_tensor(out=P.y1[:].rearrange("p (r q) -> p r q", r=8), in0=P.yo_ps[:].rearrange("p (r q) -> p r q", r=8),
                                                  in1=P.ea[:, c, hs].unsqueeze(2).to_broadcast([128, 8, 64]), op=ALU.mult),
                 reads=[P.b_yo, P.b_ea], writes=[P.b_y1])
            S.op("dve", lambda e: e.tensor_tensor(out=P.y1[:], in0=P.yd_ps[:], in1=P.y1[:], op=ALU.add), reads=[P.b_yd, P.b_y1], writes=[P.b_y1])
        S.op("dve", lambda e: e.tensor_tensor(out=Sg.rearrange("p (r q) -> p r q", r=8), in0=Sg.rearrange("p (r q) -> p r q", r=8),
                                              in1=P.cd[:, c, hs].unsqueeze(2).to_broadcast([128, 8, 64]), op=ALU.mult),
             reads=[bS, P.b_cd], writes=[bS])
        S.op("dve", lambda e: e.tensor_tensor(out=Sg, in0=P.st_ps[:], in1=Sg, op=ALU.add), reads=[P.b_st, bS], writes=[bS])
        if not pre:
            S.op("dve", lambda e: e.tensor_tensor(out=P.y1[:], in0=P.y1[:], in1=P.zs[:, c, :], op=ALU.mult), reads=[P.b_y1, P.b_zs], writes=[P.b_y1])
            S.op("act", lambda e: e.activation(out=P.y2[:], in_=P.y1[:], func=AF.Square, accum_out=P.ss[:, 0:1]), reads=[P.b_y1, P.b_y2], writes=[P.b_y2, P.b_ss])
            S.op("act", lambda e: e.activation(out=P.ss[:, 1:2], in_=P.ss[:, 0:1], func=AF.Sqrt, bias=EPS, scale=1.0 / 2048), reads=[P.b_ss], writes=[P.b_ss])
            S.op("dve", lambda e: e.reciprocal(out=P.ss[:, 1:2], in_=P.ss[:, 1:2]), reads=[P.b_ss], writes=[P.b_ss])
            S.op("act", lambda e: e.activation(out=P.yn[:], in_=P.y1[:], func=AF.Identity, scale=P.ss[:, 1:2]), reads=[P.b_y1, P.b_ss], writes=[P.b_yn])
            for j in range(4):
                S.op("pe", lambda e, j=j: e.transpose(out=P.tp_ps[:, 512 + j * 128:512 + (j + 1) * 128], in_=P.yn[:, j * 128:(j + 1) * 128],
                                                       identity=P.ident_bf), reads=[P.b_yn, P.b_cst], writes=[P.b_tp])
            for j in range(4):
                kk = g * 4 + j
                S.op("act", lambda e, j=j, kk=kk: e.activation(out=P.yT[:, kk, c * 128:(c + 1) * 128], in_=P.tp_ps[:, 512 + j * 128:512 + (j + 1) * 128],
                                                            func=AF.Identity, scale=P.gn[:, kk:kk + 1]), reads=[P.b_tp, P.b_gn], writes=[P.b_yT])

    front(0)
    for c in range(4):
        if c + 1 < 4:
            front(c + 1)
        back(c)


def ssd_outproj(P, w_out_t, x_reload):
    S = P.S
    S.dma("sp", lambda e: e.dma_start(out=P.hT[:], in_=x_reload), writes=[P.b_hT])
    for ct in range(D // CW):
        wts = [P.wtile(w_out_t, (rb, ct)) for rb in range(2)]
        for jj in range(CW // 128):
            dch = ct * (CW // 128) + jj
            ps, bps = P.next_bank()
            for rb in range(2):
                wt, bw = wts[rb]
                for k in range(16):
                    kk = rb * 16 + k
                    S.op("pe", lambda e, ps=ps, wt=wt, jj=jj, k=k, kk=kk: e.matmul(
                        ps[:], lhsT=wt[:, k, jj * 128:(jj + 1) * 128], rhs=P.yT[:, kk, :], start=(kk == 0), stop=(kk == 31)),
                        reads=[bw, P.b_yT], writes=[bps])
            ga = P.modv(0, "g_a", dch)
            S.op("dve", lambda e, ps=ps, dch=dch, ga=ga: e.scalar_tensor_tensor(
                out=P.hT[:, dch, :], in0=ps[:], scalar=ga, in1=P.hT[:, dch, :], op0=ALU.mult, op1=ALU.add),
                reads=[bps, P.b_mod, P.b_hT], writes=[P.b_hT])


NEG = 30000.0


def fox_in_setup(P, d):
    S = P.S
    P.wf = P.sb("wf", [128, 16, 16], BF16); P.b_wf = Buf("wf")
    P.bfb = P.sb("bfb", [128, 16]); P.b_bfb = Buf("bfb")
    P.stg = P.rl; P.b_stg = P.b_rl
    P.nstg = 0
    P.fx = P.sb("fx", [128, 16]); P.fa = P.sb("fa", [128, 16]); P.fl = P.sb("fl", [128, 16]); P.logf = P.sb("logf", [128, 16]); P.b_f = Buf("f")
    P.ck_sb = P.sb("ck_sb", [128, 4, 16]); P.b_ck = Buf("ck")
    P.cqT = P.sb("cqT", [16, 512]); P.b_cqT = Buf("cqT")
    P.carry = P.sb("carry", [128, 16]); P.carryT = P.sb("carryT", [16, 1]); P.b_carry = Buf("carry")
    S.dma("pool", lambda e: e.dma_start(out=P.wf[:], in_=d["w_f"]), writes=[P.b_wf])
    S.dma("sp", lambda e: e.dma_start(out=P.bfb[:], in_=d["bf_bc"]), writes=[P.b_bfb])
    S.op("dve", lambda e: e.memset(P.carry[:], 0.0), writes=[P.b_carry])
    S.op("dve", lambda e: e.memset(P.carryT[:], 0.0), writes=[P.b_carry])


def fox_inproj(P, tt, d):
    S = P.S
    mp = P.misc_ps
    tsl = slice(tt * TT, (tt + 1) * TT)
    P.h1_store = S.dma("act", lambda e: e.dma_start(out=d["h1T"][:, :, tsl], in_=P.hT[:]), reads=[P.b_hT], writes=[d["b_h1T"]])
    P.norm_mod(1, 0)
    w = d["fox_w_in_t"]
    for ct in range(16):
        wt, bw = P.wtile(w, (0, ct))
        for jj in range(2):
            hq = (ct % 8) * 2 + jj
            ps, bps = P.next_bank()
            for k in range(DC):
                S.op("pe", lambda e, ps=ps, wt=wt, jj=jj, k=k: e.matmul(
                    ps[:], lhsT=wt[:, k, jj * 128:(jj + 1) * 128], rhs=P.uT[:, k, :], start=(k == 0), stop=(k == DC - 1)),
                    reads=[bw, P.b_uT], writes=[bps])
            i = P.nstg % 2; P.nstg += 1
            stg, bst = P.stg[i], P.b_stg[i]
            if ct < 8:
                S.op("act", lambda e, ps=ps, stg=stg: e.activation(out=stg[:], in_=ps[:], func=AF.Copy, scale=float(FD) ** -0.5), reads=[bps], writes=[bst])
                dst, bd = d["qT_d"][hq, :, tsl], d["b_q"]
            else:
                S.op("act", lambda e, ps=ps, stg=stg: e.activation(out=stg[:], in_=ps[:], func=AF.Copy), reads=[bps], writes=[bst])
                dst, bd = d["kT_own"][hq, :, tsl], d["b_k"]
            S.dma("act", lambda e, dst=dst, stg=stg: e.dma_start(out=dst, in_=stg[:]), reads=[bst], writes=[bd])
    for pair in range(4):
        wts = [P.wtile(w, (0, 16 + pair * 2 + c2)) for c2 in range(2)]
        for sub in range(4):
            ps, bps = P.next_bank()
            for c2 in range(2):
                wt, bw = wts[c2]
                for k in range(DC):
                    S.op("pe", lambda e, ps=ps, wt=wt, c2=c2, k=k, sub=sub: e.matmul(
                        ps[:, c2 * 256:(c2 + 1) * 256], lhsT=P.uT[:, k, sub * 128:(sub + 1) * 128], rhs=wt[:, k, :],
                        start=(k == 0), stop=(k == DC - 1)), reads=[bw, P.b_uT], writes=[bps])
            i = P.nstg % 2; P.nstg += 1
            stg, bst = P.stg[i], P.b_stg[i]
            S.op("act", lambda e, ps=ps, stg=stg: e.activation(out=stg[:], in_=ps[:], func=AF.Copy), reads=[bps], writes=[bst])
            kt = tt * 4 + sub
            dst = d["v_own"][pair * 4:(pair + 1) * 4, :, kt, :].rearrange("h p d -> p h d")
            S.dma("act", lambda e, dst=dst, stg=stg: e.dma_start(out=dst, in_=stg[:].rearrange("p (h d) -> p h d", h=4)),
                  reads=[bst], writes=[d["b_v"]])
    bm = P.b_sc
    for sub in range(4):
        fps = mp[:, 0:16]
        for k in range(DC):
            S.op("pe", lambda e, k=k, sub=sub: e.matmul(fps, lhsT=P.uT[:, k, sub * 128:(sub + 1) * 128], rhs=P.wf[:, k, :],
                                                   start=(k == 0), stop=(k == DC - 1)), reads=[P.b_uT, P.b_wf], writes=[bm])
        S.op("dve", lambda e: e.tensor_tensor(out=P.fx[:], in0=fps, in1=P.bfb[:], op=ALU.add), reads=[bm, P.b_bfb], writes=[P.b_f])
        S.op("dve", lambda e: e.scalar_tensor_tensor(out=P.fa[:], in0=P.fx[:], scalar=-1.0, in1=P.fx[:], op0=ALU.mult, op1=ALU.min),
             reads=[P.b_f], writes=[P.b_f])
        S.op("act", lambda e: e.activation(out=P.fa[:], in_=P.fa[:], func=AF.Exp), reads=[P.b_f], writes=[P.b_f])
        S.op("act", lambda e: e.activation(out=P.fl[:], in_=P.fa[:], func=AF.Ln, bias=1.0, scale=1.0), reads=[P.b_f], writes=[P.b_f])
        S.op("dve", lambda e: e.scalar_tensor_tensor(out=P.logf[:], in0=P.fx[:], scalar=0.0, in1=P.fl[:], op0=ALU.min, op1=ALU.subtract),
             reads=[P.b_f], writes=[P.b_f])
        S.op("pe", lambda e: e.matmul(mp[:, 16:32], lhsT=P.Lm, rhs=P.logf[:], start=True, stop=True), reads=[P.b_f, P.b_cst], writes=[bm])
        S.op("pe", lambda e: e.matmul(mp[:, 32:48], lhsT=P.ones, rhs=P.logf[:], start=True, stop=True), reads=[P.b_f, P.b_cst], writes=[bm])
        S.op("pe", lambda e: e.matmul(mp[0:16, 64:192], lhsT=P.logf[:], rhs=P.Lm, start=True, stop=True), reads=[P.b_f, P.b_cst], writes=[bm])
        S.op("pe", lambda e: e.matmul(mp[0:16, 200:201], lhsT=P.logf[:], rhs=P.ones[:, 0:1], start=True, stop=True), reads=[P.b_f, P.b_cst], writes=[bm])
        S.op("dve", lambda e, sub=sub: e.tensor_tensor(out=P.ck_sb[:, sub, :], in0=mp[:, 16:32], in1=P.carry[:], op=ALU.add),
             reads=[bm, P.b_carry], writes=[P.b_ck])
        S.op("dve", lambda e, sub=sub: e.tensor_scalar(out=P.cqT[:, sub * 128:(sub + 1) * 128], in0=mp[0:16, 64:192], scalar1=P.carryT[:, 0:1], scalar2=None,
                                                       op0=ALU.add), reads=[bm, P.b_carry], writes=[P.b_cqT])
        S.op("dve", lambda e: e.tensor_tensor(out=P.carry[:], in0=mp[:, 32:48], in1=P.carry[:], op=ALU.add), reads=[bm, P.b_carry], writes=[P.b_carry])
        S.op("dve", lambda e: e.tensor_tensor(out=P.carryT[:], in0=mp[0:16, 200:201], in1=P.carryT[:], op=ALU.add), reads=[bm, P.b_carry], writes=[P.b_carry])
    S.dma("act", lambda e: e.dma_start(out=d["ck_own"][:, tt * 4:(tt + 1) * 4, :], in_=P.ck_sb[:]), reads=[P.b_ck], writes=[d["b_ckd"]])
    S.dma("act", lambda e: e.dma_start(out=d["cq_d"][:, tsl], in_=P.cqT[:]), reads=[P.b_cqT], writes=[d["b_cqd"]])


def attn_setup(P, d):
    S = P.S
    P.sc_ps = [P.ps("scps%d" % i, [128, 512]) for i in range(3)]; P.b_scp = [Buf("scp%d" % i) for i in range(3)]
    P.o_ps = [P.ps("ops%d" % i, [128, 512]) for i in range(2)]; P.b_o = [Buf("o%d" % i) for i in range(2)]
    P.sum_ps = P.ps("sumps", [128, 512]); P.b_sum = Buf("sum")
    P.kT = [P.sb("kT%d" % i, [128, 2, T], BF16) for i in range(2)]; P.b_kT = [Buf("kT%d" % i) for i in range(2)]
    P.vv = [P.sb("vv%d" % i, [128, 2, 16, 128], BF16) for i in range(2)]; P.b_vv = [Buf("vv%d" % i) for i in range(2)]
    P.qh = [P.sb("qh%d" % i, [128, 512], BF16) for i in range(2)]; P.b_qh = [Buf("qh%d" % i) for i in range(2)]
    P.cqb = [P.sb("cqb%d" % i, [128, 512]) for i in range(2)]; P.b_cqb = [Buf("cqb%d" % i) for i in range(2)]
    P.ssb = [P.sb("ssb%d" % i, [128, 512]) for i in range(3)]; P.b_ssb = [Buf("ssb%d" % i) for i in range(3)]
    P.pt = [P.sb("pt%d" % i, [128, 512], BF16) for i in range(4)]; P.b_pt = [Buf("pt%d" % i) for i in range(4)]
    P.rs = P.sb("rs", [128, 512]); P.b_rs = Buf("rs")
    P.nck = P.sb("nck", [128, 2, 16, 16]); P.b_nck = Buf("nck")
    P.offA = P.sb("offA", [128, 16]); P.b_offA = Buf("offA")
    P.negL = P.sb("negL", [128, 128]); P.b_negL = Buf("negL")
    P.fm1 = P.sb("fm1", [128, 1])
    if not hasattr(P, "flag"):
        P.flag = P.sb("flag_sb", [128, 1]); P.b_flag = Buf("flag")
        S.dma("sp", lambda e: e.dma_start(out=P.flag[:], in_=d["flag"]), writes=[P.b_flag])
    S.op("dve", lambda e: e.tensor_scalar(out=P.negL[:], in0=P.Lm, scalar1=-1.0, scalar2=NEG, op0=ALU.add, op1=ALU.mult),
         reads=[P.b_cst], writes=[P.b_negL])
    S.op("dve", lambda e: e.tensor_scalar(out=P.fm1[:], in0=P.flag[:], scalar1=-1.0, scalar2=NEG, op0=ALU.add, op1=ALU.mult),
         reads=[P.b_flag], writes=[P.b_flag])
    S.dma("sp", lambda e: e.dma_start(out=P.offA[:], in_=d["ck_prev"][127:128, 15, :].to_broadcast([128, 16])), reads=[d["b_ckp"]], writes=[P.b_offA])
    S.op("dve", lambda e: e.tensor_scalar(out=P.offA[:], in0=P.offA[:], scalar1=P.flag[:, 0:1], scalar2=None, op0=ALU.mult),
         reads=[P.b_offA, P.b_flag], writes=[P.b_offA])
    S.dma("sp", lambda e: e.dma_start(out=P.nck[:, 0, :, :], in_=d["ck_prev"]), reads=[d["b_ckp"]], writes=[P.b_nck])
    S.dma("sp", lambda e: e.dma_start(out=P.nck[:, 1, :, :], in_=d["ck_own"]), reads=[d["b_ckd"]], writes=[P.b_nck])
    S.op("dve", lambda e: e.tensor_scalar(out=P.nck[:, 0, :, :], in0=P.nck[:, 0, :, :], scalar1=-1.0, scalar2=P.fm1[:, 0:1], op0=ALU.mult, op1=ALU.add),
         reads=[P.b_nck, P.b_flag], writes=[P.b_nck])
    S.op("dve", lambda e: e.tensor_tensor(out=P.nck[:, 1, :, :], in0=P.nck[:, 1, :, :], in1=P.offA[:].unsqueeze(1).to_broadcast([128, 16, 16]), op=ALU.add),
         reads=[P.b_nck, P.b_offA], writes=[P.b_nck])
    S.op("dve", lambda e: e.tensor_scalar(out=P.nck[:, 1, :, :], in0=P.nck[:, 1, :, :], scalar1=-1.0, scalar2=None, op0=ALU.mult),
         reads=[P.b_nck], writes=[P.b_nck])


def attn_tile(P, j, d):
    S = P.S
    LA = 2
    qsl = slice(j * TT, (j + 1) * TT)
    nown = 4 * j + 4

    def load_head(h):
        i2 = h % 2
        kT, bk = P.kT[i2], P.b_kT[i2]
        vv, bv = P.vv[i2], P.b_vv[i2]
        qh, bq = P.qh[i2], P.b_qh[i2]
        cqb, bc = P.cqb[i2], P.b_cqb[i2]
        S.dma("sp", lambda e: e.dma_start(out=qh[:], in_=d["qT_d"][h, :, qsl]), writes=[bq])
        S.dma("sp", lambda e: e.dma_start(out=cqb[:], in_=d["cq_d"][h:h + 1, qsl].to_broadcast([128, TT])), writes=[bc])
        rk = [d["b_kp_l"][h // 4]] if "b_kp_l" in d else []
        rv = [d["b_vp_l"][h // 4]] if "b_vp_l" in d else []
        S.dma("sp", lambda e: e.dma_start(out=kT[:, 0, :], in_=d["kT_prev"][h]), reads=rk, writes=[bk])
        S.dma("sp", lambda e: e.dma_start(out=vv[:, 0, :, :], in_=d["v_prev"][h]), reads=rv, writes=[bv])
        S.dma("sp", lambda e: e.dma_start(out=kT[:, 1, 0:nown * 128], in_=d["kT_own"][h, :, 0:nown * 128]), writes=[bk])
        S.dma("sp", lambda e: e.dma_start(out=vv[:, 1, 0:nown, :], in_=d["v_own"][h, :, 0:nown, :]), writes=[bv])
        S.op("dve", lambda e: e.tensor_scalar(out=cqb[:], in0=cqb[:], scalar1=P.offA[:, h:h + 1], scalar2=None, op0=ALU.add),
             reads=[bc, P.b_offA], writes=[bc])

    its = []
    for h in range(FH):
        tiles = [(0, kt, 0) for kt in range(16)] + [(1, kt, max(0, kt - 4 * j)) for kt in range(nown)]
        for n, (s_, kt, a) in enumerate(tiles):
            its.append((h, s_, kt, a, n == 0, n == len(tiles) - 1))
    N = len(its)

    def front(idx):
        h, s_, kt, a, first, last = its[idx]
        if first:
            load_head(h)
        i2 = h % 2
        kT, bk, qh, bq, cqb, bc = P.kT[i2], P.b_kT[i2], P.qh[i2], P.b_qh[i2], P.cqb[i2], P.b_cqb[i2]
        q0 = a * 128
        scp, bs = P.sc_ps[idx % 3], P.b_scp[idx % 3]
        ssb, bss = P.ssb[idx % 3], P.b_ssb[idx % 3]
        pt, bp = P.pt[idx % 4], P.b_pt[idx % 4]
        S.op("pe", lambda e: e.matmul(scp[:, q0:TT], lhsT=kT[:, s_, kt * 128:(kt + 1) * 128], rhs=qh[:, q0:TT], start=True, stop=True),
             reads=[bk, bq], writes=[bs])
        S.op("dve", lambda e: e.tensor_tensor(out=ssb[:, q0:TT], in0=scp[:, q0:TT], in1=cqb[:, q0:TT], op=ALU.add), reads=[bs, bc], writes=[bss])
        if s_ == 1 and kt >= 4 * j:
            S.op("dve", lambda e: e.tensor_tensor(out=ssb[:, q0:q0 + 128], in0=ssb[:, q0:q0 + 128], in1=P.negL[:], op=ALU.add),
                 reads=[bss, P.b_negL], writes=[bss])
        S.op("act", lambda e: e.activation(out=pt[:, q0:TT], in_=ssb[:, q0:TT], func=AF.Exp, bias=P.nck[:, s_, kt, h:h + 1], scale=1.0),
             reads=[bss, P.b_nck], writes=[bp])

    def back(idx):
        h, s_, kt, a, first, last = its[idx]
        i2 = h % 2
        vv, bv = P.vv[i2], P.b_vv[i2]
        q0 = a * 128
        pt, bp = P.pt[idx % 4], P.b_pt[idx % 4]
        ops, bo = P.o_ps[i2], P.b_o[i2]
        S.op("pe", lambda e: e.matmul(ops[:, q0:TT], lhsT=vv[:, s_, kt, :], rhs=pt[:, q0:TT], start=first, stop=last), reads=[bv, bp], writes=[bo])
        S.op("pe", lambda e: e.matmul(P.sum_ps[:, q0:TT], lhsT=P.ones_bf, rhs=pt[:, q0:TT], start=first, stop=last), reads=[bp, P.b_cst], writes=[P.b_sum])
        if last:
            S.op("dve", lambda e: e.reciprocal(out=P.rs[:], in_=P.sum_ps[:]), reads=[P.b_sum], writes=[P.b_rs])
            S.op("dve", lambda e: e.tensor_tensor(out=P.yT[:, h, :], in0=ops[:], in1=P.rs[:], op=ALU.mult), reads=[bo, P.b_rs], writes=[P.b_yT])

    for idx in range(N + LA):
        if idx < N:
            front(idx)
        if idx - LA >= 0:
            back(idx - LA)


def fox_outproj(P, j, d):
    S = P.S
    S.dma("sp", lambda e: e.dma_start(out=P.hT[:], in_=d["h1T"][:, :, j * TT:(j + 1) * TT]), reads=[d["b_h1T"]], writes=[P.b_hT])
    for ct in range(D // CW):
        wt, bw = P.wtile(d["fox_w_out_t"], (0, ct))
        for jj in range(CW // 128):
            dch = ct * (CW // 128) + jj
            ps, bps = P.next_bank()
            for k in range(16):
                S.op("pe", lambda e, ps=ps, wt=wt, jj=jj, k=k: e.matmul(
                    ps[:], lhsT=wt[:, k, jj * 128:(jj + 1) * 128], rhs=P.yT[:, k, :], start=(k == 0), stop=(k == 15)),
                    reads=[bw, P.b_yT], writes=[bps])
            ga = P.modv(1, "g_a", dch)
            S.op("dve", lambda e, ps=ps, dch=dch, ga=ga: e.scalar_tensor_tensor(
                out=P.hT[:, dch, :], in0=ps[:], scalar=ga, in1=P.hT[:, dch, :], op0=ALU.mult, op1=ALU.add),
                reads=[bps, P.b_mod, P.b_hT], writes=[P.b_hT])


def build_program(mode, ntiles=NTT, do_mlp=True):
    P = Prog()
    S = P.S
    d = {}
    nmix = P.din("nmix", [128, 2, DC]); nmlp = P.din("nmlp", [128, 2, DC])
    P.setup_common()
    finals = []
    if mode == "l1":
        c_col = P.din("c_col", [128, DC])
        w_ada_t = P.din("w_ada_t", [2, 48, 128, 16, CW])
        b_ada_col = P.din("b_ada_col", [128, 2, 96])
        mod_o = P.dout("mod_o", [128, 2, 96])
        P.adaln(w_ada_t, b_ada_col, c_col, nmix, nmlp, mod_out=mod_o)
        finals.append(P.mod_store)
    else:
        mod_in = P.din("mod_in", [128, 2, 96])
        P.adaln(None, None, None, nmix, nmlp, mod_in=mod_in)
    d["flag"] = P.din("flag", [128, 1])
    if mode in ("l1", "l2"):
        xT = P.din("xT", [128, DC, T])
        d["w_in_g"] = P.din("w_in_g", [8, 5, 128, 16, CW]); d["w_dt"] = P.din("w_dt", [128, 16, 64])
        d["convw_col"] = P.din("convw_col", [128, 48, 4]); d["convb_col"] = P.din("convb_col", [128, 48])
        for n in ("dtb_bc", "alog_bc", "D_bc"):
            d[n] = P.din(n, [128, 64])
        d["gn_col"] = P.din("gn_col", [128, 32])
        S_in = P.din("S_in", [128, 4096]); halo_in = P.din("halo_in", [128, 48, 3])
        P.phase_alloc = True
        ssd_setup(P, d)
        ssd_init_state(P, S_in, halo_in)
    if mode == "l1":
        S_out = P.dout("S_out", [128, 4096]); halo_out = P.dout("halo_out", [128, 48, 3])
        for tt in range(NTT):
            S.dma("sp", lambda e, tt=tt: e.dma_start(out=P.hT[:], in_=xT[:, :, tt * TT:(tt + 1) * TT]), writes=[P.b_hT])
            P.norm_mod(0, 0)
            ssd_dt(P)
            for g in range(8):
                ssd_group(P, g, d["w_in_g"], True)
        finals.append(S.dma("act", lambda e: e.dma_start(out=S_out, in_=P.Sst[:].rearrange("p g c -> p (g c)")), reads=P.b_S, writes=[Buf()]))
        finals.append(S.dma("act", lambda e: e.dma_start(out=halo_out, in_=P.halo[:]), reads=P.b_halo, writes=[Buf()]))
    if mode == "l2":
        d["w_out_t"] = P.din("w_out_t", [2, 8, 128, 16, CW])
        w_up_t = P.din("w_up_t", [1, 32, 128, 16, CW]); w_down_t = P.din("w_down_t", [4, 8, 128, 16, CW])
        d["fox_w_in_t"] = P.din("fox_w_in_t", [1, 24, 128, 16, CW]); d["w_f"] = P.din("w_f", [128, 16, 16]); d["bf_bc"] = P.din("bf_bc", [128, 16])
        d["h1T"] = P.dout("h1T", [128, DC, T]); d["qT_d"] = P.dout("qT_d", [FH, 128, T], BF16)
        d["kT_own"] = P.dout("kT_own", [FH, 128, T], BF16); d["v_own"] = P.dout("v_own", [FH, 128, 16, 128], BF16)
        d["ck_own"] = P.dout("ck_own", [128, 16, 16]); d["cq_d"] = P.dout("cq_d", [FH, T])
        for n in ("b_h1T", "b_q", "b_k", "b_v", "b_ckd", "b_cqd"):
            d[n] = Buf(n)
        fox_in_setup(P, d)
        for tt in range(ntiles):
            xs = xT[:, :, tt * TT:(tt + 1) * TT]
            S.dma("sp", lambda e, xs=xs: e.dma_start(out=P.hT[:], in_=xs), writes=[P.b_hT])
            P.norm_mod(0, 0)
            ssd_dt(P)
            for g in range(8):
                ssd_group(P, g, d["w_in_g"], False)
            ssd_outproj(P, d["w_out_t"], xs)
            if do_mlp:
                P.mlp(0, w_up_t, w_down_t, wl=0)
            fox_inproj(P, tt, d)
        S.barrier()
    if mode == "l3":
        w_up_t = P.din("w_up_t", [1, 32, 128, 16, CW]); w_down_t = P.din("w_down_t", [4, 8, 128, 16, CW])
        d["fox_w_out_t"] = P.din("fox_w_out_t", [1, 8, 128, 16, CW])
        fnorm = P.din("fnorm", [128, DC])
        d["h1T"] = P.din("h1T", [128, DC, T]); d["qT_d"] = P.din("qT_d", [FH, 128, T], BF16)
        d["kT_own"] = P.din("kT_own", [FH, 128, T], BF16); d["v_own"] = P.din("v_own", [FH, 128, 16, 128], BF16)
        d["kT_prev"] = P.din("kT_prev", [FH, 128, T], BF16); d["v_prev"] = P.din("v_prev", [FH, 128, 16, 128], BF16)
        d["ck_own"] = P.din("ck_own", [128, 16, 16]); d["ck_prev"] = P.din("ck_prev", [128, 16, 16]); d["cq_d"] = P.din("cq_d", [FH, T])
        outT = P.dout("outT", [128, DC, T])
        for n in ("b_h1T", "b_q", "b_k", "b_v", "b_ckd", "b_cqd", "b_kp", "b_vp", "b_ckp"):
            d[n] = Buf(n)
        P.phase_alloc = True
        attn_setup(P, d)
        fn = P.sb("fn", [128, DC]); b_fn = Buf("fn")
        S.dma("sp", lambda e: e.dma_start(out=fn[:], in_=fnorm), writes=[b_fn])
        P.b_outf = P.b_hT
        for j in range(ntiles):
            attn_tile(P, j, d)
            fox_outproj(P, j, d)
            if do_mlp:
                P.mlp(1, w_up_t, w_down_t, wl=0)
            P.b_gs_save = P.b_gs
            P.norm_mod(1, 0, gs_ap=fn, out_f32=P.hT)
            finals.append(S.dma("act", lambda e, j=j: e.dma_start(out=outT[:, :, j * TT:(j + 1) * TT], in_=P.hT[:]), reads=[P.b_hT], writes=[Buf()]))
    S.barrier()
    return P.finish(finals)


def _stack2(a):
    return np.ascontiguousarray(np.stack([col_layout(a[i]) for i in range(2)], axis=1))


def host_prepare(inp):
    H = {}
    H["consts"] = make_consts()
    H["nmix"] = _stack2(inp["norm_mix"]); H["nmlp"] = _stack2(inp["norm_mlp"])
    H["w_ada_t"] = np.stack([tile_weight(inp["w_ada"][i])[0] for i in range(2)])
    H["b_ada_col"] = _stack2(inp["b_ada"])
    w = inp["ssd_w_in"][0]
    tiles = []
    for g in range(8):
        wg = np.concatenate([w[:, g * 512:(g + 1) * 512], w[:, 4096 + g * 512:4096 + (g + 1) * 512],
                             w[:, 8192 + g * 128:8192 + (g + 1) * 128], w[:, 9216 + g * 128:9216 + (g + 1) * 128]], axis=1)
        tiles.append(tile_weight(wg)[0])
    H["w_in_g"] = np.stack(tiles)
    H["w_dt"] = tile_weight(w[:, 10240:10304], cw=64)[0, 0]
    chans = []
    for g in range(8):
        for j in range(4):
            chans.append(np.arange(g * 512 + j * 128, g * 512 + (j + 1) * 128))
        chans.append(np.arange(4096 + g * 128, 4096 + (g + 1) * 128))
        chans.append(np.arange(5120 + g * 128, 5120 + (g + 1) * 128))
    chans = np.stack(chans)
    H["convw_col"] = np.ascontiguousarray(inp["ssd_conv_w"][0][:, chans].transpose(2, 1, 0))
    H["convb_col"] = np.ascontiguousarray(inp["ssd_conv_b"][0][chans].T)
    H["dtb_bc"] = np.ascontiguousarray(np.broadcast_to(inp["ssd_dt_bias"][0], (128, 64)))
    H["alog_bc"] = np.ascontiguousarray(np.broadcast_to(inp["ssd_A_log"][0], (128, 64)))
    H["D_bc"] = np.ascontiguousarray(np.broadcast_to(inp["ssd_D"][0], (128, 64)))
    H["gn_col"] = col_layout(inp["ssd_gnorm"][0])
    H["w_out_t"] = tile_weight(inp["ssd_w_out"][0])
    H["w_up_t"] = [tile_weight(inp["w_up"][i]) for i in range(2)]
    H["w_down_t"] = [tile_weight(inp["w_down"][i]) for i in range(2)]
    fw = inp["fox_w_in"][0]
    H["fox_w_in_t"] = tile_weight(fw[:, :6144])
    H["w_f"] = tile_weight(fw[:, 6144:6160], cw=16)[0, 0]
    H["bf_bc"] = np.ascontiguousarray(np.broadcast_to(inp["fox_b_f"][0], (128, 16)))
    H["fox_w_out_t"] = tile_weight(inp["fox_w_out"][0])
    H["fnorm"] = col_layout(inp["final_norm"])
    return H


_PROGS = {}


def _prog(mode):
    if mode not in _PROGS:
        _PROGS[mode] = build_program(mode)
    return _PROGS[mode]


def kernel(**inp):
    inp = {k: np.asarray(v) for k, v in inp.items()}
    H = host_prepare(inp)
    cores = list(range(8))
    xT = [feat_major(inp["x"][c // 2, (c % 2) * T:(c % 2 + 1) * T]) for c in cores]
    flag = [np.full((128, 1), float(c % 2), np.float32) for c in cores]
    base = {k: H[k] for k in ("consts", "nmix", "nmlp")}
    ssdw = {k: H[k] for k in ("w_in_g", "w_dt", "convw_col", "convb_col", "dtb_bc", "alog_bc", "D_bc", "gn_col")}
    zS = np.zeros((128, 4096), np.float32); zh = np.zeros((128, 48, 3), np.float32)
    maps = []
    for c in cores:
        m = dict(base); m.update(ssdw)
        m.update({"c_col": col_layout(inp["c"][c // 2]), "w_ada_t": H["w_ada_t"], "b_ada_col": H["b_ada_col"],
                  "flag": np.zeros((128, 1), np.float32), "xT": xT[c], "S_in": zS, "halo_in": zh})
        maps.append(m)
    r1 = run_bass_kernel_spmd(_prog("l1"), maps, core_ids=cores).results
    maps = []
    for c in cores:
        a = c - (c % 2)
        m = dict(base); m.update(ssdw)
        m.update({"mod_in": r1[c]["mod_o"], "flag": flag[c], "xT": xT[c], "S_in": r1[a]["S_out"], "halo_in": r1[a]["halo_out"],
                  "w_out_t": H["w_out_t"], "w_up_t": H["w_up_t"][0], "w_down_t": H["w_down_t"][0],
                  "fox_w_in_t": H["fox_w_in_t"], "w_f": H["w_f"], "bf_bc": H["bf_bc"]})
        maps.append(m)
    r2 = run_bass_kernel_spmd(_prog("l2"), maps, core_ids=cores).results
    maps = []
    for c in cores:
        a = c - (c % 2)
        m = dict(base)
        m.update({"mod_in": r1[c]["mod_o"], "flag": flag[c], "w_up_t": H["w_up_t"][1], "w_down_t": H["w_down_t"][1],
                  "fox_w_out_t": H["fox_w_out_t"], "fnorm": H["fnorm"],
                  "h1T": r2[c]["h1T"], "qT_d": r2[c]["qT_d"], "kT_own": r2[c]["kT_own"], "v_own": r2[c]["v_own"],
                  "ck_own": r2[c]["ck_own"], "cq_d": r2[c]["cq_d"],
                  "kT_prev": r2[a]["kT_own"], "v_prev": r2[a]["v_own"], "ck_prev": r2[a]["ck_own"]})
        maps.append(m)
    r3 = run_bass_kernel_spmd(_prog("l3"), maps, core_ids=cores).results
    out = np.empty((BATCH, SEQ, D), np.float32)
    for c in cores:
        out[c // 2, (c % 2) * T:(c % 2 + 1) * T] = from_feat_major(r3[c]["outT"])
    return out


PAIRS = [[0, 1], [2, 3], [4, 5], [6, 7]]


class View:
    def __init__(self, base, **over):
        self.__dict__["_b"] = base
        self.__dict__.update(over)

    def __getattr__(self, k):
        return getattr(self.__dict__["_b"], k)


class HeadChunks:
    def __init__(self, views):
        self.views = views

    def __getitem__(self, key):
        if not isinstance(key, tuple):
            key = (key,)
        h = key[0]
        if isinstance(h, slice):
            c = h.start // 4
            assert h.stop - h.start == 4 and h.start % 4 == 0
            return self.views[c][(slice(0, 4),) + key[1:]]
        return self.views[h // 4][(h % 4,) + key[1:]]


def build_fused(ntiles=NTT):
    P = Prog()
    S = P.S
    nc = P.nc
    d = {}
    nmix = P.din("nmix", [128, 2, DC]); nmlp = P.din("nmlp", [128, 2, DC])
    P.setup_common()
    c_col = P.din("c_col", [128, DC])
    w_ada_t = P.din("w_ada_h", [2, 24, 128, 16, CW])
    b_ada_col = P.din("b_ada_h", [128, 2, 48])
    m_own = nc.dram_tensor("m_own", [128, 96], F32).ap(); m_g = nc.dram_tensor("m_g", [256, 96], F32).ap()
    P.adaln(w_ada_t, b_ada_col, c_col, nmix, nmlp, half_dram=(m_own, m_g))
    d["flag"] = P.din("flag", [128, 1])
    xT = P.din("xT", [128, DC, T])
    d["w_in_g"] = P.din("w_in_g", [8, 5, 128, 16, CW]); d["w_dt"] = P.din("w_dt", [128, 16, 64])
    d["convw_col"] = P.din("convw_col", [128, 48, 4]); d["convb_col"] = P.din("convb_col", [128, 48])
    for n in ("dtb_bc", "alog_bc", "D_bc"):
        d[n] = P.din(n, [128, 64])
    d["gn_col"] = P.din("gn_col", [128, 32])
    d["w_out_t"] = P.din("w_out_t", [2, 8, 128, 16, CW])
    w_up_t = P.din("w_up_t", [2, 32, 128, 16, CW]); w_down_t = P.din("w_down_t", [8, 8, 128, 16, CW])
    d["fox_w_in_t"] = P.din("fox_w_in_t", [1, 24, 128, 16, CW]); d["w_f"] = P.din("w_f", [128, 16, 16]); d["bf_bc"] = P.din("bf_bc", [128, 16])
    d["fox_w_out_t"] = P.din("fox_w_out_t", [1, 8, 128, 16, CW])
    fnorm = P.din("fnorm", [128, DC])
    outT = P.dout("outT", [128, DC, T])
    d["h1T"] = nc.dram_tensor("h1T", [128, DC, T], F32).ap()
    qT2 = nc.dram_tensor("qT2", [FH * 128, T], BF16).ap()
    kT2 = [nc.dram_tensor("kT2_%d" % i, [512, T], BF16).ap() for i in range(4)]
    kT_g = [nc.dram_tensor("kT_g%d" % i, [1024, T], BF16).ap() for i in range(4)]
    v2 = [nc.dram_tensor("v2_%d" % i, [512, 2048], BF16).ap() for i in range(4)]
    v_g = [nc.dram_tensor("v_g%d" % i, [1024, 2048], BF16).ap() for i in range(4)]
    ck2 = nc.dram_tensor("ck2", [128, 256], F32).ap(); ck_g = nc.dram_tensor("ck_g", [256, 256], F32).ap()
    d["cq_d"] = nc.dram_tensor("cq_d", [FH, T], F32).ap()
    d["qT_d"] = qT2.rearrange("(h p) t -> h p t", p=128)
    d["kT_own"] = HeadChunks([x.rearrange("(h p) t -> h p t", p=128) for x in kT2])
    d["kT_prev"] = HeadChunks([x[0:512, :].rearrange("(h p) t -> h p t", p=128) for x in kT_g])
    d["v_own"] = HeadChunks([x.rearrange("(h p) (k e) -> h p k e", p=128, e=128) for x in v2])
    d["v_prev"] = HeadChunks([x[0:512, :].rearrange("(h p) (k e) -> h p k e", p=128, e=128) for x in v_g])
    d["ck_own"] = ck2.rearrange("p (k h) -> p k h", h=16)
    d["ck_prev"] = ck_g[0:128, :].rearrange("p (k h) -> p k h", h=16)
    for n in ("b_h1T", "b_q", "b_k", "b_v", "b_ckd", "b_cqd", "b_kp", "b_vp", "b_ckp"):
        d[n] = Buf(n)
    xpT = P.din("xpT", [128, DC, T])
    w_in_pre = P.din("w_in_pre", [4, 5, 128, 16, CW]); w_dt_pre = P.din("w_dt_pre", [128, 16, 32])
    convw_pre = P.din("convw_pre", [128, 24, 4]); convb_pre = P.din("convb_pre", [128, 24])
    dtb_pre = P.din("dtb_pre", [128, 32]); alog_pre = P.din("alog_pre", [128, 32])
    P.phase_alloc = True
    ssd_setup(P, d)
    cwp = P.sb("cwp", [128, 24, 4]); cbp = P.sb("cbp", [128, 24]); b_cwp = Buf("cwp")
    dtbp = P.sb("dtbp", [128, 32]); Abcp = P.sb("Abcp", [128, 32]); b_prmp = Buf("prmp")
    wdtp = P.sb("wdtp", [128, 16, 32], BF16); b_wdtp = Buf("wdtp")
    S.dma("sp", lambda e: e.dma_start(out=cwp[:], in_=convw_pre), writes=[b_cwp])
    S.dma("sp", lambda e: e.dma_start(out=cbp[:], in_=convb_pre), writes=[b_cwp])
    S.dma("sp", lambda e: e.dma_start(out=dtbp[:], in_=dtb_pre), writes=[b_prmp])
    S.dma("sp", lambda e: e.dma_start(out=Abcp[:], in_=alog_pre), writes=[b_prmp])
    S.dma("pool", lambda e: e.dma_start(out=wdtp[:], in_=w_dt_pre), writes=[b_wdtp])
    S.op("dve", lambda e: e.tensor_scalar(out=cwp[:], in0=cwp[:], scalar1=0.5, scalar2=None, op0=ALU.mult), reads=[b_cwp], writes=[b_cwp])
    S.op("dve", lambda e: e.tensor_scalar(out=cbp[:], in0=cbp[:], scalar1=0.5, scalar2=None, op0=ALU.mult), reads=[b_cwp], writes=[b_cwp])
    S.op("act", lambda e: e.activation(out=Abcp[:], in_=Abcp[:], func=AF.Exp), reads=[b_prmp], writes=[b_prmp])
    S.op("dve", lambda e: e.tensor_scalar(out=Abcp[:], in0=Abcp[:], scalar1=-1.0, scalar2=None, op0=ALU.mult), reads=[b_prmp], writes=[b_prmp])
    Q = View(P, cw=cwp, cb=cbp, b_cw=b_cwp, dtb=dtbp, Abc=Abcp, b_prm=b_prmp, wdt=wdtp, b_wdt=b_wdtp, nh=32,
             dt=P.dt[:, :, 0:32], dtA=P.dtA[:, :, 0:32], ea=P.ea[:, :, 0:32], dte=P.dte[:, :, 0:32], cd=P.cd[:, :, 0:32],
             sp1=P.sp1[:, 0:32], sp2=P.sp2[:, 0:32], sp3=P.sp3[:, 0:32])
    S.op("dve", lambda e: e.memset(P.Sst[:], 0.0), writes=P.b_S)
    S.op("dve", lambda e: e.memset(P.halo[:], 0.0), writes=P.b_halo)
    for tt in range(ntiles):
        S.dma("sp", lambda e, tt=tt: e.dma_start(out=P.hT[:], in_=xpT[:, :, tt * TT:(tt + 1) * TT]), writes=[P.b_hT])
        P.norm_mod(0, 0)
        ssd_dt(Q)
        for gl in range(4):
            ssd_group(Q, gl, w_in_pre, True)
    S_own = nc.dram_tensor("S_own2", [128, 2048], F32).ap(); S_g = nc.dram_tensor("S_g2", [256, 2048], F32).ap()
    h_own = nc.dram_tensor("h_own2", [128, 72], F32).ap(); h_g = nc.dram_tensor("h_g2", [256, 72], F32).ap()
    S.dma("act", lambda e: e.dma_start(out=S_own, in_=P.Sst[:, 0:4, :].rearrange("p g c -> p (g c)")), reads=P.b_S, writes=[Buf()])
    S.dma("act", lambda e: e.dma_start(out=h_own.rearrange("p (c j) -> p c j", j=3), in_=P.halo[:, 0:24, :]), reads=P.b_halo, writes=[Buf()])
    S.barrier()
    S.cc(lambda e: e.collective_compute("AllGather", ALU.bypass, replica_groups=PAIRS, ins=[S_own], outs=[S_g]))
    S.cc(lambda e: e.collective_compute("AllGather", ALU.bypass, replica_groups=PAIRS, ins=[h_own], outs=[h_g]))
    S.barrier()
    for r in range(2):
        S.dma("sp", lambda e, r=r: e.dma_start(out=P.Sst[:, 4 * r:4 * r + 4, :].rearrange("p g c -> p (g c)"), in_=S_g[128 * r:128 * (r + 1), :]), writes=P.b_S)
        S.dma("sp", lambda e, r=r: e.dma_start(out=P.halo[:, 24 * r:24 * r + 24, :], in_=h_g[128 * r:128 * (r + 1), :].rearrange("p (c j) -> p c j", j=3)),
              writes=P.b_halo)
    S.op("dve", lambda e: e.tensor_scalar(out=P.Sst[:], in0=P.Sst[:], scalar1=P.flag[:, 0:1], scalar2=None, op0=ALU.mult),
         reads=P.b_S + [P.b_flag], writes=P.b_S)
    S.op("dve", lambda e: e.tensor_scalar(out=P.halo[:], in0=P.halo[:], scalar1=P.flag[:, 0:1], scalar2=None, op0=ALU.mult),
         reads=P.b_halo + [P.b_flag], writes=P.b_halo)
    fox_in_setup(P, d)
    for tt in range(ntiles):
        xs = xT[:, :, tt * TT:(tt + 1) * TT]
        S.dma("sp", lambda e, xs=xs: e.dma_start(out=P.hT[:], in_=xs), writes=[P.b_hT])
        P.norm_mod(0, 0)
        ssd_dt(P)
        for g in range(8):
            ssd_group(P, g, d["w_in_g"], False)
        ssd_outproj(P, d["w_out_t"], xs)
        P.mlp(0, w_up_t, w_down_t)
        fox_inproj(P, tt, d)
    P.end_phase()
    d["b_ckp"] = Buf("ckp")
    op = S.cc(lambda e: e.collective_compute("AllGather", ALU.bypass, replica_groups=PAIRS, ins=[ck2], outs=[ck_g]))
    d["b_ckp"].writer = op
    d["b_kp_l"] = [Buf("kp%d" % i) for i in range(4)]
    d["b_vp_l"] = [Buf("vp%d" % i) for i in range(4)]
    for i in range(4):
        op = S.cc(lambda e, i=i: e.collective_compute("AllGather", ALU.bypass, replica_groups=PAIRS, ins=[kT2[i]], outs=[kT_g[i]]))
        d["b_kp_l"][i].writer = op
        op = S.cc(lambda e, i=i: e.collective_compute("AllGather", ALU.bypass, replica_groups=PAIRS, ins=[v2[i]], outs=[v_g[i]]))
        d["b_vp_l"][i].writer = op
    attn_setup(P, d)
    fn = P.sb("fn", [128, DC]); b_fn = Buf("fn")
    S.dma("sp", lambda e: e.dma_start(out=fn[:], in_=fnorm), writes=[b_fn])
    P.b_outf = P.b_hT
    finals = []
    for j in range(ntiles):
        attn_tile(P, j, d)
        fox_outproj(P, j, d)
        P.mlp(1, w_up_t, w_down_t)
        P.norm_mod(1, 0, gs_ap=fn, out_f32=P.hT)
        finals.append(S.dma("act", lambda e, j=j: e.dma_start(out=outT[:, :, j * TT:(j + 1) * TT], in_=P.hT[:]), reads=[P.b_hT], writes=[Buf()]))
    S.barrier()
    return P.finish(finals)


_FUSED = []


def kernel(**inp):
    inp = {k: np.asarray(v) for k, v in inp.items()}
    H = host_prepare(inp)
    cores = list(range(8))
    shared = {k: H[k] for k in ("consts", "nmix", "nmlp", "w_in_g", "w_dt", "convw_col", "convb_col",
                                "dtb_bc", "alog_bc", "D_bc", "gn_col", "w_out_t", "fox_w_in_t", "w_f", "bf_bc", "fox_w_out_t", "fnorm")}
    shared["w_up_t"] = np.concatenate(H["w_up_t"], axis=0)
    shared["w_down_t"] = np.concatenate(H["w_down_t"], axis=0)
    maps = []
    xTs = [feat_major(inp["x"][c // 2, (c % 2) * T:(c % 2 + 1) * T]) for c in cores]
    for c in cores:
        m = dict(shared)
        m["xT"] = xTs[c]
        m["c_col"] = col_layout(inp["c"][c // 2])
        m["flag"] = np.full((128, 1), float(c % 2), np.float32)
        hf = c % 2
        m["xpT"] = xTs[c - hf]
        m["w_ada_h"] = np.ascontiguousarray(H["w_ada_t"][:, 24 * hf:24 * hf + 24])
        m["b_ada_h"] = np.ascontiguousarray(H["b_ada_col"][:, :, 48 * hf:48 * hf + 48])
        m["w_in_pre"] = np.ascontiguousarray(H["w_in_g"][4 * hf:4 * hf + 4])
        m["w_dt_pre"] = np.ascontiguousarray(H["w_dt"][:, :, 32 * hf:32 * hf + 32])
        m["convw_pre"] = np.ascontiguousarray(H["convw_col"][:, 24 * hf:24 * hf + 24])
        m["convb_pre"] = np.ascontiguousarray(H["convb_col"][:, 24 * hf:24 * hf + 24])
        m["dtb_pre"] = np.ascontiguousarray(H["dtb_bc"][:, 32 * hf:32 * hf + 32])
        m["alog_pre"] = np.ascontiguousarray(H["alog_bc"][:, 32 * hf:32 * hf + 32])
        maps.append(m)
    if not _FUSED:
        _FUSED.append(build_fused())
    res = run_bass_kernel_spmd(_FUSED[0], maps, core_ids=cores).results
    out = np.empty((BATCH, SEQ, D), np.float32)
    for c in cores:
        out[c // 2, (c % 2) * T:(c % 2 + 1) * T] = from_feat_major(res[c]["outT"])
    return out
```

```python
import contextlib
import numpy as np
import ml_dtypes
import concourse.bass as bass
import concourse.mybir as mybir
from concourse.bass_utils import run_bass_kernel_spmd

F32 = mybir.dt.float32
BF16 = mybir.dt.bfloat16
AF = mybir.ActivationFunctionType
ALU = mybir.AluOpType

D = 2048
DC = 16
SEQ = 4096
BATCH = 4
T = 2048
TT = 512
NTT = T // TT
EPS = 1e-6
DFF = 8192
CW = 256
DI = 4096
NH = 64
NG = 8
NS = 128
HP = 64
FH = 16
FD = 128

ENGS = ("pe", "act", "dve", "pool", "sp")
NO_SELF_SYNC = ("pe",)


class Buf:
    __slots__ = ("name", "writer", "readers", "dma_readers")

    def __init__(self, name=""):
        self.name = name
        self.writer = None
        self.readers = {}
        self.dma_readers = []


class Op:
    __slots__ = ("eng", "fn", "deps", "is_dma", "needs_inc", "tok_sem", "tok_val", "dma_slot", "prev_same_sem", "is_cc")

    def __init__(self, eng, fn, is_dma):
        self.eng = eng
        self.fn = fn
        self.deps = []
        self.is_dma = is_dma
        self.needs_inc = False
        self.tok_sem = None
        self.tok_val = None
        self.dma_slot = None
        self.prev_same_sem = None
        self.is_cc = False


class Sched:
    def __init__(self, nc, n_dma_sems=10):
        self.nc = nc
        self.ops = {e: [] for e in ENGS}
        self.n_dma_sems = n_dma_sems
        self.dma_count = {e: 0 for e in ENGS}
        self.dma_last = {}
        self.cc_ops = []

    def _add(self, eng, fn, reads, writes, is_dma):
        op = Op(eng, fn, is_dma)
        deps = []
        for b in reads:
            if b.writer is not None:
                deps.append(b.writer)
        for b in writes:
            if b.writer is not None:
                deps.append(b.writer)
            deps.extend(b.readers.values())
            deps.extend(b.dma_readers)
        for b in reads:
            if is_dma:
                b.dma_readers.append(op)
            else:
                b.readers[eng] = op
        for b in writes:
            b.writer = op
            b.readers = {}
            b.dma_readers = []
        if is_dma:
            slot = self.dma_count[eng] % self.n_dma_sems
            self.dma_count[eng] += 1
            op.dma_slot = slot
            op.prev_same_sem = self.dma_last.get((eng, slot))
            self.dma_last[(eng, slot)] = op
        seen = set()
        for d in deps:
            if d is op or id(d) in seen:
                continue
            seen.add(id(d))
            op.deps.append(d)
        self.ops[eng].append(op)
        return op

    def op(self, eng, fn, reads=(), writes=()):
        return self._add(eng, fn, reads, writes, False)

    def dma(self, eng, fn, reads=(), writes=()):
        return self._add(eng, fn, reads, writes, True)

    def cc(self, fn):
        op = Op("pool", fn, True)
        op.is_cc = True
        self.cc_ops.append(op)
        self.ops["pool"].append(op)
        return op

    def barrier(self):
        lasts = []
        for e in ENGS:
            for op in reversed(self.ops[e]):
                if not op.is_dma and op.fn is not None:
                    lasts.append(op)
                    break
        dmas = list(self.dma_last.values()) + list(self.cc_ops)
        for e in ENGS:
            op = Op(e, None, False)
            op.deps = list(lasts) + dmas
            self.ops[e].append(op)

    @staticmethod
    def _skip(op, d):
        return (not d.is_dma) and (not op.is_dma) and op.fn is not None and d.eng == op.eng and op.eng in NO_SELF_SYNC

    def emit(self, final_wait_ops=()):
        nc = self.nc
        for e in ENGS:
            for op in self.ops[e]:
                for d in op.deps:
                    if d.is_dma or self._skip(op, d):
                        continue
                    d.needs_inc = True
        with contextlib.ExitStack() as st:
            csem = {e: st.enter_context(nc.semaphore("c_" + e)) for e in ENGS}
            dsem = {}
            for e in ENGS:
                for s in range(min(self.n_dma_sems, self.dma_count[e])):
                    dsem[(e, s)] = st.enter_context(nc.semaphore("d_%s_%d" % (e, s)))
            for op in self.cc_ops:
                op.tok_sem = st.enter_context(nc.semaphore("cc%d" % self.cc_ops.index(op)))
                op.tok_val = 1
            for e in ENGS:
                cnt = 0
                dcnt = {}
                for op in self.ops[e]:
                    if op.is_cc:
                        continue
                    if op.is_dma:
                        k = (e, op.dma_slot)
                        dcnt[k] = dcnt.get(k, 0) + 16
                        op.tok_sem = dsem[k]
                        op.tok_val = dcnt[k]
                    elif op.needs_inc:
                        cnt += 1
                        op.tok_sem = csem[e]
                        op.tok_val = cnt
            block = st.enter_context(nc.Block())

            def make(e):
                def body(eng):
                    waited = {}
                    for op in self.ops[e]:
                        best = {}
                        cand = []
                        if op.is_dma and op.prev_same_sem is not None:
                            p = op.prev_same_sem
                            cand.append((p.tok_sem, p.tok_val))
                        for d in op.deps:
                            if self._skip(op, d) or d.tok_sem is None:
                                continue
                            cand.append((d.tok_sem, d.tok_val))
                        for s, v in cand:
                            k = id(s)
                            if k not in best or best[k][1] < v:
                                best[k] = (s, v)
                        for k, (s, v) in best.items():
                            if waited.get(k, 0) >= v:
                                continue
                            waited[k] = v
                            eng.wait_ge(s, v)
                        if op.fn is None:
                            continue
                        inst = op.fn(eng)
                        if op.is_cc:
                            inst.then_inc(op.tok_sem)
                        elif op.is_dma:
                            inst.then_inc(op.tok_sem, 16)
                        elif op.needs_inc:
                            inst.then_inc(op.tok_sem, 1)
                    if e == "sp":
                        for op in final_wait_ops:
                            eng.wait_ge(op.tok_sem, op.tok_val)
                return body

            block.tensor(make("pe"))
            block.scalar(make("act"))
            block.vector(make("dve"))
            block.gpsimd(make("pool"))
            block.sync(make("sp"))


def tile_weight(w, cw=CW, rows_per_block=2048):
    K, N = w.shape
    rb = K // rows_per_block
    kc = rows_per_block // 128
    assert N % cw == 0
    x = w.reshape(rb, kc, 128, N // cw, cw)
    return np.ascontiguousarray(x.transpose(0, 3, 2, 1, 4))


def col_layout(v):
    return np.ascontiguousarray(v.reshape(-1, 128).T)


def feat_major(x2d):
    t, d = x2d.shape
    return np.ascontiguousarray(x2d.reshape(t, d // 128, 128).transpose(2, 1, 0))


def from_feat_major(y):
    p, c, t = y.shape
    return np.ascontiguousarray(y.transpose(2, 1, 0).reshape(t, c * p))


class Prog:
    def __init__(self):
        self.nc = bass.Bass("TRN2", target_bir_lowering=False)
        self.st = contextlib.ExitStack()
        self.S = Sched(self.nc)
        self.dram = {}
        self.nwb = 0
        self.bank_rr = 0
        self.pst = contextlib.ExitStack()
        self.phase_alloc = False

    def end_phase(self):
        self.S.barrier()
        self.pst.close()
        self.pst = contextlib.ExitStack()

    def din(self, name, shape, dt=F32):
        t = self.nc.dram_tensor(name, list(shape), dt, kind="ExternalInput").ap()
        self.dram[name] = t
        return t

    def dout(self, name, shape, dt=F32):
        t = self.nc.dram_tensor(name, list(shape), dt, kind="ExternalOutput").ap()
        self.dram[name] = t
        return t

    def dscratch(self, name, shape, dt=F32):
        t = self.nc.dram_tensor(name, list(shape), dt, kind="Internal").ap()
        self.dram[name] = t
        return t

    def sb(self, name, shape, dt=F32):
        st = self.pst if self.phase_alloc else self.st
        return st.enter_context(self.nc.sbuf_tensor(name, list(shape), dt))

    def ps(self, name, shape, dt=F32):
        st = self.pst if self.phase_alloc else self.st
        return st.enter_context(self.nc.psum_tensor(name, list(shape), dt))

    def setup_common(self):
        S = self.S
        self.consts_d = self.din("consts", [128, 4, 128])
        self.cst = self.sb("cst", [128, 4, 128])
        self.cst_bf = self.sb("cst_bf", [128, 4, 128], BF16)
        self.b_cst = Buf("cst")
        S.dma("sp", lambda e: e.dma_start(out=self.cst[:], in_=self.consts_d), writes=[self.b_cst])
        S.op("dve", lambda e: e.tensor_copy(out=self.cst_bf[:], in_=self.cst[:]), reads=[self.b_cst], writes=[self.b_cst])
        self.Lm = self.cst[:, 0, :]
        self.Um = self.cst[:, 1, :]
        self.ones = self.cst[:, 2, :]
        self.ident = self.cst[:, 3, :]
        self.ones_bf = self.cst_bf[:, 2, :]
        self.ident_bf = self.cst_bf[:, 3, :]
        self.NWB = 4
        self.wb = [self.sb("wb%d" % i, [128, 16, CW], BF16) for i in range(self.NWB)]
        self.b_wb = [Buf("wb%d" % i) for i in range(self.NWB)]
        self.mm_ps = [self.ps("mmps%d" % i, [128, 512]) for i in range(2)]
        self.b_mm = [Buf("mmps%d" % i) for i in range(2)]
        self.hT = self.sb("hT", [128, DC, TT])
        self.b_hT = Buf("hT")
        self.uT = self.sb("uT", [128, DC, TT], BF16)
        self.b_uT = Buf("uT")
        self.rstd = self.sb("rstd", [128, TT])
        self.b_rstd = Buf("rstd")
        self.ntmp = [self.sb("ntmp%d" % i, [128, TT]) for i in range(2)]
        self.b_ntmp = [Buf("ntmp%d" % i) for i in range(2)]
        self.yT = self.sb("big", [128, 32, TT], BF16)
        self.b_yT = Buf("yT")
        self.hid = self.yT[:, 0:16, :]
        self.b_hid = self.b_yT
        self.rl = [self.sb("rl%d" % i, [128, TT], BF16) for i in range(2)]
        self.b_rl = [Buf("rl%d" % i) for i in range(2)]

    def next_bank(self):
        i = self.bank_rr % 2
        self.bank_rr += 1
        return self.mm_ps[i], self.b_mm[i]

    def wtile(self, dram_tiles, idx):
        i = self.nwb % self.NWB
        self.nwb += 1
        wb, b = self.wb[i], self.b_wb[i]
        src = dram_tiles[idx] if isinstance(idx, int) else dram_tiles[idx[0], idx[1]]
        self.S.dma("pool", lambda e: e.dma_start(out=wb[:], in_=src, max_dma_last_dim=8192), writes=[b])
        return wb, b

    def adaln(self, w_ada_t, b_ada_col, c_col, norm_mix_col, norm_mlp_col, mod_out=None, mod_in=None, half_dram=None):
        S = self.S
        self.mod = self.sb("mod", [128, 2, 96])
        self.b_mod = Buf("mod")
        self.gs = self.sb("gs", [128, 2, 2, DC])
        self.b_gs = Buf("gs")
        nm = self.sb("nm", [128, 2, 2, DC])
        b_nm = Buf("nm")
        S.dma("sp", lambda e: e.dma_start(out=nm[:, :, 0, :], in_=norm_mix_col), writes=[b_nm])
        S.dma("sp", lambda e: e.dma_start(out=nm[:, :, 1, :], in_=norm_mlp_col), writes=[b_nm])
        if mod_in is not None:
            S.dma("sp", lambda e: e.dma_start(out=self.mod[:], in_=mod_in), writes=[self.b_mod])
        else:
            cc = self.sb("cc", [128, DC])
            cbf = self.sb("cbf", [128, DC], BF16)
            bada = self.sb("bada", [128, 2, 96])
            b_cc, b_bada = Buf("cc"), Buf("bada")
            S.dma("sp", lambda e: e.dma_start(out=cc[:], in_=c_col), writes=[b_cc])
            nb = 48 if half_dram is not None else 96
            S.dma("sp", lambda e: e.dma_start(out=bada[:, :, 0:nb], in_=b_ada_col), writes=[b_bada])
            S.op("act", lambda e: e.activation(out=cbf[:], in_=cc[:], func=AF.Silu), reads=[b_cc], writes=[b_cc])
            if half_dram is None:
                for i in range(2):
                    ps, bps = self.next_bank()
                    for ct in range(12288 // CW):
                        wt, bw = self.wtile(w_ada_t, (i, ct))
                        for jj in range(CW // 128):
                            col = ct * (CW // 128) + jj
                            for k in range(16):
                                S.op("pe", lambda e, ps=ps, wt=wt, jj=jj, k=k, col=col: e.matmul(
                                    ps[:, col:col + 1], lhsT=wt[:, k, jj * 128:(jj + 1) * 128], rhs=cbf[:, k:k + 1],
                                    start=(k == 0), stop=(k == 15)), reads=[bw, b_cc], writes=[bps])
                    S.op("dve", lambda e, ps=ps, i=i: e.tensor_tensor(out=self.mod[:, i, :], in0=ps[:, 0:96], in1=bada[:, i, :], op=ALU.add),
                         reads=[bps, b_bada], writes=[self.b_mod])
                if mod_out is not None:
                    self.mod_store = S.dma("act", lambda e: e.dma_start(out=mod_out, in_=self.mod[:]), reads=[self.b_mod], writes=[Buf()])
            else:
                m_own, m_g = half_dram
                mh = self.sb("modh", [128, 2, 48])
                b_mh = Buf("modh")
                for i in range(2):
                    ps, bps = self.next_bank()
                    for ct in range(24):
                        wt, bw = self.wtile(w_ada_t, (i, ct))
                        for jj in range(CW // 128):
                            col = ct * (CW // 128) + jj
                            for k in range(16):
                                S.op("pe", lambda e, ps=ps, wt=wt, jj=jj, k=k, col=col: e.matmul(
                                    ps[:, col:col + 1], lhsT=wt[:, k, jj * 128:(jj + 1) * 128], rhs=cbf[:, k:k + 1],
                                    start=(k == 0), stop=(k == 15)), reads=[bw, b_cc], writes=[bps])
                    S.op("dve", lambda e, ps=ps, i=i: e.tensor_tensor(out=mh[:, i, :], in0=ps[:, 0:48], in1=bada[:, i, 0:48], op=ALU.add),
                         reads=[bps, b_bada], writes=[b_mh])
                S.dma("act", lambda e: e.dma_start(out=m_own.rearrange("p (i c) -> p i c", i=2), in_=mh[:]), reads=[b_mh], writes=[Buf()])
                S.barrier()
                S.cc(lambda e: e.collective_compute("AllGather", ALU.bypass, replica_groups=PAIRS, ins=[m_own], outs=[m_g]))
                S.barrier()
                for r in range(2):
                    S.dma("sp", lambda e, r=r: e.dma_start(out=self.mod[:, :, 48 * r:48 * r + 48],
                                                           in_=m_g[128 * r:128 * r + 128, :].rearrange("p (i c) -> p i c", i=2)), writes=[self.b_mod])
        for i in range(2):
            for j, off in ((0, 16), (1, 64)):
                S.op("dve", lambda e, i=i, j=j, off=off: e.scalar_tensor_tensor(
                    out=self.gs[:, i, j, :], in0=self.mod[:, i, off:off + 16], scalar=1.0, in1=nm[:, i, j, :],
                    op0=ALU.add, op1=ALU.mult), reads=[self.b_mod, b_nm], writes=[self.b_gs])

    def modv(self, layer, which, c):
        off = {"sh_a": 0, "g_a": 32, "sh_f": 48, "g_f": 80}[which]
        return self.mod[:, layer, off + c:off + c + 1]

    def norm_mod(self, layer, j, ntok=TT, gs_ap=None, out_f32=None):
        S = self.S
        n = ntok
        if not hasattr(self, "b_sq"):
            self.b_sq = [Buf("sq%d" % q) for q in range(4)]
        for q in range(4):
            S.op("act", lambda e, q=q: e.activation(out=self.uT[:, 4 * q:4 * q + 4, 0:n], in_=self.hT[:, 4 * q:4 * q + 4, 0:n], func=AF.Square),
                 reads=[self.b_hT], writes=[self.b_uT, self.b_sq[q]])
        ps, bps = self.next_bank()
        for c in range(DC):
            S.op("pe", lambda e, c=c, ps=ps: e.matmul(ps[:, 0:n], lhsT=self.ones_bf, rhs=self.uT[:, c, 0:n],
                                                       start=(c == 0), stop=(c == DC - 1)),
                 reads=[self.b_sq[c // 4], self.b_cst], writes=[bps])
        S.op("act", lambda e, ps=ps: e.activation(out=self.rstd[:, 0:n], in_=ps[:, 0:n], func=AF.Sqrt, bias=EPS, scale=1.0 / D),
             reads=[bps], writes=[self.b_rstd])
        S.op("dve", lambda e: e.reciprocal(out=self.rstd[:, 0:n], in_=self.rstd[:, 0:n]), reads=[self.b_rstd], writes=[self.b_rstd])
        for c in range(DC):
            if out_f32 is not None:
                S.op("dve", lambda e, c=c: e.scalar_tensor_tensor(
                    out=out_f32[:, c, 0:n], in0=self.hT[:, c, 0:n], scalar=gs_ap[:, c:c + 1], in1=self.rstd[:, 0:n],
                    op0=ALU.mult, op1=ALU.mult), reads=[self.b_hT, self.b_rstd, self.b_gs], writes=[self.b_outf])
                continue
            tmp, bt = self.ntmp[c % 2], self.b_ntmp[c % 2]
            S.op("dve", lambda e, c=c, tmp=tmp: e.scalar_tensor_tensor(
                out=tmp[:, 0:n], in0=self.hT[:, c, 0:n], scalar=self.gs[:, layer, j, c:c + 1], in1=self.rstd[:, 0:n],
                op0=ALU.mult, op1=ALU.mult), reads=[self.b_hT, self.b_rstd, self.b_gs], writes=[bt])
            sh = self.modv(layer, "sh_a" if j == 0 else "sh_f", c)
            S.op("act", lambda e, c=c, tmp=tmp, sh=sh: e.activation(out=self.uT[:, c, 0:n], in_=tmp[:, 0:n], func=AF.Identity, bias=sh, scale=1.0),
                 reads=[bt, self.b_mod], writes=[self.b_uT, self.b_sq[c // 4]])

    def mlp(self, layer, w_up_t, w_down_t, nfb=DFF // 2048, do_down=True, wl=None):
        S = self.S
        self.norm_mod(layer, 1)
        npc = CW // 128
        wl = layer if wl is None else wl
        for fb in range(nfb):
            for ct in range(2048 // CW):
                wt, bw = self.wtile(w_up_t, (wl, fb * (2048 // CW) + ct))
                for jj in range(npc):
                    jf = ct * npc + jj
                    ps, bps = self.next_bank()
                    for k in range(DC):
                        S.op("pe", lambda e, ps=ps, wt=wt, jj=jj, k=k: e.matmul(
                            ps[:], lhsT=wt[:, k, jj * 128:(jj + 1) * 128], rhs=self.uT[:, k, :], start=(k == 0), stop=(k == DC - 1)),
                            reads=[bw, self.b_uT], writes=[bps])
                    rl, brl = self.rl[jf % 2], self.b_rl[jf % 2]
                    S.op("act", lambda e, ps=ps, rl=rl: e.activation(out=rl[:], in_=ps[:], func=AF.Relu), reads=[bps], writes=[brl])
                    S.op("dve", lambda e, rl=rl, jf=jf: e.tensor_tensor(out=self.hid[:, jf, :], in0=rl[:], in1=rl[:], op=ALU.mult),
                         reads=[brl], writes=[self.b_hid])
            for ct in range(D // CW if do_down else 0):
                wt, bw = self.wtile(w_down_t, (wl * (DFF // 2048) + fb, ct))
                for jj in range(npc):
                    d = ct * npc + jj
                    ps, bps = self.next_bank()
                    for k in range(16):
                        S.op("pe", lambda e, ps=ps, wt=wt, jj=jj, k=k: e.matmul(
                            ps[:], lhsT=wt[:, k, jj * 128:(jj + 1) * 128], rhs=self.hid[:, k, :], start=(k == 0), stop=(k == 15)),
                            reads=[bw, self.b_hid], writes=[bps])
                    g = self.modv(layer, "g_f", d)
                    S.op("dve", lambda e, ps=ps, d=d, g=g: e.scalar_tensor_tensor(
                        out=self.hT[:, d, :], in0=ps[:], scalar=g, in1=self.hT[:, d, :], op0=ALU.mult, op1=ALU.add),
                        reads=[bps, self.b_mod, self.b_hT], writes=[self.b_hT])

    def finish(self, final_ops):
        self.S.emit(final_wait_ops=final_ops)
        self.pst.close()
        self.st.close()
        return self.nc


def make_consts():
    k = np.arange(128)
    L = (k[:, None] <= k[None, :]).astype(np.float32)
    U = (k[:, None] > k[None, :]).astype(np.float32)
    ones = np.ones((128, 128), np.float32)
    ident = np.eye(128, dtype=np.float32)
    return np.ascontiguousarray(np.stack([L, U, ones, ident], axis=1))


def ssd_setup(P, d):
    S = P.S
    P.seg_ps = P.ps("segps", [128, 512]); P.b_seg = Buf("seg")
    P.tp_ps = P.ps("tpps", [128, 1024], BF16); P.b_tp = Buf("tp")
    P.yd_ps = P.ps("ydps", [128, 512]); P.b_yd = Buf("yd")
    P.yo_ps = P.ps("yops", [128, 512]); P.b_yo = Buf("yo")
    P.st_ps = P.ps("stps", [128, 512]); P.b_st = Buf("st")
    P.misc_ps = P.ps("miscps", [128, 512])
    P.b_sc = Buf("misc"); P.b_sm = [P.b_sc] * 4
    P.zs = P.sb("zs", [128, 4, 512], BF16); P.b_zs = Buf("zs")
    P.xtm = P.sb("xtm", [128, 4, 512], BF16); P.b_xtm = Buf("xtm")
    P.BT = P.sb("BT", [128, 512], BF16); P.b_BT = Buf("BT")
    P.CT = P.sb("CT", [128, 512], BF16); P.b_CT = Buf("CT")
    P.Btm = P.sb("Btm", [128, 4, 128], BF16); P.b_Btm = Buf("Btm")
    P.raw = [P.sb("raw%d" % i, [128, 515]) for i in range(2)]; P.b_raw = [Buf("raw%d" % i) for i in range(2)]
    P.acc = [P.sb("acc%d" % i, [128, 512]) for i in range(2)]; P.b_acc = [Buf("acc%d" % i) for i in range(2)]
    P.xsT = [P.sb("xsT%d" % i, [128, 512], BF16) for i in range(2)]; P.b_xsT = [Buf("xsT%d" % i) for i in range(2)]
    P.tnh = P.rl; P.b_tnh = P.b_rl
    P.nconv = 0
    P.pending_tp = None
    P.halo = P.sb("halo", [128, 48, 3]); P.b_halo = [Buf("halo%d" % i) for i in range(48)]
    P.cw = P.sb("cwc", [128, 48, 4]); P.cb = P.sb("cbc", [128, 48]); P.b_cw = Buf("cw")
    P.dtb = P.sb("dtb", [128, 64]); P.Abc = P.sb("Abc", [128, 64]); P.Dbc = P.sb("Dbc", [128, 64]); P.b_prm = Buf("prm")
    P.wdt = P.sb("wdt", [128, 16, 64], BF16); P.b_wdt = Buf("wdt")
    P.dt = P.sb("dt", [128, 4, 64]); P.dtA = P.sb("dtA", [128, 4, 64]); P.ea = P.sb("ea", [128, 4, 64])
    P.dte = P.sb("dte", [128, 4, 64]); P.cd = P.sb("cd", [128, 4, 64])
    P.b_dt = Buf("dt"); P.b_dtA = Buf("dtA"); P.b_ea = Buf("ea"); P.b_dte = Buf("dte"); P.b_cd = Buf("cd")
    P.sp1 = P.sb("sp1", [128, 64]); P.sp2 = P.sb("sp2", [128, 64]); P.sp3 = P.sb("sp3", [128, 64]); P.b_sp = Buf("sp")
    P.nh = 64
    P.R = [P.sb("R%d" % i, [128, 4, 128]) for i in range(2)]; P.b_R = [Buf("R%d" % i) for i in range(2)]
    P.dec = P.sb("dec", [128, 4, 8, 128], BF16); P.b_dec = [Buf("dec%d" % i) for i in range(4)]
    P.sm = P.sb("smk", [128, 128], BF16); P.b_smk = Buf("smk")
    P.Mh = [P.sb("Mh%d" % i, [128, 8, 128], BF16) for i in range(2)]; P.b_Mh = [Buf("Mh%d" % i) for i in range(2)]
    P.xdt = [P.sb("xdt%d" % i, [128, 512], BF16) for i in range(2)]; P.b_xdt = [Buf("xdt%d" % i) for i in range(2)]
    P.xdte = [P.sb("xdte%d" % i, [128, 512], BF16) for i in range(2)]; P.b_xdte = [Buf("xdte%d" % i) for i in range(2)]
    P.y1 = P.sb("y1", [128, 512]); P.y2 = P.sb("y2", [128, 512]); P.b_y1 = Buf("y1"); P.b_y2 = Buf("y2")
    P.yn = P.sb("yn", [128, 512], BF16); P.b_yn = Buf("yn")
    P.xD = [P.sb("xD%d" % i, [128, 512], BF16) for i in range(2)]; P.b_xD = [Buf("xD%d" % i) for i in range(2)]
    P.ss = P.sb("ss", [128, 2]); P.b_ss = Buf("ss")
    P.Sst = P.sb("Sst", [128, 8, 512]); P.b_S = [Buf("S%d" % g) for g in range(8)]
    P.Sbf = P.sb("Sbf", [128, 512], BF16); P.b_Sbf = Buf("Sbf")
    P.gn = P.sb("gn", [128, 32]); P.b_gn = Buf("gn")
    P.flag = P.sb("flag_sb", [128, 1]); P.b_flag = Buf("flag")
    S.dma("sp", lambda e: e.dma_start(out=P.cw[:], in_=d["convw_col"]), writes=[P.b_cw])
    S.dma("sp", lambda e: e.dma_start(out=P.cb[:], in_=d["convb_col"]), writes=[P.b_cw])
    S.dma("sp", lambda e: e.dma_start(out=P.dtb[:], in_=d["dtb_bc"]), writes=[P.b_prm])
    S.dma("sp", lambda e: e.dma_start(out=P.Abc[:], in_=d["alog_bc"]), writes=[P.b_prm])
    S.dma("sp", lambda e: e.dma_start(out=P.Dbc[:], in_=d["D_bc"]), writes=[P.b_prm])
    S.dma("sp", lambda e: e.dma_start(out=P.flag[:], in_=d["flag"]), writes=[P.b_flag])
    S.dma("pool", lambda e: e.dma_start(out=P.wdt[:], in_=d["w_dt"]), writes=[P.b_wdt])
    S.dma("sp", lambda e: e.dma_start(out=P.gn[:], in_=d["gn_col"]), writes=[P.b_gn])
    S.op("dve", lambda e: e.tensor_scalar(out=P.cw[:], in0=P.cw[:], scalar1=0.5, scalar2=None, op0=ALU.mult), reads=[P.b_cw], writes=[P.b_cw])
    S.op("dve", lambda e: e.tensor_scalar(out=P.cb[:], in0=P.cb[:], scalar1=0.5, scalar2=None, op0=ALU.mult), reads=[P.b_cw], writes=[P.b_cw])
    S.op("dve", lambda e: e.tensor_scalar(out=P.gn[:], in0=P.gn[:], scalar1=0.5, scalar2=None, op0=ALU.mult), reads=[P.b_gn], writes=[P.b_gn])
    S.op("act", lambda e: e.activation(out=P.Abc[:], in_=P.Abc[:], func=AF.Exp), reads=[P.b_prm], writes=[P.b_prm])
    S.op("dve", lambda e: e.tensor_scalar(out=P.Abc[:], in0=P.Abc[:], scalar1=-1.0, scalar2=None, op0=ALU.mult),
         reads=[P.b_prm], writes=[P.b_prm])


def ssd_init_state(P, S_in, halo_in):
    S = P.S
    bs = P.b_S
    S.dma("sp", lambda e: e.dma_start(out=P.Sst[:].rearrange("p g c -> p (g c)"), in_=S_in), writes=bs)
    S.dma("sp", lambda e: e.dma_start(out=P.halo[:], in_=halo_in), writes=P.b_halo)
    S.op("dve", lambda e: e.tensor_scalar(out=P.Sst[:], in0=P.Sst[:], scalar1=P.flag[:, 0:1], scalar2=None, op0=ALU.mult),
         reads=bs + [P.b_flag], writes=bs)
    S.op("dve", lambda e: e.tensor_scalar(out=P.halo[:], in0=P.halo[:], scalar1=P.flag[:, 0:1], scalar2=None, op0=ALU.mult),
         reads=P.b_halo + [P.b_flag], writes=P.b_halo)


def ssd_dt(P):
    S = P.S
    mp = P.misc_ps
    for sub in range(4):
        nh = P.nh
        ps = mp[:, 320:320 + nh]; bps = P.b_sm[3]
        for k in range(DC):
            S.op("pe", lambda e, ps=ps, k=k, sub=sub: e.matmul(ps, lhsT=P.uT[:, k, sub * 128:(sub + 1) * 128], rhs=P.wdt[:, k, :],
                                                           start=(k == 0), stop=(k == DC - 1)),
                 reads=[P.b_uT, P.b_wdt], writes=[bps])
        S.op("dve", lambda e, ps=ps: e.tensor_tensor(out=P.sp1[:], in0=ps, in1=P.dtb[:], op=ALU.add), reads=[bps, P.b_prm], writes=[P.b_sp])
        S.op("dve", lambda e: e.scalar_tensor_tensor(out=P.sp2[:], in0=P.sp1[:], scalar=-1.0, in1=P.sp1[:], op0=ALU.mult, op1=ALU.min),
             reads=[P.b_sp], writes=[P.b_sp])
        S.op("act", lambda e: e.activation(out=P.sp2[:], in_=P.sp2[:], func=AF.Exp), reads=[P.b_sp], writes=[P.b_sp])
        S.op("act", lambda e: e.activation(out=P.sp3[:], in_=P.sp2[:], func=AF.Ln, bias=1.0, scale=1.0), reads=[P.b_sp], writes=[P.b_sp])
        S.op("dve", lambda e, sub=sub: e.scalar_tensor_tensor(out=P.dt[:, sub, :], in0=P.sp1[:], scalar=0.0, in1=P.sp3[:], op0=ALU.max, op1=ALU.add),
             reads=[P.b_sp], writes=[P.b_dt])
        S.op("dve", lambda e, sub=sub: e.tensor_tensor(out=P.dtA[:, sub, :], in0=P.dt[:, sub, :], in1=P.Abc[:], op=ALU.mult),
             reads=[P.b_dt, P.b_prm], writes=[P.b_dtA])
        for i, (lhs, dst, bd) in enumerate(((P.Lm, P.ea, P.b_ea), (P.Um, P.dte, P.b_dte), (P.ones, P.cd, P.b_cd))):
            pss = mp[:, 128 + 64 * i:128 + 64 * i + nh]; bp = P.b_sm[i]
            S.op("pe", lambda e, pss=pss, lhs=lhs, sub=sub: e.matmul(pss, lhsT=lhs, rhs=P.dtA[:, sub, :], start=True, stop=True),
                 reads=[P.b_dtA, P.b_cst], writes=[bp])
            S.op("act", lambda e, pss=pss, dst=dst, sub=sub: e.activation(out=dst[:, sub, :], in_=pss, func=AF.Exp), reads=[bp], writes=[bd])


def ssd_conv_front(P, ps, bps, ci):
    S = P.S
    bh = P.b_halo[ci]
    i = P.nconv % 2
    P.nconv += 1
    raw, braw, acc, bacc = P.raw[i], P.b_raw[i], P.acc[i], P.b_acc[i]
    S.op("act", lambda e: e.activation(out=raw[:, 3:515], in_=ps[:], func=AF.Copy), reads=[bps], writes=[braw])
    S.op("act", lambda e: e.activation(out=acc[:], in_=ps[:], func=AF.Identity, bias=P.cb[:, ci:ci + 1], scale=P.cw[:, ci, 3:4]),
         reads=[bps, P.b_cw], writes=[bacc])
    S.op("dve", lambda e: e.tensor_copy(out=raw[:, 0:3], in_=P.halo[:, ci, :]), reads=[bh], writes=[braw])
    return i


def ssd_conv_back(P, i, ci, out_ap, b_out):
    S = P.S
    bh = P.b_halo[ci]
    raw, braw, acc, bacc, tnh, btnh = P.raw[i], P.b_raw[i], P.acc[i], P.b_acc[i], P.tnh[i], P.b_tnh[i]
    for j in (2, 1, 0):
        S.op("dve", lambda e, j=j: e.scalar_tensor_tensor(out=acc[:], in0=raw[:, j:j + 512], scalar=P.cw[:, ci, j:j + 1], in1=acc[:],
                                                          op0=ALU.mult, op1=ALU.add), reads=[braw, bacc, P.b_cw], writes=[bacc])
    S.op("dve", lambda e: e.tensor_copy(out=P.halo[:, ci, :], in_=raw[:, 512:515]), reads=[braw], writes=[bh])
    S.op("act", lambda e: e.activation(out=tnh[:], in_=acc[:], func=AF.Tanh), reads=[bacc], writes=[btnh])
    S.op("dve", lambda e: e.scalar_tensor_tensor(out=out_ap, in0=tnh[:], scalar=1.0, in1=acc[:], op0=ALU.add, op1=ALU.mult),
         reads=[btnh, bacc], writes=[b_out])


def ssd_transpose4(P, src, b_src, dst_fn, b_dst):
    S = P.S
    for sub in range(4):
        S.op("pe", lambda e, sub=sub: e.transpose(out=P.tp_ps[:, sub * 128:(sub + 1) * 128], in_=src[:, sub * 128:(sub + 1) * 128],
                                                   identity=P.ident_bf), reads=[b_src, P.b_cst], writes=[P.b_tp])
    for sub in range(4):
        S.op("act", lambda e, sub=sub: e.activation(out=dst_fn(sub), in_=P.tp_ps[:, sub * 128:(sub + 1) * 128], func=AF.Copy),
             reads=[P.b_tp], writes=[b_dst])


def ssd_dec_steps(P, g):
    S = P.S
    steps = []
    for c in range(4):
        for hh in range(2):
            def step(c=c, hh=hh):
                i = (c * 2 + hh) % 2
                R, bR = P.R[i], P.b_R[i]
                h4 = slice(g * 8 + hh * 4, g * 8 + hh * 4 + 4)
                S.op("dve", lambda e: e.tensor_tensor(out=R[:], in0=P.dtA[:, c, h4].unsqueeze(2).to_broadcast([128, 4, 128]),
                                                      in1=P.Lm.unsqueeze(1).to_broadcast([128, 4, 128]), op=ALU.mult),
                     reads=[P.b_dtA, P.b_cst], writes=[bR])
                S.op("pe", lambda e: e.matmul(P.seg_ps[:], lhsT=P.Um, rhs=R[:].rearrange("p r t -> p (r t)"), start=True, stop=True),
                     reads=[bR, P.b_cst], writes=[P.b_seg])
                S.op("act", lambda e: e.activation(out=P.dec[:, c, hh * 4:(hh + 1) * 4, :], in_=P.seg_ps[:].rearrange("p (r t) -> p r t", r=4), func=AF.Exp),
                     reads=[P.b_seg], writes=[P.b_dec[c]])
            steps.append(step)
    return steps


def ssd_flush_tp(P):
    if P.pending_tp is not None:
        f = P.pending_tp
        P.pending_tp = None
        f()


def ssd_group(P, g, w_in_g, pre):
    S = P.S
    mp = P.misc_ps
    steps = [] if pre else ssd_dec_steps(P, g)

    def step():
        if steps:
            steps.pop(0)()
    if not pre:
        wts = [P.wtile(w_in_g, (g, ct)) for ct in range(2)]
        for sub in range(4):
            ps, bps = P.next_bank()
            for ct in range(2):
                wt, bw = wts[ct]
                for k in range(DC):
                    S.op("pe", lambda e, ps=ps, wt=wt, ct=ct, k=k, sub=sub: e.matmul(
                        ps[:, ct * 256:(ct + 1) * 256], lhsT=P.uT[:, k, sub * 128:(sub + 1) * 128], rhs=wt[:, k, :],
                        start=(k == 0), stop=(k == DC - 1)), reads=[bw, P.b_uT], writes=[bps])
            tnh, btnh = P.tnh[sub % 2], P.b_tnh[sub % 2]
            S.op("act", lambda e, ps=ps, tnh=tnh: e.activation(out=tnh[:], in_=ps[:], func=AF.Tanh, scale=0.5), reads=[bps], writes=[btnh])
            S.op("dve", lambda e, ps=ps, sub=sub, tnh=tnh: e.scalar_tensor_tensor(out=P.zs[:, sub, :], in0=tnh[:], scalar=1.0, in1=ps[:], op0=ALU.add, op1=ALU.mult),
                 reads=[btnh, bps], writes=[P.b_zs])
            step()
    for ct in range(2, 5):
        wt, bw = P.wtile(w_in_g, (g, ct))
        for jj in range(2):
            ci = g * 6 + (ct - 2) * 2 + jj
            ps, bps = P.next_bank()
            for k in range(DC):
                S.op("pe", lambda e, ps=ps, wt=wt, jj=jj, k=k: e.matmul(
                    ps[:], lhsT=wt[:, k, jj * 128:(jj + 1) * 128], rhs=P.uT[:, k, :], start=(k == 0), stop=(k == DC - 1)),
                    reads=[bw, P.b_uT], writes=[bps])
            i = ssd_conv_front(P, ps, bps, ci)
            ssd_flush_tp(P)
            step()
            if ct < 4:
                xc = (ct - 2) * 2 + jj
                xs, bxs = P.xsT[i], P.b_xsT[i]
                ssd_conv_back(P, i, ci, xs[:], bxs)
                P.pending_tp = (lambda xs=xs, bxs=bxs, xc=xc: ssd_transpose4(
                    P, xs, bxs, lambda sub: P.xtm[:, sub, xc * 128:(xc + 1) * 128], P.b_xtm))
            elif jj == 0:
                ssd_conv_back(P, i, ci, P.BT[:], P.b_BT)
                P.pending_tp = (lambda: ssd_transpose4(P, P.BT, P.b_BT, lambda sub: P.Btm[:, sub, :], P.b_Btm))
            else:
                ssd_conv_back(P, i, ci, P.CT[:], P.b_CT)
    ssd_flush_tp(P)
    while steps:
        step()
    bS = P.b_S[g]
    Sg = P.Sst[:, g, :]
    hs = slice(g * 8, (g + 1) * 8)

    def front(c):
        i = c % 2
        cs = slice(c * 128, (c + 1) * 128)
        xdt, bxdt, xdte, bxdte = P.xdt[i], P.b_xdt[i], P.xdte[i], P.b_xdte[i]
        S.op("dve", lambda e: e.tensor_tensor(out=xdt[:].rearrange("p (r q) -> p r q", r=8), in0=P.xtm[:, c, :].rearrange("p (r q) -> p r q", r=8),
                                              in1=P.dt[:, c, hs].unsqueeze(2).to_broadcast([128, 8, 64]), op=ALU.mult),
             reads=[P.b_xtm, P.b_dt], writes=[bxdt])
        S.op("dve", lambda e: e.tensor_tensor(out=xdte[:].rearrange("p (r q) -> p r q", r=8), in0=xdt[:].rearrange("p (r q) -> p r q", r=8),
                                              in1=P.dte[:, c, hs].unsqueeze(2).to_broadcast([128, 8, 64]), op=ALU.mult),
             reads=[bxdt, P.b_dte], writes=[bxdte])
        if not pre:
            sc = mp[:, 0:128]
            Mh, bMh = P.Mh[i], P.b_Mh[i]
            xD, bxD = P.xD[i], P.b_xD[i]
            S.op("dve", lambda e: e.tensor_tensor(out=xD[:].rearrange("p (r q) -> p r q", r=8), in0=P.xtm[:, c, :].rearrange("p (r q) -> p r q", r=8),
                                                  in1=P.Dbc[:, hs].unsqueeze(2).to_broadcast([128, 8, 64]), op=ALU.mult),
                 reads=[P.b_xtm, P.b_prm], writes=[bxD])
            S.op("pe", lambda e: e.matmul(sc, lhsT=P.BT[:, cs], rhs=P.CT[:, cs], start=True, stop=True),
                 reads=[P.b_BT, P.b_CT], writes=[P.b_sc])
            S.op("dve", lambda e: e.tensor_tensor(out=P.sm[:], in0=sc, in1=P.Lm, op=ALU.mult), reads=[P.b_sc, P.b_cst], writes=[P.b_smk])
            S.op("dve", lambda e: e.tensor_tensor(out=Mh[:], in0=P.dec[:, c, :, :],
                                                  in1=P.sm[:].unsqueeze(1).to_broadcast([128, 8, 128]), op=ALU.mult),
                 reads=[P.b_dec[c], P.b_smk], writes=[bMh])

    def back(c):
        i = c % 2
        cs = slice(c * 128, (c + 1) * 128)
        xdt, bxdt, xdte, bxdte = P.xdt[i], P.b_xdt[i], P.xdte[i], P.b_xdte[i]
        if not pre:
            Mh, bMh = P.Mh[i], P.b_Mh[i]
            S.op("act", lambda e: e.activation(out=P.Sbf[:], in_=Sg, func=AF.Copy), reads=[bS], writes=[P.b_Sbf])
            xD, bxD = P.xD[i], P.b_xD[i]
            S.op("pe", lambda e: e.matmul(P.yd_ps[:], lhsT=P.ident_bf, rhs=xD[:], start=True, stop=False),
                 reads=[bxD, P.b_cst], writes=[P.b_yd])
            for r in range(8):
                S.op("pe", lambda e, r=r: e.matmul(P.yd_ps[:, r * 64:(r + 1) * 64], lhsT=Mh[:, r, :], rhs=xdt[:, r * 64:(r + 1) * 64],
                                                   start=False, stop=(r == 7)), reads=[bMh, bxdt], writes=[P.b_yd])
            S.op("pe", lambda e: e.matmul(P.yo_ps[:], lhsT=P.CT[:, cs], rhs=P.Sbf[:], start=True, stop=True),
                 reads=[P.b_CT, P.b_Sbf], writes=[P.b_yo])
        S.op("pe", lambda e: e.matmul(P.st_ps[:], lhsT=P.Btm[:, c, :], rhs=xdte[:], start=True, stop=True),
             reads=[P.b_Btm, bxdte], writes=[P.b_st])
        if not pre:
            S.op("dve", lambda e: e.tensor_tensor(out=P.y1[:].rearrange("p (r q) -> p r q", r=8), in0=P.yo_ps[:].rearrange("p (r q) -> p r q", r=8),
                                                  in1=P.ea[:, c, hs].unsqueeze(2).to_broadcast([128, 8, 64]), op=ALU.mult),
                 reads=[P.b_yo, P.b_ea], writes=[P.b_y1])
            S.op("dve", lambda e: e.tensor_tensor(out=P.y1[:], in0=P.yd_ps[:], in1=P.y1[:], op=ALU.add), reads=[P.b_yd, P.b_y1], writes=[P.b_y1])
        S.op("dve", lambda e: e.tensor_tensor(out=Sg.rearrange("p (r q) -> p r q", r=8), in0=Sg.rearrange("p (r q) -> p r q", r=8),
                                              in1=P.cd[:, c, hs].unsqueeze(2).to_broadcast([128, 8, 64]), op=ALU.mult),
             reads=[bS, P.b_cd], writes=[bS])
        S.op("dve", lambda e: e.tensor_tensor(out=Sg, in0=P.st_ps[:], in1=Sg, op=ALU.add), reads=[P.b_st, bS], writes=[bS])
        if not pre:
            S.op("dve", lambda e: e.tensor_tensor(out=P.y1[:], in0=P.y1[:], in1=P.zs[:, c, :], op=ALU.mult), reads=[P.b_y1, P.b_zs], writes=[P.b_y1])
            S.op("act", lambda e: e.activation(out=P.y2[:], in_=P.y1[:], func=AF.Square, accum_out=P.ss[:, 0:1]), reads=[P.b_y1, P.b_y2], writes=[P.b_y2, P.b_ss])
            S.op("act", lambda e: e.activation(out=P.ss[:, 1:2], in_=P.ss[:, 0:1], func=AF.Sqrt, bias=EPS, scale=1.0 / 2048), reads=[P.b_ss], writes=[P.b_ss])
            S.op("dve", lambda e: e.reciprocal(out=P.ss[:, 1:2], in_=P.ss[:, 1:2]), reads=[P.b_ss], writes=[P.b_ss])
            S.op("act", lambda e: e.activation(out=P.yn[:], in_=P.y1[:], func=AF.Identity, scale=P.ss[:, 1:2]), reads=[P.b_y1, P.b_ss], writes=[P.b_yn])
            for j in range(4):
                S.op("pe", lambda e, j=j: e.transpose(out=P.tp_ps[:, 512 + j * 128:512 + (j + 1) * 128], in_=P.yn[:, j * 128:(j + 1) * 128],
                                                       identity=P.ident_bf), reads=[P.b_yn, P.b_cst], writes=[P.b_tp])
            for j in range(4):
                kk = g * 4 + j
                S.op("act", lambda e, j=j, kk=kk: e.activation(out=P.yT[:, kk, c * 128:(c + 1) * 128], in_=P.tp_ps[:, 512 + j * 128:512 + (j + 1) * 128],
                                                            func=AF.Identity, scale=P.gn[:, kk:kk + 1]), reads=[P.b_tp, P.b_gn], writes=[P.b_yT])

    front(0)
    for c in range(4):
        if c + 1 < 4:
            front(c + 1)
        back(c)


def ssd_outproj(P, w_out_t, x_reload):
    S = P.S
    S.dma("sp", lambda e: e.dma_start(out=P.hT[:], in_=x_reload), writes=[P.b_hT])
    for ct in range(D // CW):
        wts = [P.wtile(w_out_t, (rb, ct)) for rb in range(2)]
        for jj in range(CW // 128):
            dch = ct * (CW // 128) + jj
            ps, bps = P.next_bank()
            for rb in range(2):
                wt, bw = wts[rb]
                for k in range(16):
                    kk = rb * 16 + k
                    S.op("pe", lambda e, ps=ps, wt=wt, jj=jj, k=k, kk=kk: e.matmul(
                        ps[:], lhsT=wt[:, k, jj * 128:(jj + 1) * 128], rhs=P.yT[:, kk, :], start=(kk == 0), stop=(kk == 31)),
                        reads=[bw, P.b_yT], writes=[bps])
            ga = P.modv(0, "g_a", dch)
            S.op("dve", lambda e, ps=ps, dch=dch, ga=ga: e.scalar_tensor_tensor(
                out=P.hT[:, dch, :], in0=ps[:], scalar=ga, in1=P.hT[:, dch, :], op0=ALU.mult, op1=ALU.add),
                reads=[bps, P.b_mod, P.b_hT], writes=[P.b_hT])


NEG = 30000.0


def fox_in_setup(P, d):
    S = P.S
    P.wf = P.sb("wf", [128, 16, 16], BF16); P.b_wf = Buf("wf")
    P.bfb = P.sb("bfb", [128, 16]); P.b_bfb = Buf("bfb")
    P.stg = P.rl; P.b_stg = P.b_rl
    P.nstg = 0
    P.fx = P.sb("fx", [128, 16]); P.fa = P.sb("fa", [128, 16]); P.fl = P.sb("fl", [128, 16]); P.logf = P.sb("logf", [128, 16]); P.b_f = Buf("f")
    P.ck_sb = P.sb("ck_sb", [128, 4, 16]); P.b_ck = Buf("ck")
    P.cqT = P.sb("cqT", [16, 512]); P.b_cqT = Buf("cqT")
    P.carry = P.sb("carry", [128, 16]); P.carryT = P.sb("carryT", [16, 1]); P.b_carry = Buf("carry")
    S.dma("pool", lambda e: e.dma_start(out=P.wf[:], in_=d["w_f"]), writes=[P.b_wf])
    S.dma("sp", lambda e: e.dma_start(out=P.bfb[:], in_=d["bf_bc"]), writes=[P.b_bfb])
    S.op("dve", lambda e: e.memset(P.carry[:], 0.0), writes=[P.b_carry])
    S.op("dve", lambda e: e.memset(P.carryT[:], 0.0), writes=[P.b_carry])


def fox_inproj(P, tt, d):
    S = P.S
    mp = P.misc_ps
    tsl = slice(tt * TT, (tt + 1) * TT)
    P.h1_store = S.dma("act", lambda e: e.dma_start(out=d["h1T"][:, :, tsl], in_=P.hT[:]), reads=[P.b_hT], writes=[d["b_h1T"]])
    P.norm_mod(1, 0)
    w = d["fox_w_in_t"]
    for ct in range(16):
        wt, bw = P.wtile(w, (0, ct))
        for jj in range(2):
            hq = (ct % 8) * 2 + jj
            ps, bps = P.next_bank()
            for k in range(DC):
                S.op("pe", lambda e, ps=ps, wt=wt, jj=jj, k=k: e.matmul(
                    ps[:], lhsT=wt[:, k, jj * 128:(jj + 1) * 128], rhs=P.uT[:, k, :], start=(k == 0), stop=(k == DC - 1)),
                    reads=[bw, P.b_uT], writes=[bps])
            i = P.nstg % 2; P.nstg += 1
            stg, bst = P.stg[i], P.b_stg[i]
            if ct < 8:
                S.op("act", lambda e, ps=ps, stg=stg: e.activation(out=stg[:], in_=ps[:], func=AF.Copy, scale=float(FD) ** -0.5), reads=[bps], writes=[bst])
                dst, bd = d["qT_d"][hq, :, tsl], d["b_q"]
            else:
                S.op("act", lambda e, ps=ps, stg=stg: e.activation(out=stg[:], in_=ps[:], func=AF.Copy), reads=[bps], writes=[bst])
                dst, bd = d["kT_own"][hq, :, tsl], d["b_k"]
            S.dma("act", lambda e, dst=dst, stg=stg: e.dma_start(out=dst, in_=stg[:]), reads=[bst], writes=[bd])
    for pair in range(4):
        wts = [P.wtile(w, (0, 16 + pair * 2 + c2)) for c2 in range(2)]
        for sub in range(4):
            ps, bps = P.next_bank()
            for c2 in range(2):
                wt, bw = wts[c2]
                for k in range(DC):
                    S.op("pe", lambda e, ps=ps, wt=wt, c2=c2, k=k, sub=sub: e.matmul(
                        ps[:, c2 * 256:(c2 + 1) * 256], lhsT=P.uT[:, k, sub * 128:(sub + 1) * 128], rhs=wt[:, k, :],
                        start=(k == 0), stop=(k == DC - 1)), reads=[bw, P.b_uT], writes=[bps])
            i = P.nstg % 2; P.nstg += 1
            stg, bst = P.stg[i], P.b_stg[i]
            S.op("act", lambda e, ps=ps, stg=stg: e.activation(out=stg[:], in_=ps[:], func=AF.Copy), reads=[bps], writes=[bst])
            kt = tt * 4 + sub
            dst = d["v_own"][pair * 4:(pair + 1) * 4, :, kt, :].rearrange("h p d -> p h d")
            S.dma("act", lambda e, dst=dst, stg=stg: e.dma_start(out=dst, in_=stg[:].rearrange("p (h d) -> p h d", h=4)),
                  reads=[bst], writes=[d["b_v"]])
    bm = P.b_sc
    for sub in range(4):
        fps = mp[:, 0:16]
        for k in range(DC):
            S.op("pe", lambda e, k=k, sub=sub: e.matmul(fps, lhsT=P.uT[:, k, sub * 128:(sub + 1) * 128], rhs=P.wf[:, k, :],
                                                   start=(k == 0), stop=(k == DC - 1)), reads=[P.b_uT, P.b_wf], writes=[bm])
        S.op("dve", lambda e: e.tensor_tensor(out=P.fx[:], in0=fps, in1=P.bfb[:], op=ALU.add), reads=[bm, P.b_bfb], writes=[P.b_f])
        S.op("dve", lambda e: e.scalar_tensor_tensor(out=P.fa[:], in0=P.fx[:], scalar=-1.0, in1=P.fx[:], op0=ALU.mult, op1=ALU.min),
             reads=[P.b_f], writes=[P.b_f])
        S.op("act", lambda e: e.activation(out=P.fa[:], in_=P.fa[:], func=AF.Exp), reads=[P.b_f], writes=[P.b_f])
        S.op("act", lambda e: e.activation(out=P.fl[:], in_=P.fa[:], func=AF.Ln, bias=1.0, scale=1.0), reads=[P.b_f], writes=[P.b_f])
        S.op("dve", lambda e: e.scalar_tensor_tensor(out=P.logf[:], in0=P.fx[:], scalar=0.0, in1=P.fl[:], op0=ALU.min, op1=ALU.subtract),
             reads=[P.b_f], writes=[P.b_f])
        S.op("pe", lambda e: e.matmul(mp[:, 16:32], lhsT=P.Lm, rhs=P.logf[:], start=True, stop=True), reads=[P.b_f, P.b_cst], writes=[bm])
        S.op("pe", lambda e: e.matmul(mp[:, 32:48], lhsT=P.ones, rhs=P.logf[:], start=True, stop=True), reads=[P.b_f, P.b_cst], writes=[bm])
        S.op("pe", lambda e: e.matmul(mp[0:16, 64:192], lhsT=P.logf[:], rhs=P.Lm, start=True, stop=True), reads=[P.b_f, P.b_cst], writes=[bm])
        S.op("pe", lambda e: e.matmul(mp[0:16, 200:201], lhsT=P.logf[:], rhs=P.ones[:, 0:1], start=True, stop=True), reads=[P.b_f, P.b_cst], writes=[bm])
        S.op("dve", lambda e, sub=sub: e.tensor_tensor(out=P.ck_sb[:, sub, :], in0=mp[:, 16:32], in1=P.carry[:], op=ALU.add),
             reads=[bm, P.b_carry], writes=[P.b_ck])
        S.op("dve", lambda e, sub=sub: e.tensor_scalar(out=P.cqT[:, sub * 128:(sub + 1) * 128], in0=mp[0:16, 64:192], scalar1=P.carryT[:, 0:1], scalar2=None,
                                                       op0=ALU.add), reads=[bm, P.b_carry], writes=[P.b_cqT])
        S.op("dve", lambda e: e.tensor_tensor(out=P.carry[:], in0=mp[:, 32:48], in1=P.carry[:], op=ALU.add), reads=[bm, P.b_carry], writes=[P.b_carry])
        S.op("dve", lambda e: e.tensor_tensor(out=P.carryT[:], in0=mp[0:16, 200:201], in1=P.carryT[:], op=ALU.add), reads=[bm, P.b_carry], writes=[P.b_carry])
    S.dma("act", lambda e: e.dma_start(out=d["ck_own"][:, tt * 4:(tt + 1) * 4, :], in_=P.ck_sb[:]), reads=[P.b_ck], writes=[d["b_ckd"]])
    S.dma("act", lambda e: e.dma_start(out=d["cq_d"][:, tsl], in_=P.cqT[:]), reads=[P.b_cqT], writes=[d["b_cqd"]])


def attn_setup(P, d):
    S = P.S
    P.sc_ps = [P.ps("scps%d" % i, [128, 512]) for i in range(3)]; P.b_scp = [Buf("scp%d" % i) for i in range(3)]
    P.o_ps = [P.ps("ops%d" % i, [128, 512]) for i in range(2)]; P.b_o = [Buf("o%d" % i) for i in range(2)]
    P.sum_ps = P.ps("sumps", [128, 512]); P.b_sum = Buf("sum")
    P.kT = [P.sb("kT%d" % i, [128, 2, T], BF16) for i in range(2)]; P.b_kT = [Buf("kT%d" % i) for i in range(2)]
    P.vv = [P.sb("vv%d" % i, [128, 2, 16, 128], BF16) for i in range(2)]; P.b_vv = [Buf("vv%d" % i) for i in range(2)]
    P.qh = [P.sb("qh%d" % i, [128, 512], BF16) for i in range(2)]; P.b_qh = [Buf("qh%d" % i) for i in range(2)]
    P.cqb = [P.sb("cqb%d" % i, [128, 512]) for i in range(2)]; P.b_cqb = [Buf("cqb%d" % i) for i in range(2)]
    P.ssb = [P.sb("ssb%d" % i, [128, 512]) for i in range(3)]; P.b_ssb = [Buf("ssb%d" % i) for i in range(3)]
    P.pt = [P.sb("pt%d" % i, [128, 512], BF16) for i in range(4)]; P.b_pt = [Buf("pt%d" % i) for i in range(4)]
    P.rs = P.sb("rs", [128, 512]); P.b_rs = Buf("rs")
    P.nck = P.sb("nck", [128, 2, 16, 16]); P.b_nck = Buf("nck")
    P.offA = P.sb("offA", [128, 16]); P.b_offA = Buf("offA")
    P.negL = P.sb("negL", [128, 128]); P.b_negL = Buf("negL")
    P.fm1 = P.sb("fm1", [128, 1])
    if not hasattr(P, "flag"):
        P.flag = P.sb("flag_sb", [128, 1]); P.b_flag = Buf("flag")
        S.dma("sp", lambda e: e.dma_start(out=P.flag[:], in_=d["flag"]), writes=[P.b_flag])
    S.op("dve", lambda e: e.tensor_scalar(out=P.negL[:], in0=P.Lm, scalar1=-1.0, scalar2=NEG, op0=ALU.add, op1=ALU.mult),
         reads=[P.b_cst], writes=[P.b_negL])
    S.op("dve", lambda e: e.tensor_scalar(out=P.fm1[:], in0=P.flag[:], scalar1=-1.0, scalar2=NEG, op0=ALU.add, op1=ALU.mult),
         reads=[P.b_flag], writes=[P.b_flag])
    S.dma("sp", lambda e: e.dma_start(out=P.offA[:], in_=d["ck_prev"][127:128, 15, :].to_broadcast([128, 16])), reads=[d["b_ckp"]], writes=[P.b_offA])
    S.op("dve", lambda e: e.tensor_scalar(out=P.offA[:], in0=P.offA[:], scalar1=P.flag[:, 0:1], scalar2=None, op0=ALU.mult),
         reads=[P.b_offA, P.b_flag], writes=[P.b_offA])
    S.dma("sp", lambda e: e.dma_start(out=P.nck[:, 0, :, :], in_=d["ck_prev"]), reads=[d["b_ckp"]], writes=[P.b_nck])
    S.dma("sp", lambda e: e.dma_start(out=P.nck[:, 1, :, :], in_=d["ck_own"]), reads=[d["b_ckd"]], writes=[P.b_nck])
    S.op("dve", lambda e: e.tensor_scalar(out=P.nck[:, 0, :, :], in0=P.nck[:, 0, :, :], scalar1=-1.0, scalar2=P.fm1[:, 0:1], op0=ALU.mult, op1=ALU.add),
         reads=[P.b_nck, P.b_flag], writes=[P.b_nck])
    S.op("dve", lambda e: e.tensor_tensor(out=P.nck[:, 1, :, :], in0=P.nck[:, 1, :, :], in1=P.offA[:].unsqueeze(1).to_broadcast([128, 16, 16]), op=ALU.add),
         reads=[P.b_nck, P.b_offA], writes=[P.b_nck])
    S.op("dve", lambda e: e.tensor_scalar(out=P.nck[:, 1, :, :], in0=P.nck[:, 1, :, :], scalar1=-1.0, scalar2=None, op0=ALU.mult),
         reads=[P.b_nck], writes=[P.b_nck])


def attn_tile(P, j, d):
    S = P.S
    LA = 2
    qsl = slice(j * TT, (j + 1) * TT)
    nown = 4 * j + 4

    def load_head(h):
        i2 = h % 2
        kT, bk = P.kT[i2], P.b_kT[i2]
        vv, bv = P.vv[i2], P.b_vv[i2]
        qh, bq = P.qh[i2], P.b_qh[i2]
        cqb, bc = P.cqb[i2], P.b_cqb[i2]
        S.dma("sp", lambda e: e.dma_start(out=qh[:], in_=d["qT_d"][h, :, qsl]), writes=[bq])
        S.dma("sp", lambda e: e.dma_start(out=cqb[:], in_=d["cq_d"][h:h + 1, qsl].to_broadcast([128, TT])), writes=[bc])
        rk = [d["b_kp_l"][h // 4]] if "b_kp_l" in d else []
        rv = [d["b_vp_l"][h // 4]] if "b_vp_l" in d else []
        S.dma("sp", lambda e: e.dma_start(out=kT[:, 0, :], in_=d["kT_prev"][h]), reads=rk, writes=[bk])
        S.dma("sp", lambda e: e.dma_start(out=vv[:, 0, :, :], in_=d["v_prev"][h]), reads=rv, writes=[bv])
        S.dma("sp", lambda e: e.dma_start(out=kT[:, 1, 0:nown * 128], in_=d["kT_own"][h, :, 0:nown * 128]), writes=[bk])
        S.dma("sp", lambda e: e.dma_start(out=vv[:, 1, 0:nown, :], in_=d["v_own"][h, :, 0:nown, :]), writes=[bv])
        S.op("dve", lambda e: e.tensor_scalar(out=cqb[:], in0=cqb[:], scalar1=P.offA[:, h:h + 1], scalar2=None, op0=ALU.add),
             reads=[bc, P.b_offA], writes=[bc])

    its = []
    for h in range(FH):
        tiles = [(0, kt, 0) for kt in range(16)] + [(1, kt, max(0, kt - 4 * j)) for kt in range(nown)]
        for n, (s_, kt, a) in enumerate(tiles):
            its.append((h, s_, kt, a, n == 0, n == len(tiles) - 1))
    N = len(its)

    def front(idx):
        h, s_, kt, a, first, last = its[idx]
        if first:
            load_head(h)
        i2 = h % 2
        kT, bk, qh, bq, cqb, bc = P.kT[i2], P.b_kT[i2], P.qh[i2], P.b_qh[i2], P.cqb[i2], P.b_cqb[i2]
        q0 = a * 128
        scp, bs = P.sc_ps[idx % 3], P.b_scp[idx % 3]
        ssb, bss = P.ssb[idx % 3], P.b_ssb[idx % 3]
        pt, bp = P.pt[idx % 4], P.b_pt[idx % 4]
        S.op("pe", lambda e: e.matmul(scp[:, q0:TT], lhsT=kT[:, s_, kt * 128:(kt + 1) * 128], rhs=qh[:, q0:TT], start=True, stop=True),
             reads=[bk, bq], writes=[bs])
        S.op("dve", lambda e: e.tensor_tensor(out=ssb[:, q0:TT], in0=scp[:, q0:TT], in1=cqb[:, q0:TT], op=ALU.add), reads=[bs, bc], writes=[bss])
        if s_ == 1 and kt >= 4 * j:
            S.op("dve", lambda e: e.tensor_tensor(out=ssb[:, q0:q0 + 128], in0=ssb[:, q0:q0 + 128], in1=P.negL[:], op=ALU.add),
                 reads=[bss, P.b_negL], writes=[bss])
        S.op("act", lambda e: e.activation(out=pt[:, q0:TT], in_=ssb[:, q0:TT], func=AF.Exp, bias=P.nck[:, s_, kt, h:h + 1], scale=1.0),
             reads=[bss, P.b_nck], writes=[bp])

    def back(idx):
        h, s_, kt, a, first, last = its[idx]
        i2 = h % 2
        vv, bv = P.vv[i2], P.b_vv[i2]
        q0 = a * 128
        pt, bp = P.pt[idx % 4], P.b_pt[idx % 4]
        ops, bo = P.o_ps[i2], P.b_o[i2]
        S.op("pe", lambda e: e.matmul(ops[:, q0:TT], lhsT=vv[:, s_, kt, :], rhs=pt[:, q0:TT], start=first, stop=last), reads=[bv, bp], writes=[bo])
        S.op("pe", lambda e: e.matmul(P.sum_ps[:, q0:TT], lhsT=P.ones_bf, rhs=pt[:, q0:TT], start=first, stop=last), reads=[bp, P.b_cst], writes=[P.b_sum])
        if last:
            S.op("dve", lambda e: e.reciprocal(out=P.rs[:], in_=P.sum_ps[:]), reads=[P.b_sum], writes=[P.b_rs])
            S.op("dve", lambda e: e.tensor_tensor(out=P.yT[:, h, :], in0=ops[:], in1=P.rs[:], op=ALU.mult), reads=[bo, P.b_rs], writes=[P.b_yT])

    for idx in range(N + LA):
        if idx < N:
            front(idx)
        if idx - LA >= 0:
            back(idx - LA)


def fox_outproj(P, j, d):
    S = P.S
    S.dma("sp", lambda e: e.dma_start(out=P.hT[:], in_=d["h1T"][:, :, j * TT:(j + 1) * TT]), reads=[d["b_h1T"]], writes=[P.b_hT])
    for ct in range(D // CW):
        wt, bw = P.wtile(d["fox_w_out_t"], (0, ct))
        for jj in range(CW // 128):
            dch = ct * (CW // 128) + jj
            ps, bps = P.next_bank()
            for k in range(16):
                S.op("pe", lambda e, ps=ps, wt=wt, jj=jj, k=k: e.matmul(
                    ps[:], lhsT=wt[:, k, jj * 128:(jj + 1) * 128], rhs=P.yT[:, k, :], start=(k == 0), stop=(k == 15)),
                    reads=[bw, P.b_yT], writes=[bps])
            ga = P.modv(1, "g_a", dch)
            S.op("dve", lambda e, ps=ps, dch=dch, ga=ga: e.scalar_tensor_tensor(
                out=P.hT[:, dch, :], in0=ps[:], scalar=ga, in1=P.hT[:, dch, :], op0=ALU.mult, op1=ALU.add),
                reads=[bps, P.b_mod, P.b_hT], writes=[P.b_hT])


def build_program(mode, ntiles=NTT, do_mlp=True):
    P = Prog()
    S = P.S
    d = {}
    nmix = P.din("nmix", [128, 2, DC]); nmlp = P.din("nmlp", [128, 2, DC])
    P.setup_common()
    finals = []
    if mode == "l1":
        c_col = P.din("c_col", [128, DC])
        w_ada_t = P.din("w_ada_t", [2, 48, 128, 16, CW])
        b_ada_col = P.din("b_ada_col", [128, 2, 96])
        mod_o = P.dout("mod_o", [128, 2, 96])
        P.adaln(w_ada_t, b_ada_col, c_col, nmix, nmlp, mod_out=mod_o)
        finals.append(P.mod_store)
    else:
        mod_in = P.din("mod_in", [128, 2, 96])
        P.adaln(None, None, None, nmix, nmlp, mod_in=mod_in)
    d["flag"] = P.din("flag", [128, 1])
    if mode in ("l1", "l2"):
        xT = P.din("xT", [128, DC, T])
        d["w_in_g"] = P.din("w_in_g", [8, 5, 128, 16, CW]); d["w_dt"] = P.din("w_dt", [128, 16, 64])
        d["convw_col"] = P.din("convw_col", [128, 48, 4]); d["convb_col"] = P.din("convb_col", [128, 48])
        for n in ("dtb_bc", "alog_bc", "D_bc"):
            d[n] = P.din(n, [128, 64])
        d["gn_col"] = P.din("gn_col", [128, 32])
        S_in = P.din("S_in", [128, 4096]); halo_in = P.din("halo_in", [128, 48, 3])
        P.phase_alloc = True
        ssd_setup(P, d)
        ssd_init_state(P, S_in, halo_in)
    if mode == "l1":
        S_out = P.dout("S_out", [128, 4096]); halo_out = P.dout("halo_out", [128, 48, 3])
        for tt in range(NTT):
            S.dma("sp", lambda e, tt=tt: e.dma_start(out=P.hT[:], in_=xT[:, :, tt * TT:(tt + 1) * TT]), writes=[P.b_hT])
            P.norm_mod(0, 0)
            ssd_dt(P)
            for g in range(8):
                ssd_group(P, g, d["w_in_g"], True)
        finals.append(S.dma("act", lambda e: e.dma_start(out=S_out, in_=P.Sst[:].rearrange("p g c -> p (g c)")), reads=P.b_S, writes=[Buf()]))
        finals.append(S.dma("act", lambda e: e.dma_start(out=halo_out, in_=P.halo[:]), reads=P.b_halo, writes=[Buf()]))
    if mode == "l2":
        d["w_out_t"] = P.din("w_out_t", [2, 8, 128, 16, CW])
        w_up_t = P.din("w_up_t", [1, 32, 128, 16, CW]); w_down_t = P.din("w_down_t", [4, 8, 128, 16, CW])
        d["fox_w_in_t"] = P.din("fox_w_in_t", [1, 24, 128, 16, CW]); d["w_f"] = P.din("w_f", [128, 16, 16]); d["bf_bc"] = P.din("bf_bc", [128, 16])
        d["h1T"] = P.dout("h1T", [128, DC, T]); d["qT_d"] = P.dout("qT_d", [FH, 128, T], BF16)
        d["kT_own"] = P.dout("kT_own", [FH, 128, T], BF16); d["v_own"] = P.dout("v_own", [FH, 128, 16, 128], BF16)
        d["ck_own"] = P.dout("ck_own", [128, 16, 16]); d["cq_d"] = P.dout("cq_d", [FH, T])
        for n in ("b_h1T", "b_q", "b_k", "b_v", "b_ckd", "b_cqd"):
            d[n] = Buf(n)
        fox_in_setup(P, d)
        for tt in range(ntiles):
            xs = xT[:, :, tt * TT:(tt + 1) * TT]
            S.dma("sp", lambda e, xs=xs: e.dma_start(out=P.hT[:], in_=xs), writes=[P.b_hT])
            P.norm_mod(0, 0)
            ssd_dt(P)
            for g in range(8):
                ssd_group(P, g, d["w_in_g"], False)
            ssd_outproj(P, d["w_out_t"], xs)
            if do_mlp:
                P.mlp(0, w_up_t, w_down_t, wl=0)
            fox_inproj(P, tt, d)
        S.barrier()
    if mode == "l3":
        w_up_t = P.din("w_up_t", [1, 32, 128, 16, CW]); w_down_t = P.din("w_down_t", [4, 8, 128, 16, CW])
        d["fox_w_out_t"] = P.din("fox_w_out_t", [1, 8, 128, 16, CW])
        fnorm = P.din("fnorm", [128, DC])
        d["h1T"] = P.din("h1T", [128, DC, T]); d["qT_d"] = P.din("qT_d", [FH, 128, T], BF16)
        d["kT_own"] = P.din("kT_own", [FH, 128, T], BF16); d["v_own"] = P.din("v_own", [FH, 128, 16, 128], BF16)
        d["kT_prev"] = P.din("kT_prev", [FH, 128, T], BF16); d["v_prev"] = P.din("v_prev", [FH, 128, 16, 128], BF16)
        d["ck_own"] = P.din("ck_own", [128, 16, 16]); d["ck_prev"] = P.din("ck_prev", [128, 16, 16]); d["cq_d"] = P.din("cq_d", [FH, T])
        outT = P.dout("outT", [128, DC, T])
        for n in ("b_h1T", "b_q", "b_k", "b_v", "b_ckd", "b_cqd", "b_kp", "b_vp", "b_ckp"):
            d[n] = Buf(n)
        P.phase_alloc = True
        attn_setup(P, d)
        fn = P.sb("fn", [128, DC]); b_fn = Buf("fn")
        S.dma("sp", lambda e: e.dma_start(out=fn[:], in_=fnorm), writes=[b_fn])
        P.b_outf = P.b_hT
        for j in range(ntiles):
            attn_tile(P, j, d)
            fox_outproj(P, j, d)
            if do_mlp:
                P.mlp(1, w_up_t, w_down_t, wl=0)
            P.b_gs_save = P.b_gs
            P.norm_mod(1, 0, gs_ap=fn, out_f32=P.hT)
            finals.append(S.dma("act", lambda e, j=j: e.dma_start(out=outT[:, :, j * TT:(j + 1) * TT], in_=P.hT[:]), reads=[P.b_hT], writes=[Buf()]))
    S.barrier()
    return P.finish(finals)


def _stack2(a):
    return np.ascontiguousarray(np.stack([col_layout(a[i]) for i in range(2)], axis=1))


def host_prepare(inp):
    H = {}
    H["consts"] = make_consts()
    H["nmix"] = _stack2(inp["norm_mix"]); H["nmlp"] = _stack2(inp["norm_mlp"])
    H["w_ada_t"] = np.stack([tile_weight(inp["w_ada"][i])[0] for i in range(2)])
    H["b_ada_col"] = _stack2(inp["b_ada"])
    w = inp["ssd_w_in"][0]
    tiles = []
    for g in range(8):
        wg = np.concatenate([w[:, g * 512:(g + 1) * 512], w[:, 4096 + g * 512:4096 + (g + 1) * 512],
                             w[:, 8192 + g * 128:8192 + (g + 1) * 128], w[:, 9216 + g * 128:9216 + (g + 1) * 128]], axis=1)
        tiles.append(tile_weight(wg)[0])
    H["w_in_g"] = np.stack(tiles)
    H["w_dt"] = tile_weight(w[:, 10240:10304], cw=64)[0, 0]
    chans = []
    for g in range(8):
        for j in range(4):
            chans.append(np.arange(g * 512 + j * 128, g * 512 + (j + 1) * 128))
        chans.append(np.arange(4096 + g * 128, 4096 + (g + 1) * 128))
        chans.append(np.arange(5120 + g * 128, 5120 + (g + 1) * 128))
    chans = np.stack(chans)
    H["convw_col"] = np.ascontiguousarray(inp["ssd_conv_w"][0][:, chans].transpose(2, 1, 0))
    H["convb_col"] = np.ascontiguousarray(inp["ssd_conv_b"][0][chans].T)
    H["dtb_bc"] = np.ascontiguousarray(np.broadcast_to(inp["ssd_dt_bias"][0], (128, 64)))
    H["alog_bc"] = np.ascontiguousarray(np.broadcast_to(inp["ssd_A_log"][0], (128, 64)))
    H["D_bc"] = np.ascontiguousarray(np.broadcast_to(inp["ssd_D"][0], (128, 64)))
    H["gn_col"] = col_layout(inp["ssd_gnorm"][0])
    H["w_out_t"] = tile_weight(inp["ssd_w_out"][0])
    H["w_up_t"] = [tile_weight(inp["w_up"][i]) for i in range(2)]
    H["w_down_t"] = [tile_weight(inp["w_down"][i]) for i in range(2)]
    fw = inp["fox_w_in"][0]
    H["fox_w_in_t"] = tile_weight(fw[:, :6144])
    H["w_f"] = tile_weight(fw[:, 6144:6160], cw=16)[0, 0]
    H["bf_bc"] = np.ascontiguousarray(np.broadcast_to(inp["fox_b_f"][0], (128, 16)))
    H["fox_w_out_t"] = tile_weight(inp["fox_w_out"][0])
    H["fnorm"] = col_layout(inp["final_norm"])
    return H


_PROGS = {}


def _prog(mode):
    if mode not in _PROGS:
        _PROGS[mode] = build_program(mode)
    return _PROGS[mode]


def kernel(**inp):
    inp = {k: np.asarray(v) for k, v in inp.items()}
    H = host_prepare(inp)
    cores = list(range(8))
    xT = [feat_major(inp["x"][c // 2, (c % 2) * T:(c % 2 + 1) * T]) for c in cores]
    flag = [np.full((128, 1), float(c % 2), np.float32) for c in cores]
    base = {k: H[k] for k in ("consts", "nmix", "nmlp")}
    ssdw = {k: H[k] for k in ("w_in_g", "w_dt", "convw_col", "convb_col", "dtb_bc", "alog_bc", "D_bc", "gn_col")}
    zS = np.zeros((128, 4096), np.float32); zh = np.zeros((128, 48, 3), np.float32)
    maps = []
    for c in cores:
        m = dict(base); m.update(ssdw)
        m.update({"c_col": col_layout(inp["c"][c // 2]), "w_ada_t": H["w_ada_t"], "b_ada_col": H["b_ada_col"],
                  "flag": np.zeros((128, 1), np.float32), "xT": xT[c], "S_in": zS, "halo_in": zh})
        maps.append(m)
    r1 = run_bass_kernel_spmd(_prog("l1"), maps, core_ids=cores).results
    maps = []
    for c in cores:
        a = c - (c % 2)
        m = dict(base); m.update(ssdw)
        m.update({"mod_in": r1[c]["mod_o"], "flag": flag[c], "xT": xT[c], "S_in": r1[a]["S_out"], "halo_in": r1[a]["halo_out"],
                  "w_out_t": H["w_out_t"], "w_up_t": H["w_up_t"][0], "w_down_t": H["w_down_t"][0],
                  "fox_w_in_t": H["fox_w_in_t"], "w_f": H["w_f"], "bf_bc": H["bf_bc"]})
        maps.append(m)
    r2 = run_bass_kernel_spmd(_prog("l2"), maps, core_ids=cores).results
    maps = []
    for c in cores:
        a = c - (c % 2)
        m = dict(base)
        m.update({"mod_in": r1[c]["mod_o"], "flag": flag[c], "w_up_t": H["w_up_t"][1], "w_down_t": H["w_down_t"][1],
                  "fox_w_out_t": H["fox_w_out_t"], "fnorm": H["fnorm"],
                  "h1T": r2[c]["h1T"], "qT_d": r2[c]["qT_d"], "kT_own": r2[c]["kT_own"], "v_own": r2[c]["v_own"],
                  "ck_own": r2[c]["ck_own"], "cq_d": r2[c]["cq_d"],
                  "kT_prev": r2[a]["kT_own"], "v_prev": r2[a]["v_own"], "ck_prev": r2[a]["ck_own"]})
        maps.append(m)
    r3 = run_bass_kernel_spmd(_prog("l3"), maps, core_ids=cores).results
    out = np.empty((BATCH, SEQ, D), np.float32)
    for c in cores:
        out[c // 2, (c % 2) * T:(c % 2 + 1) * T] = from_feat_major(r3[c]["outT"])
    return out


PAIRS = [[0, 1], [2, 3], [4, 5], [6, 7]]


class View:
    def __init__(self, base, **over):
        self.__dict__["_b"] = base
        self.__dict__.update(over)

    def __getattr__(self, k):
        return getattr(self.__dict__["_b"], k)


class HeadChunks:
    def __init__(self, views):
        self.views = views

    def __getitem__(self, key):
        if not isinstance(key, tuple):
            key = (key,)
        h = key[0]
        if isinstance(h, slice):
            c = h.start // 4
            assert h.stop - h.start == 4 and h.start % 4 == 0
            return self.views[c][(slice(0, 4),) + key[1:]]
        return self.views[h // 4][(h % 4,) + key[1:]]


def build_fused(ntiles=NTT):
    P = Prog()
    S = P.S
    nc = P.nc
    d = {}
    nmix = P.din("nmix", [128, 2, DC]); nmlp = P.din("nmlp", [128, 2, DC])
    P.setup_common()
    c_col = P.din("c_col", [128, DC])
    w_ada_t = P.din("w_ada_h", [2, 24, 128, 16, CW])
    b_ada_col = P.din("b_ada_h", [128, 2, 48])
    m_own = nc.dram_tensor("m_own", [128, 96], F32).ap(); m_g = nc.dram_tensor("m_g", [256, 96], F32).ap()
    P.adaln(w_ada_t, b_ada_col, c_col, nmix, nmlp, half_dram=(m_own, m_g))
    d["flag"] = P.din("flag", [128, 1])
    xT = P.din("xT", [128, DC, T])
    d["w_in_g"] = P.din("w_in_g", [8, 5, 128, 16, CW]); d["w_dt"] = P.din("w_dt", [128, 16, 64])
    d["convw_col"] = P.din("convw_col", [128, 48, 4]); d["convb_col"] = P.din("convb_col", [128, 48])
    for n in ("dtb_bc", "alog_bc", "D_bc"):
        d[n] = P.din(n, [128, 64])
    d["gn_col"] = P.din("gn_col", [128, 32])
    d["w_out_t"] = P.din("w_out_t", [2, 8, 128, 16, CW])
    w_up_t = P.din("w_up_t", [2, 32, 128, 16, CW]); w_down_t = P.din("w_down_t", [8, 8, 128, 16, CW])
    d["fox_w_in_t"] = P.din("fox_w_in_t", [1, 24, 128, 16, CW]); d["w_f"] = P.din("w_f", [128, 16, 16]); d["bf_bc"] = P.din("bf_bc", [128, 16])
    d["fox_w_out_t"] = P.din("fox_w_out_t", [1, 8, 128, 16, CW])
    fnorm = P.din("fnorm", [128, DC])
    outT = P.dout("outT", [128, DC, T])
    d["h1T"] = nc.dram_tensor("h1T", [128, DC, T], F32).ap()
    qT2 = nc.dram_tensor("qT2", [FH * 128, T], BF16).ap()
    kT2 = [nc.dram_tensor("kT2_%d" % i, [512, T], BF16).ap() for i in range(4)]
    kT_g = [nc.dram_tensor("kT_g%d" % i, [1024, T], BF16).ap() for i in range(4)]
    v2 = [nc.dram_tensor("v2_%d" % i, [512, 2048], BF16).ap() for i in range(4)]
    v_g = [nc.dram_tensor("v_g%d" % i, [1024, 2048], BF16).ap() for i in range(4)]
    ck2 = nc.dram_tensor("ck2", [128, 256], F32).ap(); ck_g = nc.dram_tensor("ck_g", [256, 256], F32).ap()
    d["cq_d"] = nc.dram_tensor("cq_d", [FH, T], F32).ap()
    d["qT_d"] = qT2.rearrange("(h p) t -> h p t", p=128)
    d["kT_own"] = HeadChunks([x.rearrange("(h p) t -> h p t", p=128) for x in kT2])
    d["kT_prev"] = HeadChunks([x[0:512, :].rearrange("(h p) t -> h p t", p=128) for x in kT_g])
    d["v_own"] = HeadChunks([x.rearrange("(h p) (k e) -> h p k e", p=128, e=128) for x in v2])
    d["v_prev"] = HeadChunks([x[0:512, :].rearrange("(h p) (k e) -> h p k e", p=128, e=128) for x in v_g])
    d["ck_own"] = ck2.rearrange("p (k h) -> p k h", h=16)
    d["ck_prev"] = ck_g[0:128, :].rearrange("p (k h) -> p k h", h=16)
    for n in ("b_h1T", "b_q", "b_k", "b_v", "b_ckd", "b_cqd", "b_kp", "b_vp", "b_ckp"):
        d[n] = Buf(n)
    xpT = P.din("xpT", [128, DC, T])
    w_in_pre = P.din("w_in_pre", [4, 5, 128, 16, CW]); w_dt_pre = P.din("w_dt_pre", [128, 16, 32])
    convw_pre = P.din("convw_pre", [128, 24, 4]); convb_pre = P.din("convb_pre", [128, 24])
    dtb_pre = P.din("dtb_pre", [128, 32]); alog_pre = P.din("alog_pre", [128, 32])
    P.phase_alloc = True
    ssd_setup(P, d)
    cwp = P.sb("cwp", [128, 24, 4]); cbp = P.sb("cbp", [128, 24]); b_cwp = Buf("cwp")
    dtbp = P.sb("dtbp", [128, 32]); Abcp = P.sb("Abcp", [128, 32]); b_prmp = Buf("prmp")
    wdtp = P.sb("wdtp", [128, 16, 32], BF16); b_wdtp = Buf("wdtp")
    S.dma("sp", lambda e: e.dma_start(out=cwp[:], in_=convw_pre), writes=[b_cwp])
    S.dma("sp", lambda e: e.dma_start(out=cbp[:], in_=convb_pre), writes=[b_cwp])
    S.dma("sp", lambda e: e.dma_start(out=dtbp[:], in_=dtb_pre), writes=[b_prmp])
    S.dma("sp", lambda e: e.dma_start(out=Abcp[:], in_=alog_pre), writes=[b_prmp])
    S.dma("pool", lambda e: e.dma_start(out=wdtp[:], in_=w_dt_pre), writes=[b_wdtp])
    S.op("dve", lambda e: e.tensor_scalar(out=cwp[:], in0=cwp[:], scalar1=0.5, scalar2=None, op0=ALU.mult), reads=[b_cwp], writes=[b_cwp])
    S.op("dve", lambda e: e.tensor_scalar(out=cbp[:], in0=cbp[:], scalar1=0.5, scalar2=None, op0=ALU.mult), reads=[b_cwp], writes=[b_cwp])
    S.op("act", lambda e: e.activation(out=Abcp[:], in_=Abcp[:], func=AF.Exp), reads=[b_prmp], writes=[b_prmp])
    S.op("dve", lambda e: e.tensor_scalar(out=Abcp[:], in0=Abcp[:], scalar1=-1.0, scalar2=None, op0=ALU.mult), reads=[b_prmp], writes=[b_prmp])
    Q = View(P, cw=cwp, cb=cbp, b_cw=b_cwp, dtb=dtbp, Abc=Abcp, b_prm=b_prmp, wdt=wdtp, b_wdt=b_wdtp, nh=32,
             dt=P.dt[:, :, 0:32], dtA=P.dtA[:, :, 0:32], ea=P.ea[:, :, 0:32], dte=P.dte[:, :, 0:32], cd=P.cd[:, :, 0:32],
             sp1=P.sp1[:, 0:32], sp2=P.sp2[:, 0:32], sp3=P.sp3[:, 0:32])
    S.op("dve", lambda e: e.memset(P.Sst[:], 0.0), writes=P.b_S)
    S.op("dve", lambda e: e.memset(P.halo[:], 0.0), writes=P.b_halo)
    for tt in range(ntiles):
        S.dma("sp", lambda e, tt=tt: e.dma_start(out=P.hT[:], in_=xpT[:, :, tt * TT:(tt + 1) * TT]), writes=[P.b_hT])
        P.norm_mod(0, 0)
        ssd_dt(Q)
        for gl in range(4):
            ssd_group(Q, gl, w_in_pre, True)
    S_own = nc.dram_tensor("S_own2", [128, 2048], F32).ap(); S_g = nc.dram_tensor("S_g2", [256, 2048], F32).ap()
    h_own = nc.dram_tensor("h_own2", [128, 72], F32).ap(); h_g = nc.dram_tensor("h_g2", [256, 72], F32).ap()
    S.dma("act", lambda e: e.dma_start(out=S_own, in_=P.Sst[:, 0:4, :].rearrange("p g c -> p (g c)")), reads=P.b_S, writes=[Buf()])
    S.dma("act", lambda e: e.dma_start(out=h_own.rearrange("p (c j) -> p c j", j=3), in_=P.halo[:, 0:24, :]), reads=P.b_halo, writes=[Buf()])
    S.barrier()
    S.cc(lambda e: e.collective_compute("AllGather", ALU.bypass, replica_groups=PAIRS, ins=[S_own], outs=[S_g]))
    S.cc(lambda e: e.collective_compute("AllGather", ALU.bypass, replica_groups=PAIRS, ins=[h_own], outs=[h_g]))
    S.barrier()
    for r in range(2):
        S.dma("sp", lambda e, r=r: e.dma_start(out=P.Sst[:, 4 * r:4 * r + 4, :].rearrange("p g c -> p (g c)"), in_=S_g[128 * r:128 * (r + 1), :]), writes=P.b_S)
        S.dma("sp", lambda e, r=r: e.dma_start(out=P.halo[:, 24 * r:24 * r + 24, :], in_=h_g[128 * r:128 * (r + 1), :].rearrange("p (c j) -> p c j", j=3)),
              writes=P.b_halo)
    S.op("dve", lambda e: e.tensor_scalar(out=P.Sst[:], in0=P.Sst[:], scalar1=P.flag[:, 0:1], scalar2=None, op0=ALU.mult),
         reads=P.b_S + [P.b_flag], writes=P.b_S)
    S.op("dve", lambda e: e.tensor_scalar(out=P.halo[:], in0=P.halo[:], scalar1=P.flag[:, 0:1], scalar2=None, op0=ALU.mult),
         reads=P.b_halo + [P.b_flag], writes=P.b_halo)
    fox_in_setup(P, d)
    for tt in range(ntiles):
        xs = xT[:, :, tt * TT:(tt + 1) * TT]
        S.dma("sp", lambda e, xs=xs: e.dma_start(out=P.hT[:], in_=xs), writes=[P.b_hT])
        P.norm_mod(0, 0)
        ssd_dt(P)
        for g in range(8):
            ssd_group(P, g, d["w_in_g"], False)
        ssd_outproj(P, d["w_out_t"], xs)
        P.mlp(0, w_up_t, w_down_t)
        fox_inproj(P, tt, d)
    P.end_phase()
    d["b_ckp"] = Buf("ckp")
    op = S.cc(lambda e: e.collective_compute("AllGather", ALU.bypass, replica_groups=PAIRS, ins=[ck2], outs=[ck_g]))
    d["b_ckp"].writer = op
    d["b_kp_l"] = [Buf("kp%d" % i) for i in range(4)]
    d["b_vp_l"] = [Buf("vp%d" % i) for i in range(4)]
    for i in range(4):
        op = S.cc(lambda e, i=i: e.collective_compute("AllGather", ALU.bypass, replica_groups=PAIRS, ins=[kT2[i]], outs=[kT_g[i]]))
        d["b_kp_l"][i].writer = op
        op = S.cc(lambda e, i=i: e.collective_compute("AllGather", ALU.bypass, replica_groups=PAIRS, ins=[v2[i]], outs=[v_g[i]]))
        d["b_vp_l"][i].writer = op
    attn_setup(P, d)
    fn = P.sb("fn", [128, DC]); b_fn = Buf("fn")
    S.dma("sp", lambda e: e.dma_start(out=fn[:], in_=fnorm), writes=[b_fn])
    P.b_outf = P.b_hT
    finals = []
    for j in range(ntiles):
        attn_tile(P, j, d)
        fox_outproj(P, j, d)
        P.mlp(1, w_up_t, w_down_t)
        P.norm_mod(1, 0, gs_ap=fn, out_f32=P.hT)
        finals.append(S.dma("act", lambda e, j=j: e.dma_start(out=outT[:, :, j * TT:(j + 1) * TT], in_=P.hT[:]), reads=[P.b_hT], writes=[Buf()]))
    S.barrier()
    return P.finish(finals)


_FUSED = []


def kernel(**inp):
    inp = {k: np.asarray(v) for k, v in inp.items()}
    H = host_prepare(inp)
    cores = list(range(8))
    shared = {k: H[k] for k in ("consts", "nmix", "nmlp", "w_in_g", "w_dt", "convw_col", "convb_col",
                                "dtb_bc", "alog_bc", "D_bc", "gn_col", "w_out_t", "fox_w_in_t", "w_f", "bf_bc", "fox_w_out_t", "fnorm")}
    shared["w_up_t"] = np.concatenate(H["w_up_t"], axis=0)
    shared["w_down_t"] = np.concatenate(H["w_down_t"], axis=0)
    maps = []
    xTs = [feat_major(inp["x"][c // 2, (c % 2) * T:(c % 2 + 1) * T]) for c in cores]
    for c in cores:
        m = dict(shared)
        m["xT"] = xTs[c]
        m["c_col"] = col_layout(inp["c"][c // 2])
        m["flag"] = np.full((128, 1), float(c % 2), np.float32)
        hf = c % 2
        m["xpT"] = xTs[c - hf]
        m["w_ada_h"] = np.ascontiguousarray(H["w_ada_t"][:, 24 * hf:24 * hf + 24])
        m["b_ada_h"] = np.ascontiguousarray(H["b_ada_col"][:, :, 48 * hf:48 * hf + 48])
        m["w_in_pre"] = np.ascontiguousarray(H["w_in_g"][4 * hf:4 * hf + 4])
        m["w_dt_pre"] = np.ascontiguousarray(H["w_dt"][:, :, 32 * hf:32 * hf + 32])
        m["convw_pre"] = np.ascontiguousarray(H["convw_col"][:, 24 * hf:24 * hf + 24])
        m["convb_pre"] = np.ascontiguousarray(H["convb_col"][:, 24 * hf:24 * hf + 24])
        m["dtb_pre"] = np.ascontiguousarray(H["dtb_bc"][:, 32 * hf:32 * hf + 32])
        m["alog_pre"] = np.ascontiguousarray(H["alog_bc"][:, 32 * hf:32 * hf + 32])
        maps.append(m)
    if not _FUSED:
        _FUSED.append(build_fused())
    res = run_bass_kernel_spmd(_FUSED[0], maps, core_ids=cores).results
    out = np.empty((BATCH, SEQ, D), np.float32)
    for c in cores:
        out[c // 2, (c % 2) * T:(c % 2 + 1) * T] = from_feat_major(res[c]["outT"])
    return out
```

```python
import contextlib
import numpy as np
import ml_dtypes
import concourse.bass as bass
import concourse.mybir as mybir
from concourse.bass_utils import run_bass_kernel_spmd

F32 = mybir.dt.float32
BF16 = mybir.dt.bfloat16
AF = mybir.ActivationFunctionType
ALU = mybir.AluOpType

D = 2048
DC = 16
SEQ = 4096
BATCH = 4
T = 2048
TT = 512
NTT = T // TT
EPS = 1e-6
DFF = 8192
CW = 256
DI = 4096
NH = 64
NG = 8
NS = 128
HP = 64
FH = 16
FD = 128

ENGS = ("pe", "act", "dve", "pool", "sp")
NO_SELF_SYNC = ("pe",)


class Buf:
    __slots__ = ("name", "writer", "readers", "dma_readers")

    def __init__(self, name=""):
        self.name = name
        self.writer = None
        self.readers = {}
        self.dma_readers = []


class Op:
    __slots__ = ("eng", "fn", "deps", "is_dma", "needs_inc", "tok_sem", "tok_val", "dma_slot", "prev_same_sem", "is_cc")

    def __init__(self, eng, fn, is_dma):
        self.eng = eng
        self.fn = fn
        self.deps = []
        self.is_dma = is_dma
        self.needs_inc = False
        self.tok_sem = None
        self.tok_val = None
        self.dma_slot = None
        self.prev_same_sem = None
        self.is_cc = False


class Sched:
    def __init__(self, nc, n_dma_sems=10):
        self.nc = nc
        self.ops = {e: [] for e in ENGS}
        self.n_dma_sems = n_dma_sems
        self.dma_count = {e: 0 for e in ENGS}
        self.dma_last = {}
        self.cc_ops = []

    def _add(self, eng, fn, reads, writes, is_dma):
        op = Op(eng, fn, is_dma)
        deps = []
        for b in reads:
            if b.writer is not None:
                deps.append(b.writer)
        for b in writes:
            if b.writer is not None:
                deps.append(b.writer)
            deps.extend(b.readers.values())
            deps.extend(b.dma_readers)
        for b in reads:
            if is_dma:
                b.dma_readers.append(op)
            else:
                b.readers[eng] = op
        for b in writes:
            b.writer = op
            b.readers = {}
            b.dma_readers = []
        if is_dma:
            slot = self.dma_count[eng] % self.n_dma_sems
            self.dma_count[eng] += 1
            op.dma_slot = slot
            op.prev_same_sem = self.dma_last.get((eng, slot))
            self.dma_last[(eng, slot)] = op
        seen = set()
        for d in deps:
            if d is op or id(d) in seen:
                continue
            seen.add(id(d))
            op.deps.append(d)
        self.ops[eng].append(op)
        return op

    def op(self, eng, fn, reads=(), writes=()):
        return self._add(eng, fn, reads, writes, False)

    def dma(self, eng, fn, reads=(), writes=()):
        return self._add(eng, fn, reads, writes, True)

    def cc(self, fn):
        op = Op("pool", fn, True)
        op.is_cc = True
        self.cc_ops.append(op)
        self.ops["pool"].append(op)
        return op

    def barrier(self):
        lasts = []
        for e in ENGS:
            for op in reversed(self.ops[e]):
                if not op.is_dma and op.fn is not None:
                    lasts.append(op)
                    break
        dmas = list(self.dma_last.values()) + list(self.cc_ops)
        for e in ENGS:
            op = Op(e, None, False)
            op.deps = list(lasts) + dmas
            self.ops[e].append(op)

    @staticmethod
    def _skip(op, d):
        return (not d.is_dma) and (not op.is_dma) and op.fn is not None and d.eng == op.eng and op.eng in NO_SELF_SYNC

    def emit(self, final_wait_ops=()):
        nc = self.nc
        for e in ENGS:
            for op in self.ops[e]:
                for d in op.deps:
                    if d.is_dma or self._skip(op, d):
                        continue
                    d.needs_inc = True
        with contextlib.ExitStack() as st:
            csem = {e: st.enter_context(nc.semaphore("c_" + e)) for e in ENGS}
            dsem = {}
            for e in ENGS:
                for s in range(min(self.n_dma_sems, self.dma_count[e])):
                    dsem[(e, s)] = st.enter_context(nc.semaphore("d_%s_%d" % (e, s)))
            for op in self.cc_ops:
                op.tok_sem = st.enter_context(nc.semaphore("cc%d" % self.cc_ops.index(op)))
                op.tok_val = 1
            for e in ENGS:
                cnt = 0
                dcnt = {}
                for op in self.ops[e]:
                    if op.is_cc:
                        continue
                    if op.is_dma:
                        k = (e, op.dma_slot)
                        dcnt[k] = dcnt.get(k, 0) + 16
                        op.tok_sem = dsem[k]
                        op.tok_val = dcnt[k]
                    elif op.needs_inc:
                        cnt += 1
                        op.tok_sem = csem[e]
                        op.tok_val = cnt
            block = st.enter_context(nc.Block())

            def make(e):
                def body(eng):
                    waited = {}
                    for op in self.ops[e]:
                        best = {}
                        cand = []
                        if op.is_dma and op.prev_same_sem is not None:
                            p = op.prev_same_sem
                            cand.append((p.tok_sem, p.tok_val))
                        for d in op.deps:
                            if self._skip(op, d) or d.tok_sem is None:
                                continue
                            cand.append((d.tok_sem, d.tok_val))
                        for s, v in cand:
                            k = id(s)
                            if k not in best or best[k][1] < v:
                                best[k] = (s, v)
                        for k, (s, v) in best.items():
                            if waited.get(k, 0) >= v:
                                continue
                            waited[k] = v
                            eng.wait_ge(s, v)
                        if op.fn is None:
                            continue
                        inst = op.fn(eng)
                        if op.is_cc:
                            inst.then_inc(op.tok_sem)
                        elif op.is_dma:
                            inst.then_inc(op.tok_sem, 16)
                        elif op.needs_inc:
                            inst.then_inc(op.tok_sem, 1)
                    if e == "sp":
                        for op in final_wait_ops:
                            eng.wait_ge(op.tok_sem, op.tok_val)
                return body

            block.tensor(make("pe"))
            block.scalar(make("act"))
            block.vector(make("dve"))
            block.gpsimd(make("pool"))
            block.sync(make("sp"))


def tile_weight(w, cw=CW, rows_per_block=2048):
    K, N = w.shape
    rb = K // rows_per_block
    kc = rows_per_block // 128
    assert N % cw == 0
    x = w.reshape(rb, kc, 128, N // cw, cw)
    return np.ascontiguousarray(x.transpose(0, 3, 2, 1, 4))


def col_layout(v):
    return np.ascontiguousarray(v.reshape(-1, 128).T)


def feat_major(x2d):
    t, d = x2d.shape
    return np.ascontiguousarray(x2d.reshape(t, d // 128, 128).transpose(2, 1, 0))


def from_feat_major(y):
    p, c, t = y.shape
    return np.ascontiguousarray(y.transpose(2, 1, 0).reshape(t, c * p))


class Prog:
    def __init__(self):
        self.nc = bass.Bass("TRN2", target_bir_lowering=False)
        self.st = contextlib.ExitStack()
        self.S = Sched(self.nc)
        self.dram = {}
        self.nwb = 0
        self.bank_rr = 0
        self.pst = contextlib.ExitStack()
        self.phase_alloc = False

    def end_phase(self):
        self.S.barrier()
        self.pst.close()
        self.pst = contextlib.ExitStack()

    def din(self, name, shape, dt=F32):
        t = self.nc.dram_tensor(name, list(shape), dt, kind="ExternalInput").ap()
        self.dram[name] = t
        return t

    def dout(self, name, shape, dt=F32):
        t = self.nc.dram_tensor(name, list(shape), dt, kind="ExternalOutput").ap()
        self.dram[name] = t
        return t

    def dscratch(self, name, shape, dt=F32):
        t = self.nc.dram_tensor(name, list(shape), dt, kind="Internal").ap()
        self.dram[name] = t
        return t

    def sb(self, name, shape, dt=F32):
        st = self.pst if self.phase_alloc else self.st
        return st.enter_context(self.nc.sbuf_tensor(name, list(shape), dt))

    def ps(self, name, shape, dt=F32):
        st = self.pst if self.phase_alloc else self.st
        return st.enter_context(self.nc.psum_tensor(name, list(shape), dt))

    def setup_common(self):
        S = self.S
        self.consts_d = self.din("consts", [128, 4, 128])
        self.cst = self.sb("cst", [128, 4, 128])
        self.cst_bf = self.sb("cst_bf", [128, 4, 128], BF16)
        self.b_cst = Buf("cst")
        S.dma("sp", lambda e: e.dma_start(out=self.cst[:], in_=self.consts_d), writes=[self.b_cst])
        S.op("dve", lambda e: e.tensor_copy(out=self.cst_bf[:], in_=self.cst[:]), reads=[self.b_cst], writes=[self.b_cst])
        self.Lm = self.cst[:, 0, :]
        self.Um = self.cst[:, 1, :]
        self.ones = self.cst[:, 2, :]
        self.ident = self.cst[:, 3, :]
        self.ones_bf = self.cst_bf[:, 2, :]
        self.ident_bf = self.cst_bf[:, 3, :]
        self.NWB = 4
        self.wb = [self.sb("wb%d" % i, [128, 16, CW], BF16) for i in range(self.NWB)]
        self.b_wb = [Buf("wb%d" % i) for i in range(self.NWB)]
        self.mm_ps = [self.ps("mmps%d" % i, [128, 512]) for i in range(2)]
        self.b_mm = [Buf("mmps%d" % i) for i in range(2)]
        self.hT = self.sb("hT", [128, DC, TT])
        self.b_hT = Buf("hT")
        self.uT = self.sb("uT", [128, DC, TT], BF16)
        self.b_uT = Buf("uT")
        self.rstd = self.sb("rstd", [128, TT])
        self.b_rstd = Buf("rstd")
        self.ntmp = [self.sb("ntmp%d" % i, [128, TT]) for i in range(2)]
        self.b_ntmp = [Buf("ntmp%d" % i) for i in range(2)]
        self.yT = self.sb("big", [128, 32, TT], BF16)
        self.b_yT = Buf("yT")
        self.hid = self.yT[:, 0:16, :]
        self.b_hid = self.b_yT
        self.rl = [self.sb("rl%d" % i, [128, TT], BF16) for i in range(2)]
        self.b_rl = [Buf("rl%d" % i) for i in range(2)]

    def next_bank(self):
        i = self.bank_rr % 2
        self.bank_rr += 1
        return self.mm_ps[i], self.b_mm[i]

    def wtile(self, dram_tiles, idx):
        i = self.nwb % self.NWB
        self.nwb += 1
        wb, b = self.wb[i], self.b_wb[i]
        src = dram_tiles[idx] if isinstance(idx, int) else dram_tiles[idx[0], idx[1]]
        self.S.dma("pool", lambda e: e.dma_start(out=wb[:], in_=src, max_dma_last_dim=8192), writes=[b])
        return wb, b

    def adaln(self, w_ada_t, b_ada_col, c_col, norm_mix_col, norm_mlp_col, mod_out=None, mod_in=None, half_dram=None):
        S = self.S
        self.mod = self.sb("mod", [128, 2, 96])
        self.b_mod = Buf("mod")
        self.gs = self.sb("gs", [128, 2, 2, DC])
        self.b_gs = Buf("gs")
        nm = self.sb("nm", [128, 2, 2, DC])
        b_nm = Buf("nm")
        S.dma("sp", lambda e: e.dma_start(out=nm[:, :, 0, :], in_=norm_mix_col), writes=[b_nm])
        S.dma("sp", lambda e: e.dma_start(out=nm[:, :, 1, :], in_=norm_mlp_col), writes=[b_nm])
        if mod_in is not None:
            S.dma("sp", lambda e: e.dma_start(out=self.mod[:], in_=mod_in), writes=[self.b_mod])
        else:
            cc = self.sb("cc", [128, DC])
            cbf = self.sb("cbf", [128, DC], BF16)
            bada = self.sb("bada", [128, 2, 96])
            b_cc, b_bada = Buf("cc"), Buf("bada")
            S.dma("sp", lambda e: e.dma_start(out=cc[:], in_=c_col), writes=[b_cc])
            nb = 48 if half_dram is not None else 96
            S.dma("sp", lambda e: e.dma_start(out=bada[:, :, 0:nb], in_=b_ada_col), writes=[b_bada])
            S.op("act", lambda e: e.activation(out=cbf[:], in_=cc[:], func=AF.Silu), reads=[b_cc], writes=[b_cc])
            if half_dram is None:
                for i in range(2):
                    ps, bps = self.next_bank()
                    for ct in range(12288 // CW):
                        wt, bw = self.wtile(w_ada_t, (i, ct))
                        for jj in range(CW // 128):
                            col = ct * (CW // 128) + jj
                            for k in range(16):
                                S.op("pe", lambda e, ps=ps, wt=wt, jj=jj, k=k, col=col: e.matmul(
                                    ps[:, col:col + 1], lhsT=wt[:, k, jj * 128:(jj + 1) * 128], rhs=cbf[:, k:k + 1],
                                    start=(k == 0), stop=(k == 15)), reads=[bw, b_cc], writes=[bps])
                    S.op("dve", lambda e, ps=ps, i=i: e.tensor_tensor(out=self.mod[:, i, :], in0=ps[:, 0:96], in1=bada[:, i, :], op=ALU.add),
                         reads=[bps, b_bada], writes=[self.b_mod])
                if mod_out is not None:
                    self.mod_store = S.dma("act", lambda e: e.dma_start(out=mod_out, in_=self.mod[:]), reads=[self.b_mod], writes=[Buf()])
            else:
                m_own, m_g = half_dram
                mh = self.sb("modh", [128, 2, 48])
                b_mh = Buf("modh")
                for i in range(2):
                    ps, bps = self.next_bank()
                    for ct in range(24):
                        wt, bw = self.wtile(w_ada_t, (i, ct))
                        for jj in range(CW // 128):
                            col = ct * (CW // 128) + jj
                            for k in range(16):
                                S.op("pe", lambda e, ps=ps, wt=wt, jj=jj, k=k, col=col: e.matmul(
                                    ps[:, col:col + 1], lhsT=wt[:, k, jj * 128:(jj + 1) * 128], rhs=cbf[:, k:k + 1],
                                    start=(k == 0), stop=(k == 15)), reads=[bw, b_cc], writes=[bps])
                    S.op("dve", lambda e, ps=ps, i=i: e.tensor_tensor(out=mh[:, i, :], in0=ps[:, 0:48], in1=bada[:, i, 0:48], op=ALU.add),
                         reads=[bps, b_bada], writes=[b_mh])
                S.dma("act", lambda e: e.dma_start(out=m_own.rearrange("p (i c) -> p i c", i=2), in_=mh[:]), reads=[b_mh], writes=[Buf()])
                S.barrier()
                S.cc(lambda e: e.collective_compute("AllGather", ALU.bypass, replica_groups=PAIRS, ins=[m_own], outs=[m_g]))
                S.barrier()
                for r in range(2):
                    S.dma("sp", lambda e, r=r: e.dma_start(out=self.mod[:, :, 48 * r:48 * r + 48],
                                                           in_=m_g[128 * r:128 * r + 128, :].rearrange("p (i c) -> p i c", i=2)), writes=[self.b_mod])
        for i in range(2):
            for j, off in ((0, 16), (1, 64)):
                S.op("dve", lambda e, i=i, j=j, off=off: e.scalar_tensor_tensor(
                    out=self.gs[:, i, j, :], in0=self.mod[:, i, off:off + 16], scalar=1.0, in1=nm[:, i, j, :],
                    op0=ALU.add, op1=ALU.mult), reads=[self.b_mod, b_nm], writes=[self.b_gs])

    def modv(self, layer, which, c):
        off = {"sh_a": 0, "g_a": 32, "sh_f": 48, "g_f": 80}[which]
        return self.mod[:, layer, off + c:off + c + 1]

    def norm_mod(self, layer, j, ntok=TT, gs_ap=None, out_f32=None):
        S = self.S
        n = ntok
        if not hasattr(self, "b_sq"):
            self.b_sq = [Buf("sq%d" % q) for q in range(4)]
        for q in range(4):
            S.op("act", lambda e, q=q: e.activation(out=self.uT[:, 4 * q:4 * q + 4, 0:n], in_=self.hT[:, 4 * q:4 * q + 4, 0:n], func=AF.Square),
                 reads=[self.b_hT], writes=[self.b_uT, self.b_sq[q]])
        ps, bps = self.next_bank()
        for c in range(DC):
            S.op("pe", lambda e, c=c, ps=ps: e.matmul(ps[:, 0:n], lhsT=self.ones_bf, rhs=self.uT[:, c, 0:n],
                                                       start=(c == 0), stop=(c == DC - 1)),
                 reads=[self.b_sq[c // 4], self.b_cst], writes=[bps])
        S.op("act", lambda e, ps=ps: e.activation(out=self.rstd[:, 0:n], in_=ps[:, 0:n], func=AF.Sqrt, bias=EPS, scale=1.0 / D),
             reads=[bps], writes=[self.b_rstd])
        S.op("dve", lambda e: e.reciprocal(out=self.rstd[:, 0:n], in_=self.rstd[:, 0:n]), reads=[self.b_rstd], writes=[self.b_rstd])
        for c in range(DC):
            if out_f32 is not None:
                S.op("dve", lambda e, c=c: e.scalar_tensor_tensor(
                    out=out_f32[:, c, 0:n], in0=self.hT[:, c, 0:n], scalar=gs_ap[:, c:c + 1], in1=self.rstd[:, 0:n],
                    op0=ALU.mult, op1=ALU.mult), reads=[self.b_hT, self.b_rstd, self.b_gs], writes=[self.b_outf])
                continue
            tmp, bt = self.ntmp[c % 2], self.b_ntmp[c % 2]
            S.op("dve", lambda e, c=c, tmp=tmp: e.scalar_tensor_tensor(
                out=tmp[:, 0:n], in0=self.hT[:, c, 0:n], scalar=self.gs[:, layer, j, c:c + 1], in1=self.rstd[:, 0:n],
                op0=ALU.mult, op1=ALU.mult), reads=[self.b_hT, self.b_rstd, self.b_gs], writes=[bt])
            sh = self.modv(layer, "sh_a" if j == 0 else "sh_f", c)
            S.op("act", lambda e, c=c, tmp=tmp, sh=sh: e.activation(out=self.uT[:, c, 0:n], in_=tmp[:, 0:n], func=AF.Identity, bias=sh, scale=1.0),
                 reads=[bt, self.b_mod], writes=[self.b_uT, self.b_sq[c // 4]])

    def mlp(self, layer, w_up_t, w_down_t, nfb=DFF // 2048, do_down=True, wl=None):
        S = self.S
        self.norm_mod(layer, 1)
        npc = CW // 128
        wl = layer if wl is None else wl
        for fb in range(nfb):
            for ct in range(2048 // CW):
                wt, bw = self.wtile(w_up_t, (wl, fb * (2048 // CW) + ct))
                for jj in range(npc):
                    jf = ct * npc + jj
                    ps, bps = self.next_bank()
                    for k in range(DC):
                        S.op("pe", lambda e, ps=ps, wt=wt, jj=jj, k=k: e.matmul(
                            ps[:], lhsT=wt[:, k, jj * 128:(jj + 1) * 128], rhs=self.uT[:, k, :], start=(k == 0), stop=(k == DC - 1)),
                            reads=[bw, self.b_uT], writes=[bps])
                    rl, brl = self.rl[jf % 2], self.b_rl[jf % 2]
                    S.op("act", lambda e, ps=ps, rl=rl: e.activation(out=rl[:], in_=ps[:], func=AF.Relu), reads=[bps], writes=[brl])
                    S.op("dve", lambda e, rl=rl, jf=jf: e.tensor_tensor(out=self.hid[:, jf, :], in0=rl[:], in1=rl[:], op=ALU.mult),
                         reads=[brl], writes=[self.b_hid])
            for ct in range(D // CW if do_down else 0):
                wt, bw = self.wtile(w_down_t, (wl * (DFF // 2048) + fb, ct))
                for jj in range(npc):
                    d = ct * npc + jj
                    ps, bps = self.next_bank()
                    for k in range(16):
                        S.op("pe", lambda e, ps=ps, wt=wt, jj=jj, k=k: e.matmul(
                            ps[:], lhsT=wt[:, k, jj * 128:(jj + 1) * 128], rhs=self.hid[:, k, :], start=(k == 0), stop=(k == 15)),
                            reads=[bw, self.b_hid], writes=[bps])
                    g = self.modv(layer, "g_f", d)
                    S.op("dve", lambda e, ps=ps, d=d, g=g: e.scalar_tensor_tensor(
                        out=self.hT[:, d, :], in0=ps[:], scalar=g, in1=self.hT[:, d, :], op0=ALU.mult, op1=ALU.add),
                        reads=[bps, self.b_mod, self.b_hT], writes=[self.b_hT])

    def finish(self, final_ops):
        self.S.emit(final_wait_ops=final_ops)
        self.pst.close()
        self.st.close()
        return self.nc


def make_consts():
    k = np.arange(128)
    L = (k[:, None] <= k[None, :]).astype(np.float32)
    U = (k[:, None] > k[None, :]).astype(np.float32)
    ones = np.ones((128, 128), np.float32)
    ident = np.eye(128, dtype=np.float32)
    return np.ascontiguousarray(np.stack([L, U, ones, ident], axis=1))


def ssd_setup(P, d):
    S = P.S
    P.seg_ps = P.ps("segps", [128, 512]); P.b_seg = Buf("seg")
    P.tp_ps = P.ps("tpps", [128, 1024], BF16); P.b_tp = Buf("tp")
    P.yd_ps = P.ps("ydps", [128, 512]); P.b_yd = Buf("yd")
    P.yo_ps = P.ps("yops", [128, 512]); P.b_yo = Buf("yo")
    P.st_ps = P.ps("stps", [128, 512]); P.b_st = Buf("st")
    P.misc_ps = P.ps("miscps", [128, 512])
    P.b_sc = Buf("misc"); P.b_sm = [P.b_sc] * 4
    P.zs = P.sb("zs", [128, 4, 512], BF16); P.b_zs = Buf("zs")
    P.xtm = P.sb("xtm", [128, 4, 512], BF16); P.b_xtm = Buf("xtm")
    P.BT = P.sb("BT", [128, 512], BF16); P.b_BT = Buf("BT")
    P.CT = P.sb("CT", [128, 512], BF16); P.b_CT = Buf("CT")
    P.Btm = P.sb("Btm", [128, 4, 128], BF16); P.b_Btm = Buf("Btm")
    P.raw = [P.sb("raw%d" % i, [128, 515]) for i in range(2)]; P.b_raw = [Buf("raw%d" % i) for i in range(2)]
    P.acc = [P.sb("acc%d" % i, [128, 512]) for i in range(2)]; P.b_acc = [Buf("acc%d" % i) for i in range(2)]
    P.xsT = [P.sb("xsT%d" % i, [128, 512], BF16) for i in range(2)]; P.b_xsT = [Buf("xsT%d" % i) for i in range(2)]
    P.tnh = P.rl; P.b_tnh = P.b_rl
    P.nconv = 0
    P.pending_tp = None
    P.halo = P.sb("halo", [128, 48, 3]); P.b_halo = [Buf("halo%d" % i) for i in range(48)]
    P.cw = P.sb("cwc", [128, 48, 4]); P.cb = P.sb("cbc", [128, 48]); P.b_cw = Buf("cw")
    P.dtb = P.sb("dtb", [128, 64]); P.Abc = P.sb("Abc", [128, 64]); P.Dbc = P.sb("Dbc", [128, 64]); P.b_prm = Buf("prm")
    P.wdt = P.sb("wdt", [128, 16, 64], BF16); P.b_wdt = Buf("wdt")
    P.dt = P.sb("dt", [128, 4, 64]); P.dtA = P.sb("dtA", [128, 4, 64]); P.ea = P.sb("ea", [128, 4, 64])
    P.dte = P.sb("dte", [128, 4, 64]); P.cd = P.sb("cd", [128, 4, 64])
    P.b_dt = Buf("dt"); P.b_dtA = Buf("dtA"); P.b_ea = Buf("ea"); P.b_dte = Buf("dte"); P.b_cd = Buf("cd")
    P.sp1 = P.sb("sp1", [128, 64]); P.sp2 = P.sb("sp2", [128, 64]); P.sp3 = P.sb("sp3", [128, 64]); P.b_sp = Buf("sp")
    P.nh = 64
    P.R = [P.sb("R%d" % i, [128, 4, 128]) for i in range(2)]; P.b_R = [Buf("R%d" % i) for i in range(2)]
    P.dec = P.sb("dec", [128, 4, 8, 128], BF16); P.b_dec = [Buf("dec%d" % i) for i in range(4)]
    P.sm = P.sb("smk", [128, 128], BF16); P.b_smk = Buf("smk")
    P.Mh = [P.sb("Mh%d" % i, [128, 8, 128], BF16) for i in range(2)]; P.b_Mh = [Buf("Mh%d" % i) for i in range(2)]
    P.xdt = [P.sb("xdt%d" % i, [128, 512], BF16) for i in range(2)]; P.b_xdt = [Buf("xdt%d" % i) for i in range(2)]
    P.xdte = [P.sb("xdte%d" % i, [128, 512], BF16) for i in range(2)]; P.b_xdte = [Buf("xdte%d" % i) for i in range(2)]
    P.y1 = P.sb("y1", [128, 512]); P.y2 = P.sb("y2", [128, 512]); P.b_y1 = Buf("y1"); P.b_y2 = Buf("y2")
    P.yn = P.sb("yn", [128, 512], BF16); P.b_yn = Buf("yn")
    P.xD = P.sb("xD", [128, 512], BF16); P.b_xD = Buf("xD")
    P.ss = P.sb("ss", [128, 2]); P.b_ss = Buf("ss")
    P.Sst = P.sb("Sst", [128, 8, 512]); P.b_S = [Buf("S%d" % g) for g in range(8)]
    P.Sbf = P.sb("Sbf", [128, 512], BF16); P.b_Sbf = Buf("Sbf")
    P.gn = P.sb("gn", [128, 32]); P.b_gn = Buf("gn")
    P.flag = P.sb("flag_sb", [128, 1]); P.b_flag = Buf("flag")
    S.dma("sp", lambda e: e.dma_start(out=P.cw[:], in_=d["convw_col"]), writes=[P.b_cw])
    S.dma("sp", lambda e: e.dma_start(out=P.cb[:], in_=d["convb_col"]), writes=[P.b_cw])
    S.dma("sp", lambda e: e.dma_start(out=P.dtb[:], in_=d["dtb_bc"]), writes=[P.b_prm])
    S.dma("sp", lambda e: e.dma_start(out=P.Abc[:], in_=d["alog_bc"]), writes=[P.b_prm])
    S.dma("sp", lambda e: e.dma_start(out=P.Dbc[:], in_=d["D_bc"]), writes=[P.b_prm])
    S.dma("sp", lambda e: e.dma_start(out=P.flag[:], in_=d["flag"]), writes=[P.b_flag])
    S.dma("pool", lambda e: e.dma_start(out=P.wdt[:], in_=d["w_dt"]), writes=[P.b_wdt])
    S.dma("sp", lambda e: e.dma_start(out=P.gn[:], in_=d["gn_col"]), writes=[P.b_gn])
    S.op("dve", lambda e: e.tensor_scalar(out=P.cw[:], in0=P.cw[:], scalar1=0.5, scalar2=None, op0=ALU.mult), reads=[P.b_cw], writes=[P.b_cw])
    S.op("dve", lambda e: e.tensor_scalar(out=P.cb[:], in0=P.cb[:], scalar1=0.5, scalar2=None, op0=ALU.mult), reads=[P.b_cw], writes=[P.b_cw])
    S.op("dve", lambda e: e.tensor_scalar(out=P.gn[:], in0=P.gn[:], scalar1=0.5, scalar2=None, op0=ALU.mult), reads=[P.b_gn], writes=[P.b_gn])
    S.op("act", lambda e: e.activation(out=P.Abc[:], in_=P.Abc[:], func=AF.Exp), reads=[P.b_prm], writes=[P.b_prm])
    S.op("dve", lambda e: e.tensor_scalar(out=P.Abc[:], in0=P.Abc[:], scalar1=-1.0, scalar2=None, op0=ALU.mult),
         reads=[P.b_prm], writes=[P.b_prm])


def ssd_init_state(P, S_in, halo_in):
    S = P.S
    bs = P.b_S
    S.dma("sp", lambda e: e.dma_start(out=P.Sst[:].rearrange("p g c -> p (g c)"), in_=S_in), writes=bs)
    S.dma("sp", lambda e: e.dma_start(out=P.halo[:], in_=halo_in), writes=P.b_halo)
    S.op("dve", lambda e: e.tensor_scalar(out=P.Sst[:], in0=P.Sst[:], scalar1=P.flag[:, 0:1], scalar2=None, op0=ALU.mult),
         reads=bs + [P.b_flag], writes=bs)
    S.op("dve", lambda e: e.tensor_scalar(out=P.halo[:], in0=P.halo[:], scalar1=P.flag[:, 0:1], scalar2=None, op0=ALU.mult),
         reads=P.b_halo + [P.b_flag], writes=P.b_halo)


def ssd_dt(P):
    S = P.S
    mp = P.misc_ps
    for sub in range(4):
        nh = P.nh
        ps = mp[:, 320:320 + nh]; bps = P.b_sm[3]
        for k in range(DC):
            S.op("pe", lambda e, ps=ps, k=k, sub=sub: e.matmul(ps, lhsT=P.uT[:, k, sub * 128:(sub + 1) * 128], rhs=P.wdt[:, k, :],
                                                           start=(k == 0), stop=(k == DC - 1)),
                 reads=[P.b_uT, P.b_wdt], writes=[bps])
        S.op("dve", lambda e, ps=ps: e.tensor_tensor(out=P.sp1[:], in0=ps, in1=P.dtb[:], op=ALU.add), reads=[bps, P.b_prm], writes=[P.b_sp])
        S.op("dve", lambda e: e.scalar_tensor_tensor(out=P.sp2[:], in0=P.sp1[:], scalar=-1.0, in1=P.sp1[:], op0=ALU.mult, op1=ALU.min),
             reads=[P.b_sp], writes=[P.b_sp])
        S.op("act", lambda e: e.activation(out=P.sp2[:], in_=P.sp2[:], func=AF.Exp), reads=[P.b_sp], writes=[P.b_sp])
        S.op("act", lambda e: e.activation(out=P.sp3[:], in_=P.sp2[:], func=AF.Ln, bias=1.0, scale=1.0), reads=[P.b_sp], writes=[P.b_sp])
        S.op("dve", lambda e, sub=sub: e.scalar_tensor_tensor(out=P.dt[:, sub, :], in0=P.sp1[:], scalar=0.0, in1=P.sp3[:], op0=ALU.max, op1=ALU.add),
             reads=[P.b_sp], writes=[P.b_dt])
        S.op("dve", lambda e, sub=sub: e.tensor_tensor(out=P.dtA[:, sub, :], in0=P.dt[:, sub, :], in1=P.Abc[:], op=ALU.mult),
             reads=[P.b_dt, P.b_prm], writes=[P.b_dtA])
        for i, (lhs, dst, bd) in enumerate(((P.Lm, P.ea, P.b_ea), (P.Um, P.dte, P.b_dte), (P.ones, P.cd, P.b_cd))):
            pss = mp[:, 128 + 64 * i:128 + 64 * i + nh]; bp = P.b_sm[i]
            S.op("pe", lambda e, pss=pss, lhs=lhs, sub=sub: e.matmul(pss, lhsT=lhs, rhs=P.dtA[:, sub, :], start=True, stop=True),
                 reads=[P.b_dtA, P.b_cst], writes=[bp])
            S.op("act", lambda e, pss=pss, dst=dst, sub=sub: e.activation(out=dst[:, sub, :], in_=pss, func=AF.Exp), reads=[bp], writes=[bd])


def ssd_conv_front(P, ps, bps, ci):
    S = P.S
    bh = P.b_halo[ci]
    i = P.nconv % 2
    P.nconv += 1
    raw, braw, acc, bacc = P.raw[i], P.b_raw[i], P.acc[i], P.b_acc[i]
    S.op("act", lambda e: e.activation(out=raw[:, 3:515], in_=ps[:], func=AF.Copy), reads=[bps], writes=[braw])
    S.op("act", lambda e: e.activation(out=acc[:], in_=ps[:], func=AF.Identity, bias=P.cb[:, ci:ci + 1], scale=P.cw[:, ci, 3:4]),
         reads=[bps, P.b_cw], writes=[bacc])
    S.op("act", lambda e: e.activation(out=raw[:, 0:3], in_=P.halo[:, ci, :], func=AF.Copy), reads=[bh], writes=[braw])
    return i


def ssd_conv_back(P, i, ci, out_ap, b_out):
    S = P.S
    bh = P.b_halo[ci]
    raw, braw, acc, bacc, tnh, btnh = P.raw[i], P.b_raw[i], P.acc[i], P.b_acc[i], P.tnh[i], P.b_tnh[i]
    for j in (2, 1, 0):
        S.op("dve", lambda e, j=j: e.scalar_tensor_tensor(out=acc[:], in0=raw[:, j:j + 512], scalar=P.cw[:, ci, j:j + 1], in1=acc[:],
                                                          op0=ALU.mult, op1=ALU.add), reads=[braw, bacc, P.b_cw], writes=[bacc])
    S.op("act", lambda e: e.activation(out=P.halo[:, ci, :], in_=raw[:, 512:515], func=AF.Copy), reads=[braw], writes=[bh])
    S.op("act", lambda e: e.activation(out=tnh[:], in_=acc[:], func=AF.Tanh), reads=[bacc], writes=[btnh])
    S.op("dve", lambda e: e.scalar_tensor_tensor(out=out_ap, in0=tnh[:], scalar=1.0, in1=acc[:], op0=ALU.add, op1=ALU.mult),
         reads=[btnh, bacc], writes=[b_out])


def ssd_transpose4(P, src, b_src, dst_fn, b_dst):
    S = P.S
    for sub in range(4):
        S.op("pe", lambda e, sub=sub: e.transpose(out=P.tp_ps[:, sub * 128:(sub + 1) * 128], in_=src[:, sub * 128:(sub + 1) * 128],
                                                   identity=P.ident_bf), reads=[b_src, P.b_cst], writes=[P.b_tp])
    for sub in range(4):
        S.op("act", lambda e, sub=sub: e.activation(out=dst_fn(sub), in_=P.tp_ps[:, sub * 128:(sub + 1) * 128], func=AF.Copy),
             reads=[P.b_tp], writes=[b_dst])


def ssd_dec_steps(P, g):
    S = P.S
    steps = []
    for c in range(4):
        for hh in range(2):
            def step(c=c, hh=hh):
                i = (c * 2 + hh) % 2
                R, bR = P.R[i], P.b_R[i]
                h4 = slice(g * 8 + hh * 4, g * 8 + hh * 4 + 4)
                S.op("dve", lambda e: e.tensor_tensor(out=R[:], in0=P.dtA[:, c, h4].unsqueeze(2).to_broadcast([128, 4, 128]),
                                                      in1=P.Lm.unsqueeze(1).to_broadcast([128, 4, 128]), op=ALU.mult),
                     reads=[P.b_dtA, P.b_cst], writes=[bR])
                S.op("pe", lambda e: e.matmul(P.seg_ps[:], lhsT=P.Um, rhs=R[:].rearrange("p r t -> p (r t)"), start=True, stop=True),
                     reads=[bR, P.b_cst], writes=[P.b_seg])
                S.op("act", lambda e: e.activation(out=P.dec[:, c, hh * 4:(hh + 1) * 4, :], in_=P.seg_ps[:].rearrange("p (r t) -> p r t", r=4), func=AF.Exp),
                     reads=[P.b_seg], writes=[P.b_dec[c]])
            steps.append(step)
    return steps


def ssd_flush_tp(P):
    if P.pending_tp is not None:
        f = P.pending_tp
        P.pending_tp = None
        f()


def ssd_group(P, g, w_in_g, pre):
    S = P.S
    mp = P.misc_ps
    steps = [] if pre else ssd_dec_steps(P, g)

    def step():
        if steps:
            steps.pop(0)()
    if not pre:
        wts = [P.wtile(w_in_g, (g, ct)) for ct in range(2)]
        for sub in range(4):
            ps, bps = P.next_bank()
            for ct in range(2):
                wt, bw = wts[ct]
                for k in range(DC):
                    S.op("pe", lambda e, ps=ps, wt=wt, ct=ct, k=k, sub=sub: e.matmul(
                        ps[:, ct * 256:(ct + 1) * 256], lhsT=P.uT[:, k, sub * 128:(sub + 1) * 128], rhs=wt[:, k, :],
                        start=(k == 0), stop=(k == DC - 1)), reads=[bw, P.b_uT], writes=[bps])
            tnh, btnh = P.tnh[sub % 2], P.b_tnh[sub % 2]
            S.op("act", lambda e, ps=ps, tnh=tnh: e.activation(out=tnh[:], in_=ps[:], func=AF.Tanh, scale=0.5), reads=[bps], writes=[btnh])
            S.op("dve", lambda e, ps=ps, sub=sub, tnh=tnh: e.scalar_tensor_tensor(out=P.zs[:, sub, :], in0=tnh[:], scalar=1.0, in1=ps[:], op0=ALU.add, op1=ALU.mult),
                 reads=[btnh, bps], writes=[P.b_zs])
            step()
    for ct in range(2, 5):
        wt, bw = P.wtile(w_in_g, (g, ct))
        for jj in range(2):
            ci = g * 6 + (ct - 2) * 2 + jj
            ps, bps = P.next_bank()
            for k in range(DC):
                S.op("pe", lambda e, ps=ps, wt=wt, jj=jj, k=k: e.matmul(
                    ps[:], lhsT=wt[:, k, jj * 128:(jj + 1) * 128], rhs=P.uT[:, k, :], start=(k == 0), stop=(k == DC - 1)),
                    reads=[bw, P.b_uT], writes=[bps])
            i = ssd_conv_front(P, ps, bps, ci)
            ssd_flush_tp(P)
            step()
            if ct < 4:
                xc = (ct - 2) * 2 + jj
                xs, bxs = P.xsT[i], P.b_xsT[i]
                ssd_conv_back(P, i, ci, xs[:], bxs)
                P.pending_tp = (lambda xs=xs, bxs=bxs, xc=xc: ssd_transpose4(
                    P, xs, bxs, lambda sub: P.xtm[:, sub, xc * 128:(xc + 1) * 128], P.b_xtm))
            elif jj == 0:
                ssd_conv_back(P, i, ci, P.BT[:], P.b_BT)
                P.pending_tp = (lambda: ssd_transpose4(P, P.BT, P.b_BT, lambda sub: P.Btm[:, sub, :], P.b_Btm))
            else:
                ssd_conv_back(P, i, ci, P.CT[:], P.b_CT)
    ssd_flush_tp(P)
    while steps:
        step()
    bS = P.b_S[g]
    Sg = P.Sst[:, g, :]
    hs = slice(g * 8, (g + 1) * 8)

    def front(c):
        i = c % 2
        cs = slice(c * 128, (c + 1) * 128)
        xdt, bxdt, xdte, bxdte = P.xdt[i], P.b_xdt[i], P.xdte[i], P.b_xdte[i]
        S.op("dve", lambda e: e.tensor_tensor(out=xdt[:].rearrange("p (r q) -> p r q", r=8), in0=P.xtm[:, c, :].rearrange("p (r q) -> p r q", r=8),
                                              in1=P.dt[:, c, hs].unsqueeze(2).to_broadcast([128, 8, 64]), op=ALU.mult),
             reads=[P.b_xtm, P.b_dt], writes=[bxdt])
        S.op("dve", lambda e: e.tensor_tensor(out=xdte[:].rearrange("p (r q) -> p r q", r=8), in0=xdt[:].rearrange("p (r q) -> p r q", r=8),
                                              in1=P.dte[:, c, hs].unsqueeze(2).to_broadcast([128, 8, 64]), op=ALU.mult),
             reads=[bxdt, P.b_dte], writes=[bxdte])
        if not pre:
            sc = mp[:, 0:128]
            Mh, bMh = P.Mh[i], P.b_Mh[i]
            S.op("pe", lambda e: e.matmul(sc, lhsT=P.BT[:, cs], rhs=P.CT[:, cs], start=True, stop=True),
                 reads=[P.b_BT, P.b_CT], writes=[P.b_sc])
            S.op("dve", lambda e: e.tensor_tensor(out=P.sm[:], in0=sc, in1=P.Lm, op=ALU.mult), reads=[P.b_sc, P.b_cst], writes=[P.b_smk])
            S.op("dve", lambda e: e.tensor_tensor(out=Mh[:], in0=P.dec[:, c, :, :],
                                                  in1=P.sm[:].unsqueeze(1).to_broadcast([128, 8, 128]), op=ALU.mult),
                 reads=[P.b_dec[c], P.b_smk], writes=[bMh])

    def back(c):
        i = c % 2
        cs = slice(c * 128, (c + 1) * 128)
        xdt, bxdt, xdte, bxdte = P.xdt[i], P.b_xdt[i], P.xdte[i], P.b_xdte[i]
        if not pre:
            Mh, bMh = P.Mh[i], P.b_Mh[i]
            S.op("act", lambda e: e.activation(out=P.Sbf[:], in_=Sg, func=AF.Copy), reads=[bS], writes=[P.b_Sbf])
            S.op("dve", lambda e: e.tensor_tensor(out=P.xD[:].rearrange("p (r q) -> p r q", r=8), in0=P.xtm[:, c, :].rearrange("p (r q) -> p r q", r=8),
                                                  in1=P.Dbc[:, hs].unsqueeze(2).to_broadcast([128, 8, 64]), op=ALU.mult),
                 reads=[P.b_xtm, P.b_prm], writes=[P.b_xD])
            S.op("pe", lambda e: e.matmul(P.yd_ps[:], lhsT=P.ident_bf, rhs=P.xD[:], start=True, stop=False),
                 reads=[P.b_xD, P.b_cst], writes=[P.b_yd])
            for r in range(8):
                S.op("pe", lambda e, r=r: e.matmul(P.yd_ps[:, r * 64:(r + 1) * 64], lhsT=Mh[:, r, :], rhs=xdt[:, r * 64:(r + 1) * 64],
                                                   start=False, stop=(r == 7)), reads=[bMh, bxdt], writes=[P.b_yd])
            S.op("pe", lambda e: e.matmul(P.yo_ps[:], lhsT=P.CT[:, cs], rhs=P.Sbf[:], start=True, stop=True),
                 reads=[P.b_CT, P.b_Sbf], writes=[P.b_yo])
        S.op("pe", lambda e: e.matmul(P.st_ps[:], lhsT=P.Btm[:, c, :], rhs=xdte[:], start=True, stop=True),
             reads=[P.b_Btm, bxdte], writes=[P.b_st])
        if not pre:
            S.op("dve", lambda e: e.tensor_tensor(out=P.y1[:].rearrange("p (r q) -> p r q", r=8), in0=P.yo_ps[:].rearrange("p (r q) -> p r q", r=8),
                                                  in1=P.ea[:, c, hs].unsqueeze(2).to_broadcast([128, 8, 64]), op=ALU.mult),
                 reads=[P.b_yo, P.b_ea], writes=[P.b_y1])
            S.op("dve", lambda e: e.tensor_tensor(out=P.y1[:], in0=P.yd_ps[:], in1=P.y1[:], op=ALU.add), reads=[P.b_yd, P.b_y1], writes=[P.b_y1])
        S.op("dve", lambda e: e.tensor_tensor(out=Sg.rearrange("p (r q) -> p r q", r=8), in0=Sg.rearrange("p (r q) -> p r q", r=8),
                                              in1=P.cd[:, c, hs].unsqueeze(2).to_broadcast([128, 8, 64]), op=ALU.mult),
             reads=[bS, P.b_cd], writes=[bS])
        S.op("dve", lambda e: e.tensor_tensor(out=Sg, in0=P.st_ps[:], in1=Sg, op=ALU.add), reads=[P.b_st, bS], writes=[bS])
        if not pre:
            S.op("dve", lambda e: e.tensor_tensor(out=P.y1[:], in0=P.y1[:], in1=P.zs[:, c, :], op=ALU.mult), reads=[P.b_y1, P.b_zs], writes=[P.b_y1])
            S.op("act", lambda e: e.activation(out=P.y2[:], in_=P.y1[:], func=AF.Square, accum_out=P.ss[:, 0:1]), reads=[P.b_y1, P.b_y2], writes=[P.b_y2, P.b_ss])
            S.op("act", lambda e: e.activation(out=P.ss[:, 1:2], in_=P.ss[:, 0:1], func=AF.Sqrt, bias=EPS, scale=1.0 / 2048), reads=[P.b_ss], writes=[P.b_ss])
            S.op("dve", lambda e: e.reciprocal(out=P.ss[:, 1:2], in_=P.ss[:, 1:2]), reads=[P.b_ss], writes=[P.b_ss])
            S.op("act", lambda e: e.activation(out=P.yn[:], in_=P.y1[:], func=AF.Identity, scale=P.ss[:, 1:2]), reads=[P.b_y1, P.b_ss], writes=[P.b_yn])
            for j in range(4):
                S.op("pe", lambda e, j=j: e.transpose(out=P.tp_ps[:, 512 + j * 128:512 + (j + 1) * 128], in_=P.yn[:, j * 128:(j + 1) * 128],
                                                       identity=P.ident_bf), reads=[P.b_yn, P.b_cst], writes=[P.b_tp])
            for j in range(4):
                kk = g * 4 + j
                S.op("act", lambda e, j=j, kk=kk: e.activation(out=P.yT[:, kk, c * 128:(c + 1) * 128], in_=P.tp_ps[:, 512 + j * 128:512 + (j + 1) * 128],
                                                            func=AF.Identity, scale=P.gn[:, kk:kk + 1]), reads=[P.b_tp, P.b_gn], writes=[P.b_yT])

    front(0)
    for c in range(4):
        if c + 1 < 4:
            front(c + 1)
        back(c)


def ssd_outproj(P, w_out_t, x_reload):
    S = P.S
    S.dma("sp", lambda e: e.dma_start(out=P.hT[:], in_=x_reload), writes=[P.b_hT])
    for ct in range(D // CW):
        wts = [P.wtile(w_out_t, (rb, ct)) for rb in range(2)]
        for jj in range(CW // 128):
            dch = ct * (CW // 128) + jj
            ps, bps = P.next_bank()
            for rb in range(2):
                wt, bw = wts[rb]
                for k in range(16):
                    kk = rb * 16 + k
                    S.op("pe", lambda e, ps=ps, wt=wt, jj=jj, k=k, kk=kk: e.matmul(
                        ps[:], lhsT=wt[:, k, jj * 128:(jj + 1) * 128], rhs=P.yT[:, kk, :], start=(kk == 0), stop=(kk == 31)),
                        reads=[bw, P.b_yT], writes=[bps])
            ga = P.modv(0, "g_a", dch)
            S.op("dve", lambda e, ps=ps, dch=dch, ga=ga: e.scalar_tensor_tensor(
                out=P.hT[:, dch, :], in0=ps[:], scalar=ga, in1=P.hT[:, dch, :], op0=ALU.mult, op1=ALU.add),
                reads=[bps, P.b_mod, P.b_hT], writes=[P.b_hT])


NEG = 30000.0


def fox_in_setup(P, d):
    S = P.S
    P.wf = P.sb("wf", [128, 16, 16], BF16); P.b_wf = Buf("wf")
    P.bfb = P.sb("bfb", [128, 16]); P.b_bfb = Buf("bfb")
    P.stg = P.rl; P.b_stg = P.b_rl
    P.nstg = 0
    P.fx = P.sb("fx", [128, 16]); P.fa = P.sb("fa", [128, 16]); P.fl = P.sb("fl", [128, 16]); P.logf = P.sb("logf", [128, 16]); P.b_f = Buf("f")
    P.ck_sb = P.sb("ck_sb", [128, 4, 16]); P.b_ck = Buf("ck")
    P.cqT = P.sb("cqT", [16, 512]); P.b_cqT = Buf("cqT")
    P.carry = P.sb("carry", [128, 16]); P.carryT = P.sb("carryT", [16, 1]); P.b_carry = Buf("carry")
    S.dma("pool", lambda e: e.dma_start(out=P.wf[:], in_=d["w_f"]), writes=[P.b_wf])
    S.dma("sp", lambda e: e.dma_start(out=P.bfb[:], in_=d["bf_bc"]), writes=[P.b_bfb])
    S.op("dve", lambda e: e.memset(P.carry[:], 0.0), writes=[P.b_carry])
    S.op("dve", lambda e: e.memset(P.carryT[:], 0.0), writes=[P.b_carry])


def fox_inproj(P, tt, d):
    S = P.S
    mp = P.misc_ps
    tsl = slice(tt * TT, (tt + 1) * TT)
    P.h1_store = S.dma("act", lambda e: e.dma_start(out=d["h1T"][:, :, tsl], in_=P.hT[:]), reads=[P.b_hT], writes=[d["b_h1T"]])
    P.norm_mod(1, 0)
    w = d["fox_w_in_t"]
    for ct in range(16):
        wt, bw = P.wtile(w, (0, ct))
        for jj in range(2):
            hq = (ct % 8) * 2 + jj
            ps, bps = P.next_bank()
            for k in range(DC):
                S.op("pe", lambda e, ps=ps, wt=wt, jj=jj, k=k: e.matmul(
                    ps[:], lhsT=wt[:, k, jj * 128:(jj + 1) * 128], rhs=P.uT[:, k, :], start=(k == 0), stop=(k == DC - 1)),
                    reads=[bw, P.b_uT], writes=[bps])
            i = P.nstg % 2; P.nstg += 1
            stg, bst = P.stg[i], P.b_stg[i]
            if ct < 8:
                S.op("act", lambda e, ps=ps, stg=stg: e.activation(out=stg[:], in_=ps[:], func=AF.Copy, scale=float(FD) ** -0.5), reads=[bps], writes=[bst])
                dst, bd = d["qT_d"][hq, :, tsl], d["b_q"]
            else:
                S.op("act", lambda e, ps=ps, stg=stg: e.activation(out=stg[:], in_=ps[:], func=AF.Copy), reads=[bps], writes=[bst])
                dst, bd = d["kT_own"][hq, :, tsl], d["b_k"]
            S.dma("act", lambda e, dst=dst, stg=stg: e.dma_start(out=dst, in_=stg[:]), reads=[bst], writes=[bd])
    for pair in range(4):
        wts = [P.wtile(w, (0, 16 + pair * 2 + c2)) for c2 in range(2)]
        for sub in range(4):
            ps, bps = P.next_bank()
            for c2 in range(2):
                wt, bw = wts[c2]
                for k in range(DC):
                    S.op("pe", lambda e, ps=ps, wt=wt, c2=c2, k=k, sub=sub: e.matmul(
                        ps[:, c2 * 256:(c2 + 1) * 256], lhsT=P.uT[:, k, sub * 128:(sub + 1) * 128], rhs=wt[:, k, :],
                        start=(k == 0), stop=(k == DC - 1)), reads=[bw, P.b_uT], writes=[bps])
            i = P.nstg % 2; P.nstg += 1
            stg, bst = P.stg[i], P.b_stg[i]
            S.op("act", lambda e, ps=ps, stg=stg: e.activation(out=stg[:], in_=ps[:], func=AF.Copy), reads=[bps], writes=[bst])
            kt = tt * 4 + sub
            dst = d["v_own"][pair * 4:(pair + 1) * 4, :, kt, :].rearrange("h p d -> p h d")
            S.dma("act", lambda e, dst=dst, stg=stg: e.dma_start(out=dst, in_=stg[:].rearrange("p (h d) -> p h d", h=4)),
                  reads=[bst], writes=[d["b_v"]])
    bm = P.b_sc
    for sub in range(4):
        fps = mp[:, 0:16]
        for k in range(DC):
            S.op("pe", lambda e, k=k, sub=sub: e.matmul(fps, lhsT=P.uT[:, k, sub * 128:(sub + 1) * 128], rhs=P.wf[:, k, :],
                                                   start=(k == 0), stop=(k == DC - 1)), reads=[P.b_uT, P.b_wf], writes=[bm])
        S.op("dve", lambda e: e.tensor_tensor(out=P.fx[:], in0=fps, in1=P.bfb[:], op=ALU.add), reads=[bm, P.b_bfb], writes=[P.b_f])
        S.op("dve", lambda e: e.scalar_tensor_tensor(out=P.fa[:], in0=P.fx[:], scalar=-1.0, in1=P.fx[:], op0=ALU.mult, op1=ALU.min),
             reads=[P.b_f], writes=[P.b_f])
        S.op("act", lambda e: e.activation(out=P.fa[:], in_=P.fa[:], func=AF.Exp), reads=[P.b_f], writes=[P.b_f])
        S.op("act", lambda e: e.activation(out=P.fl[:], in_=P.fa[:], func=AF.Ln, bias=1.0, scale=1.0), reads=[P.b_f], writes=[P.b_f])
        S.op("dve", lambda e: e.scalar_tensor_tensor(out=P.logf[:], in0=P.fx[:], scalar=0.0, in1=P.fl[:], op0=ALU.min, op1=ALU.subtract),
             reads=[P.b_f], writes=[P.b_f])
        S.op("pe", lambda e: e.matmul(mp[:, 16:32], lhsT=P.Lm, rhs=P.logf[:], start=True, stop=True), reads=[P.b_f, P.b_cst], writes=[bm])
        S.op("pe", lambda e: e.matmul(mp[:, 32:48], lhsT=P.ones, rhs=P.logf[:], start=True, stop=True), reads=[P.b_f, P.b_cst], writes=[bm])
        S.op("pe", lambda e: e.matmul(mp[0:16, 64:192], lhsT=P.logf[:], rhs=P.Lm, start=True, stop=True), reads=[P.b_f, P.b_cst], writes=[bm])
        S.op("pe", lambda e: e.matmul(mp[0:16, 200:201], lhsT=P.logf[:], rhs=P.ones[:, 0:1], start=True, stop=True), reads=[P.b_f, P.b_cst], writes=[bm])
        S.op("dve", lambda e, sub=sub: e.tensor_tensor(out=P.ck_sb[:, sub, :], in0=mp[:, 16:32], in1=P.carry[:], op=ALU.add),
             reads=[bm, P.b_carry], writes=[P.b_ck])
        S.op("dve", lambda e, sub=sub: e.tensor_scalar(out=P.cqT[:, sub * 128:(sub + 1) * 128], in0=mp[0:16, 64:192], scalar1=P.carryT[:, 0:1], scalar2=None,
                                                       op0=ALU.add), reads=[bm, P.b_carry], writes=[P.b_cqT])
        S.op("dve", lambda e: e.tensor_tensor(out=P.carry[:], in0=mp[:, 32:48], in1=P.carry[:], op=ALU.add), reads=[bm, P.b_carry], writes=[P.b_carry])
        S.op("dve", lambda e: e.tensor_tensor(out=P.carryT[:], in0=mp[0:16, 200:201], in1=P.carryT[:], op=ALU.add), reads=[bm, P.b_carry], writes=[P.b_carry])
    S.dma("act", lambda e: e.dma_start(out=d["ck_own"][:, tt * 4:(tt + 1) * 4, :], in_=P.ck_sb[:]), reads=[P.b_ck], writes=[d["b_ckd"]])
    S.dma("act", lambda e: e.dma_start(out=d["cq_d"][:, tsl], in_=P.cqT[:]), reads=[P.b_cqT], writes=[d["b_cqd"]])


def attn_setup(P, d):
    S = P.S
    P.sc_ps = [P.ps("scps%d" % i, [128, 512]) for i in range(3)]; P.b_scp = [Buf("scp%d" % i) for i in range(3)]
    P.o_ps = [P.ps("ops%d" % i, [128, 512]) for i in range(2)]; P.b_o = [Buf("o%d" % i) for i in range(2)]
    P.sum_ps = P.ps("sumps", [128, 512]); P.b_sum = Buf("sum")
    P.kT = [P.sb("kT%d" % i, [128, 2, T], BF16) for i in range(2)]; P.b_kT = [Buf("kT%d" % i) for i in range(2)]
    P.vv = [P.sb("vv%d" % i, [128, 2, 16, 128], BF16) for i in range(2)]; P.b_vv = [Buf("vv%d" % i) for i in range(2)]
    P.qh = [P.sb("qh%d" % i, [128, 512], BF16) for i in range(2)]; P.b_qh = [Buf("qh%d" % i) for i in range(2)]
    P.cqb = [P.sb("cqb%d" % i, [128, 512]) for i in range(2)]; P.b_cqb = [Buf("cqb%d" % i) for i in range(2)]
    P.ssb = [P.sb("ssb%d" % i, [128, 512]) for i in range(3)]; P.b_ssb = [Buf("ssb%d" % i) for i in range(3)]
    P.pt = [P.sb("pt%d" % i, [128, 512], BF16) for i in range(4)]; P.b_pt = [Buf("pt%d" % i) for i in range(4)]
    P.rs = P.sb("rs", [128, 512]); P.b_rs = Buf("rs")
    P.nck = P.sb("nck", [128, 2, 16, 16]); P.b_nck = Buf("nck")
    P.offA = P.sb("offA", [128, 16]); P.b_offA = Buf("offA")
    P.negL = P.sb("negL", [128, 128]); P.b_negL = Buf("negL")
    P.fm1 = P.sb("fm1", [128, 1])
    if not hasattr(P, "flag"):
        P.flag = P.sb("flag_sb", [128, 1]); P.b_flag = Buf("flag")
        S.dma("sp", lambda e: e.dma_start(out=P.flag[:], in_=d["flag"]), writes=[P.b_flag])
    S.op("dve", lambda e: e.tensor_scalar(out=P.negL[:], in0=P.Lm, scalar1=-1.0, scalar2=NEG, op0=ALU.add, op1=ALU.mult),
         reads=[P.b_cst], writes=[P.b_negL])
    S.op("dve", lambda e: e.tensor_scalar(out=P.fm1[:], in0=P.flag[:], scalar1=-1.0, scalar2=NEG, op0=ALU.add, op1=ALU.mult),
         reads=[P.b_flag], writes=[P.b_flag])
    S.dma("sp", lambda e: e.dma_start(out=P.offA[:], in_=d["ck_prev"][127:128, 15, :].to_broadcast([128, 16])), reads=[d["b_ckp"]], writes=[P.b_offA])
    S.op("dve", lambda e: e.tensor_scalar(out=P.offA[:], in0=P.offA[:], scalar1=P.flag[:, 0:1], scalar2=None, op0=ALU.mult),
         reads=[P.b_offA, P.b_flag], writes=[P.b_offA])
    S.dma("sp", lambda e: e.dma_start(out=P.nck[:, 0, :, :], in_=d["ck_prev"]), reads=[d["b_ckp"]], writes=[P.b_nck])
    S.dma("sp", lambda e: e.dma_start(out=P.nck[:, 1, :, :], in_=d["ck_own"]), reads=[d["b_ckd"]], writes=[P.b_nck])
    S.op("dve", lambda e: e.tensor_scalar(out=P.nck[:, 0, :, :], in0=P.nck[:, 0, :, :], scalar1=-1.0, scalar2=P.fm1[:, 0:1], op0=ALU.mult, op1=ALU.add),
         reads=[P.b_nck, P.b_flag], writes=[P.b_nck])
    S.op("dve", lambda e: e.tensor_tensor(out=P.nck[:, 1, :, :], in0=P.nck[:, 1, :, :], in1=P.offA[:].unsqueeze(1).to_broadcast([128, 16, 16]), op=ALU.add),
         reads=[P.b_nck, P.b_offA], writes=[P.b_nck])
    S.op("dve", lambda e: e.tensor_scalar(out=P.nck[:, 1, :, :], in0=P.nck[:, 1, :, :], scalar1=-1.0, scalar2=None, op0=ALU.mult),
         reads=[P.b_nck], writes=[P.b_nck])


def attn_tile(P, j, d):
    S = P.S
    LA = 2
    qsl = slice(j * TT, (j + 1) * TT)
    nown = 4 * j + 4

    def load_head(h):
        i2 = h % 2
        kT, bk = P.kT[i2], P.b_kT[i2]
        vv, bv = P.vv[i2], P.b_vv[i2]
        qh, bq = P.qh[i2], P.b_qh[i2]
        cqb, bc = P.cqb[i2], P.b_cqb[i2]
        S.dma("sp", lambda e: e.dma_start(out=qh[:], in_=d["qT_d"][h, :, qsl]), writes=[bq])
        S.dma("sp", lambda e: e.dma_start(out=cqb[:], in_=d["cq_d"][h:h + 1, qsl].to_broadcast([128, TT])), writes=[bc])
        rk = [d["b_kp_l"][h // 4]] if "b_kp_l" in d else []
        rv = [d["b_vp_l"][h // 4]] if "b_vp_l" in d else []
        S.dma("sp", lambda e: e.dma_start(out=kT[:, 0, :], in_=d["kT_prev"][h]), reads=rk, writes=[bk])
        S.dma("sp", lambda e: e.dma_start(out=vv[:, 0, :, :], in_=d["v_prev"][h]), reads=rv, writes=[bv])
        S.dma("sp", lambda e: e.dma_start(out=kT[:, 1, 0:nown * 128], in_=d["kT_own"][h, :, 0:nown * 128]), writes=[bk])
        S.dma("sp", lambda e: e.dma_start(out=vv[:, 1, 0:nown, :], in_=d["v_own"][h, :, 0:nown, :]), writes=[bv])
        S.op("dve", lambda e: e.tensor_scalar(out=cqb[:], in0=cqb[:], scalar1=P.offA[:, h:h + 1], scalar2=None, op0=ALU.add),
             reads=[bc, P.b_offA], writes=[bc])

    its = []
    for h in range(FH):
        tiles = [(0, kt, 0) for kt in range(16)] + [(1, kt, max(0, kt - 4 * j)) for kt in range(nown)]
        for n, (s_, kt, a) in enumerate(tiles):
            its.append((h, s_, kt, a, n == 0, n == len(tiles) - 1))
    N = len(its)

    def front(idx):
        h, s_, kt, a, first, last = its[idx]
        if first:
            load_head(h)
        i2 = h % 2
        kT, bk, qh, bq, cqb, bc = P.kT[i2], P.b_kT[i2], P.qh[i2], P.b_qh[i2], P.cqb[i2], P.b_cqb[i2]
        q0 = a * 128
        scp, bs = P.sc_ps[idx % 3], P.b_scp[idx % 3]
        ssb, bss = P.ssb[idx % 3], P.b_ssb[idx % 3]
        pt, bp = P.pt[idx % 4], P.b_pt[idx % 4]
        S.op("pe", lambda e: e.matmul(scp[:, q0:TT], lhsT=kT[:, s_, kt * 128:(kt + 1) * 128], rhs=qh[:, q0:TT], start=True, stop=True),
             reads=[bk, bq], writes=[bs])
        S.op("dve", lambda e: e.tensor_tensor(out=ssb[:, q0:TT], in0=scp[:, q0:TT], in1=cqb[:, q0:TT], op=ALU.add), reads=[bs, bc], writes=[bss])
        if s_ == 1 and kt >= 4 * j:
            S.op("dve", lambda e: e.tensor_tensor(out=ssb[:, q0:q0 + 128], in0=ssb[:, q0:q0 + 128], in1=P.negL[:], op=ALU.add),
                 reads=[bss, P.b_negL], writes=[bss])
        S.op("act", lambda e: e.activation(out=pt[:, q0:TT], in_=ssb[:, q0:TT], func=AF.Exp, bias=P.nck[:, s_, kt, h:h + 1], scale=1.0),
             reads=[bss, P.b_nck], writes=[bp])

    def back(idx):
        h, s_, kt, a, first, last = its[idx]
        i2 = h % 2
        vv, bv = P.vv[i2], P.b_vv[i2]
        q0 = a * 128
        pt, bp = P.pt[idx % 4], P.b_pt[idx % 4]
        ops, bo = P.o_ps[i2], P.b_o[i2]
        S.op("pe", lambda e: e.matmul(ops[:, q0:TT], lhsT=vv[:, s_, kt, :], rhs=pt[:, q0:TT], start=first, stop=last), reads=[bv, bp], writes=[bo])
        S.op("pe", lambda e: e.matmul(P.sum_ps[:, q0:TT], lhsT=P.ones_bf, rhs=pt[:, q0:TT], start=first, stop=last), reads=[bp, P.b_cst], writes=[P.b_sum])
        if last:
            S.op("dve", lambda e: e.reciprocal(out=P.rs[:], in_=P.sum_ps[:]), reads=[P.b_sum], writes=[P.b_rs])
            S.op("dve", lambda e: e.tensor_tensor(out=P.yT[:, h, :], in0=ops[:], in1=P.rs[:], op=ALU.mult), reads=[bo, P.b_rs], writes=[P.b_yT])

    for idx in range(N + LA):
        if idx < N:
            front(idx)
        if idx - LA >= 0:
            back(idx - LA)


def fox_outproj(P, j, d):
    S = P.S
    S.dma("sp", lambda e: e.dma_start(out=P.hT[:], in_=d["h1T"][:, :, j * TT:(j + 1) * TT]), reads=[d["b_h1T"]], writes=[P.b_hT])
    for ct in range(D // CW):
        wt, bw = P.wtile(d["fox_w_out_t"], (0, ct))
        for jj in range(CW // 128):
            dch = ct * (CW // 128) + jj
            ps, bps = P.next_bank()
            for k in range(16):
                S.op("pe", lambda e, ps=ps, wt=wt, jj=jj, k=k: e.matmul(
                    ps[:], lhsT=wt[:, k, jj * 128:(jj + 1) * 128], rhs=P.yT[:, k, :], start=(k == 0), stop=(k == 15)),
                    reads=[bw, P.b_yT], writes=[bps])
            ga = P.modv(1, "g_a", dch)
            S.op("dve", lambda e, ps=ps, dch=dch, ga=ga: e.scalar_tensor_tensor(
                out=P.hT[:, dch, :], in0=ps[:], scalar=ga, in1=P.hT[:, dch, :], op0=ALU.mult, op1=ALU.add),
                reads=[bps, P.b_mod, P.b_hT], writes=[P.b_hT])


def build_program(mode, ntiles=NTT, do_mlp=True):
    P = Prog()
    S = P.S
    d = {}
    nmix = P.din("nmix", [128, 2, DC]); nmlp = P.din("nmlp", [128, 2, DC])
    P.setup_common()
    finals = []
    if mode == "l1":
        c_col = P.din("c_col", [128, DC])
        w_ada_t = P.din("w_ada_t", [2, 48, 128, 16, CW])
        b_ada_col = P.din("b_ada_col", [128, 2, 96])
        mod_o = P.dout("mod_o", [128, 2, 96])
        P.adaln(w_ada_t, b_ada_col, c_col, nmix, nmlp, mod_out=mod_o)
        finals.append(P.mod_store)
    else:
        mod_in = P.din("mod_in", [128, 2, 96])
        P.adaln(None, None, None, nmix, nmlp, mod_in=mod_in)
    d["flag"] = P.din("flag", [128, 1])
    if mode in ("l1", "l2"):
        xT = P.din("xT", [128, DC, T])
        d["w_in_g"] = P.din("w_in_g", [8, 5, 128, 16, CW]); d["w_dt"] = P.din("w_dt", [128, 16, 64])
        d["convw_col"] = P.din("convw_col", [128, 48, 4]); d["convb_col"] = P.din("convb_col", [128, 48])
        for n in ("dtb_bc", "alog_bc", "D_bc"):
            d[n] = P.din(n, [128, 64])
        d["gn_col"] = P.din("gn_col", [128, 32])
        S_in = P.din("S_in", [128, 4096]); halo_in = P.din("halo_in", [128, 48, 3])
        P.phase_alloc = True
        ssd_setup(P, d)
        ssd_init_state(P, S_in, halo_in)
    if mode == "l1":
        S_out = P.dout("S_out", [128, 4096]); halo_out = P.dout("halo_out", [128, 48, 3])
        for tt in range(NTT):
            S.dma("sp", lambda e, tt=tt: e.dma_start(out=P.hT[:], in_=xT[:, :, tt * TT:(tt + 1) * TT]), writes=[P.b_hT])
            P.norm_mod(0, 0)
            ssd_dt(P)
            for g in range(8):
                ssd_group(P, g, d["w_in_g"], True)
        finals.append(S.dma("act", lambda e: e.dma_start(out=S_out, in_=P.Sst[:].rearrange("p g c -> p (g c)")), reads=P.b_S, writes=[Buf()]))
        finals.append(S.dma("act", lambda e: e.dma_start(out=halo_out, in_=P.halo[:]), reads=P.b_halo, writes=[Buf()]))
    if mode == "l2":
        d["w_out_t"] = P.din("w_out_t", [2, 8, 128, 16, CW])
        w_up_t = P.din("w_up_t", [1, 32, 128, 16, CW]); w_down_t = P.din("w_down_t", [4, 8, 128, 16, CW])
        d["fox_w_in_t"] = P.din("fox_w_in_t", [1, 24, 128, 16, CW]); d["w_f"] = P.din("w_f", [128, 16, 16]); d["bf_bc"] = P.din("bf_bc", [128, 16])
        d["h1T"] = P.dout("h1T", [128, DC, T]); d["qT_d"] = P.dout("qT_d", [FH, 128, T], BF16)
        d["kT_own"] = P.dout("kT_own", [FH, 128, T], BF16); d["v_own"] = P.dout("v_own", [FH, 128, 16, 128], BF16)
        d["ck_own"] = P.dout("ck_own", [128, 16, 16]); d["cq_d"] = P.dout("cq_d", [FH, T])
        for n in ("b_h1T", "b_q", "b_k", "b_v", "b_ckd", "b_cqd"):
            d[n] = Buf(n)
        fox_in_setup(P, d)
        for tt in range(ntiles):
            xs = xT[:, :, tt * TT:(tt + 1) * TT]
            S.dma("sp", lambda e, xs=xs: e.dma_start(out=P.hT[:], in_=xs), writes=[P.b_hT])
            P.norm_mod(0, 0)
            ssd_dt(P)
            for g in range(8):
                ssd_group(P, g, d["w_in_g"], False)
            ssd_outproj(P, d["w_out_t"], xs)
            if do_mlp:
                P.mlp(0, w_up_t, w_down_t, wl=0)
            fox_inproj(P, tt, d)
        S.barrier()
    if mode == "l3":
        w_up_t = P.din("w_up_t", [1, 32, 128, 16, CW]); w_down_t = P.din("w_down_t", [4, 8, 128, 16, CW])
        d["fox_w_out_t"] = P.din("fox_w_out_t", [1, 8, 128, 16, CW])
        fnorm = P.din("fnorm", [128, DC])
        d["h1T"] = P.din("h1T", [128, DC, T]); d["qT_d"] = P.din("qT_d", [FH, 128, T], BF16)
        d["kT_own"] = P.din("kT_own", [FH, 128, T], BF16); d["v_own"] = P.din("v_own", [FH, 128, 16, 128], BF16)
        d["kT_prev"] = P.din("kT_prev", [FH, 128, T], BF16); d["v_prev"] = P.din("v_prev", [FH, 128, 16, 128], BF16)
        d["ck_own"] = P.din("ck_own", [128, 16, 16]); d["ck_prev"] = P.din("ck_prev", [128, 16, 16]); d["cq_d"] = P.din("cq_d", [FH, T])
        outT = P.dout("outT", [128, DC, T])
        for n in ("b_h1T", "b_q", "b_k", "b_v", "b_ckd", "b_cqd", "b_kp", "b_vp", "b_ckp"):
            d[n] = Buf(n)
        P.phase_alloc = True
        attn_setup(P, d)
        fn = P.sb("fn", [128, DC]); b_fn = Buf("fn")
        S.dma("sp", lambda e: e.dma_start(out=fn[:], in_=fnorm), writes=[b_fn])
        P.b_outf = P.b_hT
        for j in range(ntiles):
            attn_tile(P, j, d)
            fox_outproj(P, j, d)
            if do_mlp:
                P.mlp(1, w_up_t, w_down_t, wl=0)
            P.b_gs_save = P.b_gs
            P.norm_mod(1, 0, gs_ap=fn, out_f32=P.hT)
            finals.append(S.dma("act", lambda e, j=j: e.dma_start(out=outT[:, :, j * TT:(j + 1) * TT], in_=P.hT[:]), reads=[P.b_hT], writes=[Buf()]))
    S.barrier()
    return P.finish(finals)


def _stack2(a):
    return np.ascontiguousarray(np.stack([col_layout(a[i]) for i in range(2)], axis=1))


def host_prepare(inp):
    H = {}
    H["consts"] = make_consts()
    H["nmix"] = _stack2(inp["norm_mix"]); H["nmlp"] = _stack2(inp["norm_mlp"])
    H["w_ada_t"] = np.stack([tile_weight(inp["w_ada"][i])[0] for i in range(2)])
    H["b_ada_col"] = _stack2(inp["b_ada"])
    w = inp["ssd_w_in"][0]
    tiles = []
    for g in range(8):
        wg = np.concatenate([w[:, g * 512:(g + 1) * 512], w[:, 4096 + g * 512:4096 + (g + 1) * 512],
                             w[:, 8192 + g * 128:8192 + (g + 1) * 128], w[:, 9216 + g * 128:9216 + (g + 1) * 128]], axis=1)
        tiles.append(tile_weight(wg)[0])
    H["w_in_g"] = np.stack(tiles)
    H["w_dt"] = tile_weight(w[:, 10240:10304], cw=64)[0, 0]
    chans = []
    for g in range(8):
        for j in range(4):
            chans.append(np.arange(g * 512 + j * 128, g * 512 + (j + 1) * 128))
        chans.append(np.arange(4096 + g * 128, 4096 + (g + 1) * 128))
        chans.append(np.arange(5120 + g * 128, 5120 + (g + 1) * 128))
    chans = np.stack(chans)
    H["convw_col"] = np.ascontiguousarray(inp["ssd_conv_w"][0][:, chans].transpose(2, 1, 0))
    H["convb_col"] = np.ascontiguousarray(inp["ssd_conv_b"][0][chans].T)
    H["dtb_bc"] = np.ascontiguousarray(np.broadcast_to(inp["ssd_dt_bias"][0], (128, 64)))
    H["alog_bc"] = np.ascontiguousarray(np.broadcast_to(inp["ssd_A_log"][0], (128, 64)))
    H["D_bc"] = np.ascontiguousarray(np.broadcast_to(inp["ssd_D"][0], (128, 64)))
    H["gn_col"] = col_layout(inp["ssd_gnorm"][0])
    H["w_out_t"] = tile_weight(inp["ssd_w_out"][0])
    H["w_up_t"] = [tile_weight(inp["w_up"][i]) for i in range(2)]
    H["w_down_t"] = [tile_weight(inp["w_down"][i]) for i in range(2)]
    fw = inp["fox_w_in"][0]
    H["fox_w_in_t"] = tile_weight(fw[:, :6144])
    H["w_f"] = tile_weight(fw[:, 6144:6160], cw=16)[0, 0]
    H["bf_bc"] = np.ascontiguousarray(np.broadcast_to(inp["fox_b_f"][0], (128, 16)))
    H["fox_w_out_t"] = tile_weight(inp["fox_w_out"][0])
    H["fnorm"] = col_layout(inp["final_norm"])
    return H


_PROGS = {}


def _prog(mode):
    if mode not in _PROGS:
        _PROGS[mode] = build_program(mode)
    return _PROGS[mode]


def kernel(**inp):
    inp = {k: np.asarray(v) for k, v in inp.items()}
    H = host_prepare(inp)
    cores = list(range(8))
    xT = [feat_major(inp["x"][c // 2, (c % 2) * T:(c % 2 + 1) * T]) for c in cores]
    flag = [np.full((128, 1), float(c % 2), np.float32) for c in cores]
    base = {k: H[k] for k in ("consts", "nmix", "nmlp")}
    ssdw = {k: H[k] for k in ("w_in_g", "w_dt", "convw_col", "convb_col", "dtb_bc", "alog_bc", "D_bc", "gn_col")}
    zS = np.zeros((128, 4096), np.float32); zh = np.zeros((128, 48, 3), np.float32)
    maps = []
    for c in cores:
        m = dict(base); m.update(ssdw)
        m.update({"c_col": col_layout(inp["c"][c // 2]), "w_ada_t": H["w_ada_t"], "b_ada_col": H["b_ada_col"],
                  "flag": np.zeros((128, 1), np.float32), "xT": xT[c], "S_in": zS, "halo_in": zh})
        maps.append(m)
    r1 = run_bass_kernel_spmd(_prog("l1"), maps, core_ids=cores).results
    maps = []
    for c in cores:
        a = c - (c % 2)
        m = dict(base); m.update(ssdw)
        m.update({"mod_in": r1[c]["mod_o"], "flag": flag[c], "xT": xT[c], "S_in": r1[a]["S_out"], "halo_in": r1[a]["halo_out"],
                  "w_out_t": H["w_out_t"], "w_up_t": H["w_up_t"][0], "w_down_t": H["w_down_t"][0],
                  "fox_w_in_t": H["fox_w_in_t"], "w_f": H["w_f"], "bf_bc": H["bf_bc"]})
        maps.append(m)
    r2 = run_bass_kernel_spmd(_prog("l2"), maps, core_ids=cores).results
    maps = []
    for c in cores:
        a = c - (c % 2)
        m = dict(base)
        m.update({"mod_in": r1[c]["mod_o"], "flag": flag[c], "w_up_t": H["w_up_t"][1], "w_down_t": H["w_down_t"][1],
                  "fox_w_out_t": H["fox_w_out_t"], "fnorm": H["fnorm"],
                  "h1T": r2[c]["h1T"], "qT_d": r2[c]["qT_d"], "kT_own": r2[c]["kT_own"], "v_own": r2[c]["v_own"],
                  "ck_own": r2[c]["ck_own"], "cq_d": r2[c]["cq_d"],
                  "kT_prev": r2[a]["kT_own"], "v_prev": r2[a]["v_own"], "ck_prev": r2[a]["ck_own"]})
        maps.append(m)
    r3 = run_bass_kernel_spmd(_prog("l3"), maps, core_ids=cores).results
    out = np.empty((BATCH, SEQ, D), np.float32)
    for c in cores:
        out[c // 2, (c % 2) * T:(c % 2 + 1) * T] = from_feat_major(r3[c]["outT"])
    return out


PAIRS = [[0, 1], [2, 3], [4, 5], [6, 7]]


class View:
    def __init__(self, base, **over):
        self.__dict__["_b"] = base
        self.__dict__.update(over)

    def __getattr__(self, k):
        return getattr(self.__dict__["_b"], k)


class HeadChunks:
    def __init__(self, views):
        self.views = views

    def __getitem__(self, key):
        if not isinstance(key, tuple):
            key = (key,)
        h = key[0]
        if isinstance(h, slice):
            c = h.start // 4
            assert h.stop - h.start == 4 and h.start % 4 == 0
            return self.views[c][(slice(0, 4),) + key[1:]]
        return self.views[h // 4][(h % 4,) + key[1:]]


def build_fused(ntiles=NTT):
    P = Prog()
    S = P.S
    nc = P.nc
    d = {}
    nmix = P.din("nmix", [128, 2, DC]); nmlp = P.din("nmlp", [128, 2, DC])
    P.setup_common()
    c_col = P.din("c_col", [128, DC])
    w_ada_t = P.din("w_ada_h", [2, 24, 128, 16, CW])
    b_ada_col = P.din("b_ada_h", [128, 2, 48])
    m_own = nc.dram_tensor("m_own", [128, 96], F32).ap(); m_g = nc.dram_tensor("m_g", [256, 96], F32).ap()
    P.adaln(w_ada_t, b_ada_col, c_col, nmix, nmlp, half_dram=(m_own, m_g))
    d["flag"] = P.din("flag", [128, 1])
    xT = P.din("xT", [128, DC, T])
    d["w_in_g"] = P.din("w_in_g", [8, 5, 128, 16, CW]); d["w_dt"] = P.din("w_dt", [128, 16, 64])
    d["convw_col"] = P.din("convw_col", [128, 48, 4]); d["convb_col"] = P.din("convb_col", [128, 48])
    for n in ("dtb_bc", "alog_bc", "D_bc"):
        d[n] = P.din(n, [128, 64])
    d["gn_col"] = P.din("gn_col", [128, 32])
    d["w_out_t"] = P.din("w_out_t", [2, 8, 128, 16, CW])
    w_up_t = P.din("w_up_t", [2, 32, 128, 16, CW]); w_down_t = P.din("w_down_t", [8, 8, 128, 16, CW])
    d["fox_w_in_t"] = P.din("fox_w_in_t", [1, 24, 128, 16, CW]); d["w_f"] = P.din("w_f", [128, 16, 16]); d["bf_bc"] = P.din("bf_bc", [128, 16])
    d["fox_w_out_t"] = P.din("fox_w_out_t", [1, 8, 128, 16, CW])
    fnorm = P.din("fnorm", [128, DC])
    outT = P.dout("outT", [128, DC, T])
    d["h1T"] = nc.dram_tensor("h1T", [128, DC, T], F32).ap()
    qT2 = nc.dram_tensor("qT2", [FH * 128, T], BF16).ap()
    kT2 = [nc.dram_tensor("kT2_%d" % i, [512, T], BF16).ap() for i in range(4)]
    kT_g = [nc.dram_tensor("kT_g%d" % i, [1024, T], BF16).ap() for i in range(4)]
    v2 = [nc.dram_tensor("v2_%d" % i, [512, 2048], BF16).ap() for i in range(4)]
    v_g = [nc.dram_tensor("v_g%d" % i, [1024, 2048], BF16).ap() for i in range(4)]
    ck2 = nc.dram_tensor("ck2", [128, 256], F32).ap(); ck_g = nc.dram_tensor("ck_g", [256, 256], F32).ap()
    d["cq_d"] = nc.dram_tensor("cq_d", [FH, T], F32).ap()
    d["qT_d"] = qT2.rearrange("(h p) t -> h p t", p=128)
    d["kT_own"] = HeadChunks([x.rearrange("(h p) t -> h p t", p=128) for x in kT2])
    d["kT_prev"] = HeadChunks([x[0:512, :].rearrange("(h p) t -> h p t", p=128) for x in kT_g])
    d["v_own"] = HeadChunks([x.rearrange("(h p) (k e) -> h p k e", p=128, e=128) for x in v2])
    d["v_prev"] = HeadChunks([x[0:512, :].rearrange("(h p) (k e) -> h p k e", p=128, e=128) for x in v_g])
    d["ck_own"] = ck2.rearrange("p (k h) -> p k h", h=16)
    d["ck_prev"] = ck_g[0:128, :].rearrange("p (k h) -> p k h", h=16)
    for n in ("b_h1T", "b_q", "b_k", "b_v", "b_ckd", "b_cqd", "b_kp", "b_vp", "b_ckp"):
        d[n] = Buf(n)
    xpT = P.din("xpT", [128, DC, T])
    w_in_pre = P.din("w_in_pre", [4, 5, 128, 16, CW]); w_dt_pre = P.din("w_dt_pre", [128, 16, 32])
    convw_pre = P.din("convw_pre", [128, 24, 4]); convb_pre = P.din("convb_pre", [128, 24])
    dtb_pre = P.din("dtb_pre", [128, 32]); alog_pre = P.din("alog_pre", [128, 32])
    P.phase_alloc = True
    ssd_setup(P, d)
    cwp = P.sb("cwp", [128, 24, 4]); cbp = P.sb("cbp", [128, 24]); b_cwp = Buf("cwp")
    dtbp = P.sb("dtbp", [128, 32]); Abcp = P.sb("Abcp", [128, 32]); b_prmp = Buf("prmp")
    wdtp = P.sb("wdtp", [128, 16, 32], BF16); b_wdtp = Buf("wdtp")
    S.dma("sp", lambda e: e.dma_start(out=cwp[:], in_=convw_pre), writes=[b_cwp])
    S.dma("sp", lambda e: e.dma_start(out=cbp[:], in_=convb_pre), writes=[b_cwp])
    S.dma("sp", lambda e: e.dma_start(out=dtbp[:], in_=dtb_pre), writes=[b_prmp])
    S.dma("sp", lambda e: e.dma_start(out=Abcp[:], in_=alog_pre), writes=[b_prmp])
    S.dma("pool", lambda e: e.dma_start(out=wdtp[:], in_=w_dt_pre), writes=[b_wdtp])
    S.op("dve", lambda e: e.tensor_scalar(out=cwp[:], in0=cwp[:], scalar1=0.5, scalar2=None, op0=ALU.mult), reads=[b_cwp], writes=[b_cwp])
    S.op("dve", lambda e: e.tensor_scalar(out=cbp[:], in0=cbp[:], scalar1=0.5, scalar2=None, op0=ALU.mult), reads=[b_cwp], writes=[b_cwp])
    S.op("act", lambda e: e.activation(out=Abcp[:], in_=Abcp[:], func=AF.Exp), reads=[b_prmp], writes=[b_prmp])
    S.op("dve", lambda e: e.tensor_scalar(out=Abcp[:], in0=Abcp[:], scalar1=-1.0, scalar2=None, op0=ALU.mult), reads=[b_prmp], writes=[b_prmp])
    Q = View(P, cw=cwp, cb=cbp, b_cw=b_cwp, dtb=dtbp, Abc=Abcp, b_prm=b_prmp, wdt=wdtp, b_wdt=b_wdtp, nh=32,
             dt=P.dt[:, :, 0:32], dtA=P.dtA[:, :, 0:32], ea=P.ea[:, :, 0:32], dte=P.dte[:, :, 0:32], cd=P.cd[:, :, 0:32],
             sp1=P.sp1[:, 0:32], sp2=P.sp2[:, 0:32], sp3=P.sp3[:, 0:32])
    S.op("dve", lambda e: e.memset(P.Sst[:], 0.0), writes=P.b_S)
    S.op("dve", lambda e: e.memset(P.halo[:], 0.0), writes=P.b_halo)
    for tt in range(ntiles):
        S.dma("sp", lambda e, tt=tt: e.dma_start(out=P.hT[:], in_=xpT[:, :, tt * TT:(tt + 1) * TT]), writes=[P.b_hT])
        P.norm_mod(0, 0)
        ssd_dt(Q)
        for gl in range(4):
            ssd_group(Q, gl, w_in_pre, True)
    S_own = nc.dram_tensor("S_own2", [128, 2048], F32).ap(); S_g = nc.dram_tensor("S_g2", [256, 2048], F32).ap()
    h_own = nc.dram_tensor("h_own2", [128, 72], F32).ap(); h_g = nc.dram_tensor("h_g2", [256, 72], F32).ap()
    S.dma("act", lambda e: e.dma_start(out=S_own, in_=P.Sst[:, 0:4, :].rearrange("p g c -> p (g c)")), reads=P.b_S, writes=[Buf()])
    S.dma("act", lambda e: e.dma_start(out=h_own.rearrange("p (c j) -> p c j", j=3), in_=P.halo[:, 0:24, :]), reads=P.b_halo, writes=[Buf()])
    S.barrier()
    S.cc(lambda e: e.collective_compute("AllGather", ALU.bypass, replica_groups=PAIRS, ins=[S_own], outs=[S_g]))
    S.cc(lambda e: e.collective_compute("AllGather", ALU.bypass, replica_groups=PAIRS, ins=[h_own], outs=[h_g]))
    S.barrier()
    for r in range(2):
        S.dma("sp", lambda e, r=r: e.dma_start(out=P.Sst[:, 4 * r:4 * r + 4, :].rearrange("p g c -> p (g c)"), in_=S_g[128 * r:128 * (r + 1), :]), writes=P.b_S)
        S.dma("sp", lambda e, r=r: e.dma_start(out=P.halo[:, 24 * r:24 * r + 24, :], in_=h_g[128 * r:128 * (r + 1), :].rearrange("p (c j) -> p c j", j=3)),
              writes=P.b_halo)
    S.op("dve", lambda e: e.tensor_scalar(out=P.Sst[:], in0=P.Sst[:], scalar1=P.flag[:, 0:1], scalar2=None, op0=ALU.mult),
         reads=P.b_S + [P.b_flag], writes=P.b_S)
    S.op("dve", lambda e: e.tensor_scalar(out=P.halo[:], in0=P.halo[:], scalar1=P.flag[:, 0:1], scalar2=None, op0=ALU.mult),
         reads=P.b_halo + [P.b_flag], writes=P.b_halo)
    fox_in_setup(P, d)
    for tt in range(ntiles):
        xs = xT[:, :, tt * TT:(tt + 1) * TT]
        S.dma("sp", lambda e, xs=xs: e.dma_start(out=P.hT[:], in_=xs), writes=[P.b_hT])
        P.norm_mod(0, 0)
        ssd_dt(P)
        for g in range(8):
            ssd_group(P, g, d["w_in_g"], False)
        ssd_outproj(P, d["w_out_t"], xs)
        P.mlp(0, w_up_t, w_down_t)
        fox_inproj(P, tt, d)
    P.end_phase()
    d["b_ckp"] = Buf("ckp")
    op = S.cc(lambda e: e.collective_compute("AllGather", ALU.bypass, replica_groups=PAIRS, ins=[ck2], outs=[ck_g]))
    d["b_ckp"].writer = op
    d["b_kp_l"] = [Buf("kp%d" % i) for i in range(4)]
    d["b_vp_l"] = [Buf("vp%d" % i) for i in range(4)]
    for i in range(4):
        op = S.cc(lambda e, i=i: e.collective_compute("AllGather", ALU.bypass, replica_groups=PAIRS, ins=[kT2[i]], outs=[kT_g[i]]))
        d["b_kp_l"][i].writer = op
        op = S.cc(lambda e, i=i: e.collective_compute("AllGather", ALU.bypass, replica_groups=PAIRS, ins=[v2[i]], outs=[v_g[i]]))
        d["b_vp_l"][i].writer = op
    attn_setup(P, d)
    fn = P.sb("fn", [128, DC]); b_fn = Buf("fn")
    S.dma("sp", lambda e: e.dma_start(out=fn[:], in_=fnorm), writes=[b_fn])
    P.b_outf = P.b_hT
    finals = []
    for j in range(ntiles):
        attn_tile(P, j, d)
        fox_outproj(P, j, d)
        P.mlp(1, w_up_t, w_down_t)
        P.norm_mod(1, 0, gs_ap=fn, out_f32=P.hT)
        finals.append(S.dma("act", lambda e, j=j: e.dma_start(out=outT[:, :, j * TT:(j + 1) * TT], in_=P.hT[:]), reads=[P.b_hT], writes=[Buf()]))
    S.barrier()
    return P.finish(finals)


_FUSED = []


def kernel(**inp):
    inp = {k: np.asarray(v) for k, v in inp.items()}
    H = host_prepare(inp)
    cores = list(range(8))
    shared = {k: H[k] for k in ("consts", "nmix", "nmlp", "w_in_g", "w_dt", "convw_col", "convb_col",
                                "dtb_bc", "alog_bc", "D_bc", "gn_col", "w_out_t", "fox_w_in_t", "w_f", "bf_bc", "fox_w_out_t", "fnorm")}
    shared["w_up_t"] = np.concatenate(H["w_up_t"], axis=0)
    shared["w_down_t"] = np.concatenate(H["w_down_t"], axis=0)
    maps = []
    xTs = [feat_major(inp["x"][c // 2, (c % 2) * T:(c % 2 + 1) * T]) for c in cores]
    for c in cores:
        m = dict(shared)
        m["xT"] = xTs[c]
        m["c_col"] = col_layout(inp["c"][c // 2])
        m["flag"] = np.full((128, 1), float(c % 2), np.float32)
        hf = c % 2
        m["xpT"] = xTs[c - hf]
        m["w_ada_h"] = np.ascontiguousarray(H["w_ada_t"][:, 24 * hf:24 * hf + 24])
        m["b_ada_h"] = np.ascontiguousarray(H["b_ada_col"][:, :, 48 * hf:48 * hf + 48])
        m["w_in_pre"] = np.ascontiguousarray(H["w_in_g"][4 * hf:4 * hf + 4])
        m["w_dt_pre"] = np.ascontiguousarray(H["w_dt"][:, :, 32 * hf:32 * hf + 32])
        m["convw_pre"] = np.ascontiguousarray(H["convw_col"][:, 24 * hf:24 * hf + 24])
        m["convb_pre"] = np.ascontiguousarray(H["convb_col"][:, 24 * hf:24 * hf + 24])
        m["dtb_pre"] = np.ascontiguousarray(H["dtb_bc"][:, 32 * hf:32 * hf + 32])
        m["alog_pre"] = np.ascontiguousarray(H["alog_bc"][:, 32 * hf:32 * hf + 32])
        maps.append(m)
    if not _FUSED:
        _FUSED.append(build_fused())
    res = run_bass_kernel_spmd(_FUSED[0], maps, core_ids=cores).results
    out = np.empty((BATCH, SEQ, D), np.float32)
    for c in cores:
        out[c // 2, (c % 2) * T:(c % 2 + 1) * T] = from_feat_major(res[c]["outT"])
    return out
```
